# Optimizing a Trainium2 kernel written in Bass

```python
import jax, jax.numpy as jnp
from jax import lax
import numpy as np

D_MODEL = 2048
BATCH = 4
SEQ = 8192
DEPTH = 1

MEM_LEN = 256
ROPE_THETA = 500000.0
EPS = 1e-6
Q_BLOCK = 128
NEG = -1e30

NSA_HEADS = 16
NSA_KV_HEADS = 4
NSA_HPG = NSA_HEADS // NSA_KV_HEADS
NSA_DK = 96
NSA_DV = 64
NSA_ROT = NSA_DK // 4
CMP_BLOCK = 32
CMP_STRIDE = 16
SEL_BLOCK = 64
SEL_TOPK = 16
WINDOW = 512

MLA_HEADS = 16
MLA_NOPE = 64
MLA_ROPE = 32
MLA_DV = 64
MLA_Q_RANK = 512
MLA_KV_RANK = 256

XA_HEADS = 4
XA_DIM = 128

D_FF = 4 * D_MODEL

SPLITS = (NSA_HEADS * NSA_DK,
          NSA_KV_HEADS * NSA_DK, NSA_KV_HEADS * NSA_DV,
          NSA_KV_HEADS * NSA_DK, NSA_KV_HEADS * NSA_DV,
          NSA_KV_HEADS * NSA_DK, NSA_KV_HEADS * NSA_DV,
          NSA_HEADS * 3,
          MLA_Q_RANK, MLA_KV_RANK, MLA_ROPE,
          D_MODEL, D_MODEL)
D_IN = (NSA_HEADS * NSA_DK + 3 * NSA_KV_HEADS * (NSA_DK + NSA_DV) + NSA_HEADS * 3
        + MLA_Q_RANK + MLA_KV_RANK + MLA_ROPE + 2 * D_MODEL)

kernel_name = 'hybrid_nsa_mla_gated_block'


def rmsnorm(x, g):
    xf = x.astype(jnp.float32)
    y = xf * lax.rsqrt(jnp.mean(xf * xf, axis=-1, keepdims=True) + EPS)
    return (y * g.astype(jnp.float32)).astype(x.dtype)


def rope_tables(n_pos, dim):
    inv = 1.0 / (ROPE_THETA ** (jnp.arange(0, dim, 2, dtype=jnp.float32) / dim))
    ang = jnp.arange(n_pos, dtype=jnp.float32)[:, None] * inv[None, :]
    return jnp.cos(ang), jnp.sin(ang)


def apply_rope(x, cos, sin):
    half = x.shape[-1] // 2
    shp = (cos.shape[0],) + (1,) * (x.ndim - 3) + (half,)
    c, s = cos.reshape(shp), sin.reshape(shp)
    xf = x.astype(jnp.float32)
    x1, x2 = xf[..., :half], xf[..., half:]
    return jnp.concatenate([x1 * c - x2 * s, x1 * s + x2 * c], axis=-1).astype(x.dtype)


def partial_rope(x, cos, sin):
    return jnp.concatenate([apply_rope(x[..., :NSA_ROT], cos, sin), x[..., NSA_ROT:]], axis=-1)


def masked_softmax(s, mask):
    s = jnp.where(mask, s.astype(jnp.float32), NEG)
    m = jnp.max(s, axis=-1, keepdims=True)
    e = jnp.where(mask, jnp.exp(s - m), 0.0)
    return e / jnp.maximum(jnp.sum(e, axis=-1, keepdims=True), 1e-30)


def compress(kv, pos, w1, w2):
    b, s, g, d = kv.shape
    nc = (s - CMP_BLOCK) // CMP_STRIDE + 1
    idx = (np.arange(nc) * CMP_STRIDE)[:, None] + np.arange(CMP_BLOCK)[None, :]
    blk = kv[:, idx] + pos[None, None, :, None, :].astype(kv.dtype)
    blk = blk.transpose(0, 1, 3, 2, 4).reshape(b, nc, g, CMP_BLOCK * d)
    return jax.nn.gelu(blk @ w1) @ w2


def cmp_to_sel_matrix(nc, nsb):
    cs = np.arange(nc) * CMP_STRIDE
    ce = cs + CMP_BLOCK
    ss = np.arange(nsb) * SEL_BLOCK
    se = ss + SEL_BLOCK
    ov = np.clip(np.minimum(ce[:, None], se[None, :]) - np.maximum(cs[:, None], ss[None, :]), 0, None)
    return jnp.asarray(ov.astype(np.float32) / np.float32(CMP_BLOCK))


def nsa_attention(q, kc, vc, ks, vs, kw, vw, gates,
                  cmp_pos_k, cmp_w1_k, cmp_w2_k, cmp_pos_v, cmp_w1_v, cmp_w2_v):
    b, s = q.shape[:2]
    G, HPG = NSA_KV_HEADS, NSA_HPG
    scale = NSA_DK ** -0.5
    kcc = compress(kc, cmp_pos_k, cmp_w1_k, cmp_w2_k)
    vcc = compress(vc, cmp_pos_v, cmp_w1_v, cmp_w2_v)
    nc = kcc.shape[1]
    cmp_end = jnp.arange(nc) * CMP_STRIDE + CMP_BLOCK - 1
    nsb = s // SEL_BLOCK
    topk = min(SEL_TOPK, nsb)
    m_sel = cmp_to_sel_matrix(nc, nsb)
    ks_blk = ks.reshape(b, nsb, SEL_BLOCK, G, NSA_DK).transpose(0, 3, 1, 2, 4)
    vs_blk = vs.reshape(b, nsb, SEL_BLOCK, G, NSA_DV).transpose(0, 3, 1, 2, 4)
    kw_pad = jnp.pad(kw, ((0, 0), (WINDOW, 0), (0, 0), (0, 0)))
    vw_pad = jnp.pad(vw, ((0, 0), (WINDOW, 0), (0, 0), (0, 0)))
    bi = jnp.arange(b)[:, None, None, None]
    gi = jnp.arange(G)[None, :, None, None]
    blk_ids = jnp.arange(nsb)

    def block(i):
        s0 = i * Q_BLOCK
        t = s0 + jnp.arange(Q_BLOCK)
        qb = lax.dynamic_slice_in_dim(q, s0, Q_BLOCK, axis=1).reshape(b, Q_BLOCK, G, HPG, NSA_DK)
        gb = lax.dynamic_slice_in_dim(gates, s0, Q_BLOCK, axis=1).reshape(b, Q_BLOCK, G, HPG, 3)
        s_c = jnp.einsum('bqghd,bcgd->bghqc', qb, kcc) * scale
        p_c = masked_softmax(s_c, cmp_end[None, :] <= t[:, None])
        o_c = jnp.einsum('bghqc,bcgd->bqghd', p_c.astype(vcc.dtype), vcc)
        imp = jnp.einsum('bghqc,cj->bgqj', p_c, m_sel)
        cur = t // SEL_BLOCK
        valid = blk_ids[None, :] <= cur[:, None]
        forced = (blk_ids[None, :] == 0) | (blk_ids[None, :] == cur[:, None]) | (blk_ids[None, :] == cur[:, None] - 1)
        score = jnp.where(valid, jnp.where(forced, 1e4, imp), -1.0)
        _, sel = lax.top_k(score, topk)
        kg = ks_blk[bi, gi, sel]
        vg = vs_blk[bi, gi, sel]
        kpos = sel[..., None] * SEL_BLOCK + jnp.arange(SEL_BLOCK)
        m_s = (kpos <= t[None, None, :, None, None]).reshape(b, G, 1, Q_BLOCK, topk * SEL_BLOCK)
        s_s = jnp.einsum('bqghd,bgqnrd->bghqnr', qb, kg).reshape(b, G, HPG, Q_BLOCK, topk * SEL_BLOCK) * scale
        p_s = masked_softmax(s_s, m_s).reshape(b, G, HPG, Q_BLOCK, topk, SEL_BLOCK)
        o_s = jnp.einsum('bghqnr,bgqnrd->bqghd', p_s.astype(vg.dtype), vg)
        kwb = lax.dynamic_slice_in_dim(kw_pad, s0, Q_BLOCK + WINDOW, axis=1)
        vwb = lax.dynamic_slice_in_dim(vw_pad, s0, Q_BLOCK + WINDOW, axis=1)
        kp = s0 - WINDOW + jnp.arange(Q_BLOCK + WINDOW)
        diff = t[:, None] - kp[None, :]
        m_w = (kp[None, :] >= 0) & (diff >= 0) & (diff < WINDOW)
        s_w = jnp.einsum('bqghd,bkgd->bghqk', qb, kwb) * scale
        p_w = masked_softmax(s_w, m_w)
        o_w = jnp.einsum('bghqk,bkgd->bqghd', p_w.astype(vwb.dtype), vwb)
        o = gb[..., 0:1] * o_c + gb[..., 1:2] * o_s + gb[..., 2:3] * o_w
        return o.reshape(b, Q_BLOCK, NSA_HEADS * NSA_DV).astype(q.dtype)

    out = lax.map(block, jnp.arange(s // Q_BLOCK))
    return out.transpose(1, 0, 2, 3).reshape(b, s, NSA_HEADS * NSA_DV)


def mla_attention(c_q, c_kv, k_rope, g_q, w_uq, g_kv, w_uk, w_uv, cos, sin):
    b, s = c_q.shape[:2]
    q = (rmsnorm(c_q, g_q) @ w_uq).reshape(b, s, MLA_HEADS, MLA_NOPE + MLA_ROPE)
    q_nope = q[..., :MLA_NOPE]
    q_rope = apply_rope(q[..., MLA_NOPE:], cos, sin)
    ckv = rmsnorm(c_kv, g_kv)
    k_nope = (ckv @ w_uk).reshape(b, s, MLA_HEADS, MLA_NOPE)
    v = (ckv @ w_uv).reshape(b, s, MLA_HEADS, MLA_DV)
    k_r = apply_rope(k_rope, cos, sin)
    scale = (MLA_NOPE + MLA_ROPE) ** -0.5
    kpos = jnp.arange(s)

    def block(i):
        s0 = i * Q_BLOCK
        t = s0 + jnp.arange(Q_BLOCK)
        qn = lax.dynamic_slice_in_dim(q_nope, s0, Q_BLOCK, axis=1)
        qr = lax.dynamic_slice_in_dim(q_rope, s0, Q_BLOCK, axis=1)
        sc = (jnp.einsum('bqhd,bkhd->bhqk', qn, k_nope)
              + jnp.einsum('bqhd,bkd->bhqk', qr, k_r)) * scale
        p = masked_softmax(sc, kpos[None, :] <= t[:, None])
        o = jnp.einsum('bhqk,bkhd->bqhd', p.astype(v.dtype), v)
        return o.reshape(b, Q_BLOCK, MLA_HEADS * MLA_DV).astype(c_q.dtype)

    out = lax.map(block, jnp.arange(s // Q_BLOCK))
    return out.transpose(1, 0, 2, 3).reshape(b, s, MLA_HEADS * MLA_DV)


def memory_cross_attention(xn, memn, wq, wkv, wo):
    b, s = xn.shape[:2]
    m = memn.shape[1]
    q = (xn @ wq).reshape(b, s, XA_HEADS, XA_DIM)
    k, v = jnp.split((memn @ wkv).reshape(m * 0 + b, m, 2, XA_HEADS, XA_DIM), 2, axis=2)
    k, v = k[:, :, 0], v[:, :, 0]
    sc = jnp.einsum('bqhd,bmhd->bhqm', q, k).astype(jnp.float32) * (XA_DIM ** -0.5)
    p = jax.nn.softmax(sc, axis=-1)
    o = jnp.einsum('bhqm,bmhd->bqhd', p.astype(v.dtype), v).reshape(b, s, XA_HEADS * XA_DIM)
    return (o @ wo).astype(xn.dtype)


def setup_inputs(seed: int = 0) -> dict:
    key = jax.random.key(seed)
    ks = jax.random.split(key, 32)
    L = DEPTH

    def nrm(k, shape, scale):
        return jax.random.normal(k, shape, jnp.float32) * scale

    def gain(k, shape):
        return 1.0 + 0.02 * jax.random.normal(k, shape, jnp.float32)

    return {
        "x": nrm(ks[0], (BATCH, SEQ, D_MODEL), 1.0),
        "mem": nrm(ks[1], (BATCH, MEM_LEN, D_MODEL), 1.0),
        "g_mix": gain(ks[2], (L, D_MODEL)),
        "w_in": nrm(ks[3], (L, D_MODEL, D_IN), D_MODEL ** -0.5),
        "cmp_pos_k": nrm(ks[4], (L, CMP_BLOCK, NSA_DK), 0.02),
        "cmp_w1_k": nrm(ks[5], (L, CMP_BLOCK * NSA_DK, NSA_DK), (CMP_BLOCK * NSA_DK) ** -0.5),
        "cmp_w2_k": nrm(ks[6], (L, NSA_DK, NSA_DK), NSA_DK ** -0.5),
        "cmp_pos_v": nrm(ks[7], (L, CMP_BLOCK, NSA_DV), 0.02),
        "cmp_w1_v": nrm(ks[8], (L, CMP_BLOCK * NSA_DV, NSA_DV), (CMP_BLOCK * NSA_DV) ** -0.5),
        "cmp_w2_v": nrm(ks[9], (L, NSA_DV, NSA_DV), NSA_DV ** -0.5),
        "mla_g_q": gain(ks[10], (L, MLA_Q_RANK)),
        "mla_w_uq": nrm(ks[11], (L, MLA_Q_RANK, MLA_HEADS * (MLA_NOPE + MLA_ROPE)), MLA_Q_RANK ** -0.5),
        "mla_g_kv": gain(ks[12], (L, MLA_KV_RANK)),
        "mla_w_uk": nrm(ks[13], (L, MLA_KV_RANK, MLA_HEADS * MLA_NOPE), MLA_KV_RANK ** -0.5),
        "mla_w_uv": nrm(ks[14], (L, MLA_KV_RANK, MLA_HEADS * MLA_DV), MLA_KV_RANK ** -0.5),
        "w_o_nsa": nrm(ks[15], (L, NSA_HEADS * NSA_DV, D_MODEL), (NSA_HEADS * NSA_DV) ** -0.5),
        "w_o_mla": nrm(ks[16], (L, MLA_HEADS * MLA_DV, D_MODEL), (MLA_HEADS * MLA_DV) ** -0.5),
        "w_out": nrm(ks[17], (L, D_MODEL, D_MODEL), D_MODEL ** -0.5),
        "g_xattn": gain(ks[18], (L, D_MODEL)),
        "g_mem": gain(ks[19], (L, D_MODEL)),
        "xa_wq": nrm(ks[20], (L, D_MODEL, XA_HEADS * XA_DIM), D_MODEL ** -0.5),
        "xa_wkv": nrm(ks[21], (L, D_MODEL, 2 * XA_HEADS * XA_DIM), D_MODEL ** -0.5),
        "xa_wo": nrm(ks[22], (L, XA_HEADS * XA_DIM, D_MODEL), (XA_HEADS * XA_DIM) ** -0.5),
        "g_mlp": gain(ks[23], (L, D_MODEL)),
        "w_ff1": nrm(ks[24], (L, D_MODEL, D_FF), D_MODEL ** -0.5),
        "w_ff2": nrm(ks[25], (L, D_FF, D_MODEL), D_FF ** -0.5),
        "g_final": gain(ks[26], (D_MODEL,)),
    }


def reference(x, mem, g_mix, w_in, cmp_pos_k, cmp_w1_k, cmp_w2_k, cmp_pos_v, cmp_w1_v, cmp_w2_v,
              mla_g_q, mla_w_uq, mla_g_kv, mla_w_uk, mla_w_uv, w_o_nsa, w_o_mla, w_out,
              g_xattn, g_mem, xa_wq, xa_wkv, xa_wo, g_mlp, w_ff1, w_ff2, g_final):
    b, s, _ = x.shape
    cos_a, sin_a = rope_tables(s, NSA_ROT)
    cos_b, sin_b = rope_tables(s, MLA_ROPE)
    bounds = [int(v) for v in np.cumsum(SPLITS)[:-1]]
    G = NSA_KV_HEADS
    h = x
    for l in range(DEPTH):
        xn = rmsnorm(h, g_mix[l])
        (q_a, k_c, v_c, k_s, v_s, k_w, v_w, g_nsa, c_q, c_kv, k_r,
         gate_a, gate_b) = jnp.split(xn @ w_in[l], bounds, axis=-1)
        q_a = partial_rope(q_a.reshape(b, s, NSA_HEADS, NSA_DK), cos_a, sin_a)
        k_c = partial_rope(k_c.reshape(b, s, G, NSA_DK), cos_a, sin_a)
        k_s = partial_rope(k_s.reshape(b, s, G, NSA_DK), cos_a, sin_a)
        k_w = partial_rope(k_w.reshape(b, s, G, NSA_DK), cos_a, sin_a)
        v_c = v_c.reshape(b, s, G, NSA_DV)
        v_s = v_s.reshape(b, s, G, NSA_DV)
        v_w = v_w.reshape(b, s, G, NSA_DV)
        g_nsa = jax.nn.sigmoid(g_nsa.reshape(b, s, NSA_HEADS, 3))
        o_a = nsa_attention(q_a, k_c, v_c, k_s, v_s, k_w, v_w, g_nsa,
                            cmp_pos_k[l], cmp_w1_k[l], cmp_w2_k[l],
                            cmp_pos_v[l], cmp_w1_v[l], cmp_w2_v[l])
        o_b = mla_attention(c_q, c_kv, k_r, mla_g_q[l], mla_w_uq[l], mla_g_kv[l],
                            mla_w_uk[l], mla_w_uv[l], cos_b, sin_b)
        mixed = (jax.nn.sigmoid(gate_a) * (o_a @ w_o_nsa[l])
                 + jax.nn.sigmoid(gate_b) * (o_b @ w_o_mla[l]))
        h = h + mixed @ w_out[l]
        h = h + memory_cross_attention(rmsnorm(h, g_xattn[l]), rmsnorm(mem, g_mem[l]),
                                       xa_wq[l], xa_wkv[l], xa_wo[l])
        hn = rmsnorm(h, g_mlp[l])
        h = h + jnp.square(jax.nn.relu(hn @ w_ff1[l])) @ w_ff2[l]
    return rmsnorm(h, g_final)
```

```python
import numpy as np
import concourse.bass as bass
import concourse.mybir as mybir
from concourse.bass_utils import run_bass_kernel_spmd

F32 = mybir.dt.float32
BF16 = mybir.dt.bfloat16
ALU = mybir.AluOpType
AF = mybir.ActivationFunctionType
AX = mybir.AxisListType

D = 2048
DFF = 8192
MEM = 256
EPS = 1e-6
XA_H, XA_D = 4, 128
SEM_CH = 12000
N_DSEM = 32


class Buf:
    __slots__ = ("w", "r", "name", "excl")

    def __init__(self, name="", excl=False):
        self.w = None
        self.r = {}
        self.name = name
        self.excl = excl


class Sched:
    def __init__(self, nc):
        self.nc = nc
        self.eng = {}
        for name, h in (("pe", nc.tensor), ("act", nc.scalar), ("dve", nc.vector),
                        ("pool", nc.gpsimd), ("sp", nc.sync)):
            self.eng[name] = dict(h=h, n=0, sems=[], seen={}, seen_d={})
        self.dsem = [nc.semaphore(f"dsem{i}").__enter__() for i in range(N_DSEM)]
        self.dcnt = [0] * N_DSEM
        self.dnext = {"sp": 0, "pool": 0}
        self.dbase = {"sp": 0, "pool": N_DSEM // 2}
        self.n_wait = 0

    def _sem(self, en, chunk):
        e = self.eng[en]
        while len(e["sems"]) <= chunk:
            e["sems"].append(self.nc.semaphore(f"prog_{en}_{len(e['sems'])}").__enter__())
        return e["sems"][chunk]

    def _wait(self, en, tok):
        e = self.eng[en]
        if tok[0] == "e":
            _, e2, seq = tok
            if e2 == en and en == "pe":
                return
            if e["seen"].get(e2, 0) >= seq:
                return
            chunk = (seq - 1) // SEM_CH
            e["h"].wait_ge(self._sem(e2, chunk), seq - chunk * SEM_CH)
            e["seen"][e2] = seq
            self.n_wait += 1
        else:
            _, s, val = tok
            if e["seen_d"].get(s, 0) >= val:
                return
            e["h"].wait_ge(self.dsem[s], val)
            e["seen_d"][s] = val
            self.n_wait += 1

    def _deps(self, en, reads, writes):
        for b in reads:
            if b.w is not None:
                self._wait(en, b.w)
        for b in writes:
            if b.w is not None:
                self._wait(en, b.w)
            for t in b.r.values():
                self._wait(en, t)

    def op(self, en, fn, reads=(), writes=()):
        e = self.eng[en]
        if any(b.excl for b in reads):
            writes = list(writes) + [b for b in reads if b.excl]
            reads = [b for b in reads if not b.excl]
        self._deps(en, reads, writes)
        ins = fn(e["h"])
        e["n"] += 1
        seq = e["n"]
        chunk = (seq - 1) // SEM_CH
        ins.then_inc(self._sem(en, chunk), 1)
        tok = ("e", en, seq)
        for b in reads:
            b.r[en] = tok
        for b in writes:
            b.w = tok
            b.r = {}
        return tok

    def dma(self, qn, out, in_, reads=(), writes=()):
        e = self.eng[qn]
        self._deps(qn, reads, writes)
        s = self.dbase[qn] + self.dnext[qn]
        self.dnext[qn] = (self.dnext[qn] + 1) % (N_DSEM // 2)
        if self.dcnt[s] > 0:
            self._wait(qn, ("d", s, self.dcnt[s]))
        ins = e["h"].dma_start(out=out, in_=in_)
        self.dcnt[s] += 16
        ins.then_inc(self.dsem[s], 16)
        tok = ("d", s, self.dcnt[s])
        for b in reads:
            b.r[("d", s)] = tok
        for b in writes:
            b.w = tok
            b.r = {}
        return tok

    def barrier(self):
        toks = [("e", en, e["n"]) for en, e in self.eng.items() if e["n"] > 0]
        dt = [("d", s, c) for s, c in enumerate(self.dcnt) if c > 0]
        for en in self.eng:
            for t in toks:
                if t[1] != en:
                    self._wait(en, t)
            for t in dt:
                self._wait(en, t)

    def finish(self):
        for s, c in enumerate(self.dcnt):
            if c > 0:
                self._wait("sp", ("d", s, c))


class T:
    def __init__(self, nc, name, shape, dtype, psum=False):
        if psum:
            self.t = nc.psum_tensor(name, shape, dtype).__enter__()
        else:
            self.t = nc.sbuf_tensor(name, shape, dtype).__enter__()
        self.b = Buf(name, excl=psum)

    def __getitem__(self, idx):
        return self.t[idx]


NSA_H, NSA_G, NSA_DK, NSA_DV, NSA_ROT = 16, 4, 96, 64, 24
MLA_H, MLA_NOPE, MLA_ROPE, MLA_DV, MLA_QR, MLA_KVR = 16, 64, 32, 64, 512, 256
ROPE_THETA = 500000.0
C_Q, C_KC, C_VC, C_KS, C_VS, C_KW, C_VW, C_GN, C_CQ, C_CKV, C_KR, C_GA, C_GB = (
    0, 1536, 1920, 2176, 2560, 2816, 3200, 3456, 3504, 4016, 4272, 4304, 6352)
D_IN = 8400
NSA_SCALE = float(NSA_DK) ** -0.5
MLA_SCALE = float(MLA_NOPE + MLA_ROPE) ** -0.5
WKC = 1984


def build_program(S, dbg=None):
    NQT = S // 1024
    NOWN = NQT * 512
    NT = S // 512
    NKT = S // 128
    NC = S // 16 - 1
    NCT = (NC + 127) // 128
    nc = bass.Bass("TRN2", target_bir_lowering=False)
    sc = Sched(nc)

    def din(name, shape, dt=F32):
        return nc.dram_tensor(name, list(shape), dt, kind="ExternalInput").ap()

    x_full = din("x_full", [S, D])
    x_own = din("x_own", [NOWN, D])
    mem = din("mem", [MEM, D])
    gvecs = {k: din(k, [1, D]) for k in ("g_mix", "g_xattn", "g_mem", "g_mlp", "g_final")}
    g_q_in = din("mla_g_q", [1, MLA_QR])
    g_kv_in = din("mla_g_kv", [1, MLA_KVR])
    w_in = din("w_in", [D, D_IN])
    cmp_pos_k = din("cmp_pos_k", [32, NSA_DK])
    cmp_w1_k = din("cmp_w1_k", [32 * NSA_DK, NSA_DK])
    cmp_w2_k = din("cmp_w2_k", [NSA_DK, NSA_DK])
    cmp_pos_v = din("cmp_pos_v", [32, NSA_DV])
    cmp_w1_v = din("cmp_w1_v", [32 * NSA_DV, NSA_DV])
    cmp_w2_v = din("cmp_w2_v", [NSA_DV, NSA_DV])
    w_uq = din("mla_w_uq", [MLA_QR, MLA_H * 96])
    w_uk = din("mla_w_uk", [MLA_KVR, MLA_H * 64])
    w_uv = din("mla_w_uv", [MLA_KVR, MLA_H * 64])
    w_o_nsa = din("w_o_nsa", [1024, D])
    w_o_mla = din("w_o_mla", [1024, D])
    w_out = din("w_out", [D, D])
    xa_wq = din("xa_wq", [D, 512])
    xa_wkv = din("xa_wkv", [D, 1024])
    xa_wo = din("xa_wo", [512, D])
    w_ff1 = din("w_ff1", [D, DFF])
    w_ff2 = din("w_ff2", [DFF, D])
    ident_in = din("ident", [128, 128])
    rkn_in = din("rkn", [128, S])
    tkr_in = din("tkr", [S, 64])
    rqn_in = din("rqn", [NQT, 128, 512])
    rqm_in = din("rqm", [NQT, 128, 512])
    qpos_in = din("qpos", [NQT, 512])
    cur_in = din("cur", [NQT, 128, 4])
    kpos_in = din("kpos", [128, NKT])
    cend_in = din("cend", [128, NCT])
    jidx_in = din("jidx", [128, 128])
    msel_in = din("msel", [128, NCT, 128])
    gsel_in = din("gsel", [128, S])
    out = nc.dram_tensor("out", [NOWN, D], F32, kind="ExternalOutput").ap()

    KN = nc.dram_tensor("KN", [12, 96, S], BF16).ap()
    KM = nc.dram_tensor("KM", [MLA_H, 96, S], BF16).ap()
    VSW = nc.dram_tensor("VSW", [NKT, 128, 8, 65], BF16).ap()
    VM = nc.dram_tensor("VM", [NKT, 128, MLA_H, 65], BF16).ap()
    VCT = nc.dram_tensor("VCT", [256, S], BF16).ap()
    WB = {}
    for nm, src in (("w_in", w_in), ("mla_w_uq", w_uq), ("w_o_nsa", w_o_nsa), ("w_o_mla", w_o_mla), ("w_out", w_out),
                    ("xa_wq", xa_wq), ("xa_wo", xa_wo), ("w_ff1", w_ff1), ("w_ff2", w_ff2)):
        WB[id(src)] = (nc.dram_tensor(nm + "_bf", list(src.shape), BF16).ap(), Buf(nm + "_bf"), src)
    use_bf = [False]

    def wsrc(W):
        if use_bf[0] and id(W) in WB:
            return WB[id(W)][0], [WB[id(W)][1]]
        return W, []

    def precast():
        for wb, buf, src in WB.values():
            R = src.shape[0]
            step = 256
            for r0 in range(0, R, step):
                sc.dma("pool", wb[r0:r0 + step, :], src[r0:r0 + step, :], writes=[buf])
        use_bf[0] = True

    ident = T(nc, "identb", [128, 128], BF16)
    gbs = T(nc, "gbs", [128, D], F32)
    h = T(nc, "h", [128, 4, D], F32)
    xnb = [T(nc, "xnb0", [128, D], BF16)]
    stat = T(nc, "stat", [128, 16], F32)
    actT = T(nc, "actT", [128, 16, 512], BF16)
    wsl = [T(nc, f"wsl{i}", [128, 16, 512], BF16) for i in range(2)]
    rtmp = [T(nc, f"rtmp{i}", [128, 512], F32) for i in range(2)]
    kx = T(nc, "kx_sb", [128, XA_H, MEM], BF16)
    vx = T(nc, "vx", [128, 2, XA_H, 132], BF16)
    rden = T(nc, "rden", [128, 8], F32)
    kposc = T(nc, "kposc", [128, NKT], F32)
    cendc = T(nc, "cendc", [128, NCT], F32)
    kccT = T(nc, "kccT", [128, NSA_G, NCT * 128], BF16)
    vcaug = T(nc, "vcaug", [128, NCT, NSA_G, 196], BF16)
    ARENA = 41472
    arena = T(nc, "arena", [128, ARENA], BF16)
    fsm = T(nc, "fsm", [128, 3072], F32)
    ps = [T(nc, f"ps{i}", [128, 512], F32, psum=True) for i in range(6)]
    psTs = [T(nc, f"psT{i}", [128, 1024], BF16, psum=True) for i in range(2)]

    class View:
        def __init__(self, ap, name, buf=None):
            self.ap = ap
            self.b = buf if buf is not None else Buf(name)

        def __getitem__(self, idx):
            return self.ap[idx]

    def shaped(ap, shape):
        if len(shape) == 2:
            ap = ap.rearrange("p (a b) -> p a b", b=shape[1])
        elif len(shape) == 3:
            ap = ap.rearrange("p (a b c) -> p a b c", b=shape[1], c=shape[2])
        return ap

    def carve(name, off, shape, buf=None):
        n = int(np.prod(shape))
        assert off + n <= ARENA, (name, off, n)
        return View(shaped(arena[:, off:off + n], shape), name, buf)

    hbf = h[:, :, :].rearrange("p s d -> p (s d)").bitcast(BF16)

    def carve_h(name, off, shape):
        n = int(np.prod(shape))
        assert off + n <= 16384
        return View(shaped(hbf[:, off:off + n], shape), name, h.b)

    def fcarve(name, off, shape):
        n = int(np.prod(shape))
        assert off + n <= 3072
        return View(shaped(fsm[:, off:off + n], shape), name)

    hid = carve("hid", 0, [64, 512])
    qx = carve("qx", 0, [XA_H, 512])
    pTx = [carve(f"pTx{i}", 2048 + i * 512, [512]) for i in range(2)]
    ox = carve("ox", 3072, [4 * 512])
    oxT = carve("oxT", 5120, [4, 512])
    memT = carve("memT", 8192, [16, MEM])
    WK = carve("WK", 0, [16, WKC])
    Qb = Buf("Q")
    Q = carve("Q", 0, [16, 512], Qb)
    oaT = carve("oaT", 0, [8, 512], Qb)
    obT = carve("obT", 4096, [8, 512], Qb)
    Ob = Buf("O")
    oa = carve("oa", 8192, [4096], Ob)
    ob = carve("ob", 12288, [4096], Ob)
    oa3 = View(oa[:, :].rearrange("p (s c) -> p s c", c=1024), "oa3", Ob)
    ob3 = View(ob[:, :].rearrange("p (s c) -> p s c", c=1024), "ob3", Ob)
    mixT = carve("mixT", 8192, [16, 512], Ob)
    Ksl = [carve(f"Ksl{i}", 16384 + i * 1536, [1536]) for i in range(2)]
    Vsl = [carve(f"Vsl{i}", 19456 + i * 784, [12, 65]) for i in range(2)]
    pT = [carve(f"pT{i}", 29344 + i * 512, [512]) for i in range(6)]
    smask = [carve(f"smask{i}", 22560 + i * 512, [512]) for i in range(2)]
    selT = carve("selT", 23584, [NSA_G, 512])
    cqn = carve("cqn", 25632, [512])
    cqnT = carve("cqnT", 26144, [4, 512])
    sel01 = carve("sel01", 28192, [128])
    kst = [carve(f"kst{i}", 28320 + i * 512, [512]) for i in range(2)]
    GS = carve("GS", ARENA - S, [S]) if S <= 8192 else None
    assert 32416 <= ARENA - S
    cm = [carve_h(f"cm{z}", z * 512, [512]) for z in range(8)]
    wm = [carve_h(f"wm{z}", 4096 + z * 512, [512]) for z in range(12)]
    cmk = [carve_h(f"cmk{z}", 10240 + z * 512, [512]) for z in range(4)]
    qrow = fcarve("qrow", 0, [512])
    rq = fcarve("rq", 512, [512])
    gates = fcarve("gates", 1024, [4, 48])
    tmpo = fcarve("tmpo", 1216, [4, 256])
    impacc = fcarve("impacc", 2240, [4, 128])
    scoreA = fcarve("scoreA", 2752, [128])
    scoreB = fcarve("scoreB", 2880, [128])
    small = T(nc, "small_sb", [128, 64], F32)
    jidx = T(nc, "jidx_sb", [128, 128], F32)
    e0 = T(nc, "e0", [128, 128], F32)
    tkA = [View(h[:, 3, s * 256:s * 256 + 128], f"tkA{s}", h.b) for s in range(4)]
    tkB = [View(h[:, 3, s * 256 + 128:s * 256 + 256], f"tkB{s}", h.b) for s in range(4)]
    curc = T(nc, "curc", [128, 4], F32)
    gq_b = View(gbs[:, 0:MLA_QR], "gq_b", gbs.b)
    gkv_b = fcarve("gkv_b", 768, [MLA_KVR])

    def load_g(name):
        sc.dma("sp", gbs[:], gvecs[name].partition_broadcast(128), writes=[gbs.b])
        return gbs

    wslot_i = [0]

    def next_wslot():
        slot = wsl[wslot_i[0] % 2]
        wslot_i[0] += 1
        return slot

    def load_w(W, k0, nk, c0, ncols):
        slot = next_wslot()
        Ws, rb = wsrc(W)
        src = Ws[k0 * 128:(k0 + nk) * 128, c0:c0 + ncols].rearrange("(k p) c -> p k c", p=128)
        for a in range(0, nk, 4):
            n = min(4, nk - a)
            sc.dma("pool", slot[:, a:a + n, 0:ncols], src[:, a:a + n, :], reads=rb, writes=[slot.b])
        return slot

    import os
    KSTOP = int(os.environ.get("KSTOP", "0"))

    class _Stop(Exception):
        pass

    def ck(n):
        if KSTOP == n:
            raise _Stop()

    tr_i = [0]

    def transposes(src_t, src_cols, dst_fn, reads, dst_buf, nchunk, rows=128, width=128):
        for j0 in range(0, nchunk, 4):
            n = min(4, nchunk - j0)
            psT = psTs[tr_i[0] % 2]
            tr_i[0] += 1
            for j in range(n):
                sc.op("pe", lambda e: e.transpose(
                    out=psT[0:width, j * 128: j * 128 + rows],
                    in_=src_t[0:rows, src_cols + (j0 + j) * width: src_cols + (j0 + j + 1) * width],
                    identity=ident[0:rows, 0:rows]), reads=list(reads) + [ident.b], writes=[psT.b])
            srcv = psT[0:width, 0:n * 128].rearrange("p (j c) -> p j c", c=128)[:, :, 0:rows]
            if tr_i[0] % 2:
                sc.op("act", lambda e: e.copy(out=dst_fn(j0, n), in_=srcv), writes=[psT.b, dst_buf])
            else:
                sc.op("dve", lambda e: e.tensor_copy(out=dst_fn(j0, n), in_=srcv), writes=[psT.b, dst_buf])

    def rstd_of(src_ap, src_bufs, width, col, junk_ap, junk_buf):
        sc.op("dve", lambda e: e.memset(stat[:, col:col + 1], 0.0), writes=[stat.b])
        sc.op("act", lambda e: e.activation(out=junk_ap, in_=src_ap, func=AF.Square, scale=float(width) ** -0.5,
                                            accum_out=stat[:, col:col + 1]),
              reads=src_bufs, writes=[junk_buf, stat.b])
        sc.op("dve", lambda e: e.tensor_scalar_add(out=stat[:, col:col + 1], in0=stat[:, col:col + 1], scalar1=EPS),
              writes=[stat.b])
        sc.op("act", lambda e: e.sqrt(out=stat[:, col:col + 1], in_=stat[:, col:col + 1]), writes=[stat.b])
        sc.op("dve", lambda e: e.reciprocal(out=stat[:, col:col + 1], in_=stat[:, col:col + 1]), writes=[stat.b])

    def norm_T(src, nsub, g, dst):
        for s in range(nsub):
            xb = xnb[0]
            rstd_of(src[:, s, :], [src.b], D, s, xb[:], xb.b)
            sc.op("dve", lambda e: e.scalar_tensor_tensor(out=xb[:], in0=src[:, s, :], scalar=stat[:, s:s + 1],
                                                          in1=g[:], op0=ALU.mult, op1=ALU.mult),
                  reads=[src.b, stat.b, g.b], writes=[xb.b])
            transposes(xb, 0, lambda j0, n: dst[:, j0:j0 + n, s * 128:(s + 1) * 128], [xb.b], dst.b, 16)

    def mm_fm(pst, M, wslot, c0, act, nk):
        for kc in range(nk):
            sc.op("pe", lambda e: e.matmul(pst[0:M, :], lhsT=wslot[:, kc, c0:c0 + M], rhs=act[:, kc, :],
                                           start=(kc == 0), stop=(kc == nk - 1)),
                  reads=[wslot.b, act.b], writes=[pst.b])

    def mm_tm(pst, ncols, act, s, wslot, c0, nk, kc0=0, first=True, last=True):
        for kc in range(nk):
            sc.op("pe", lambda e: e.matmul(pst[:, 0:ncols], lhsT=act[:, kc0 + kc, s * 128:(s + 1) * 128],
                                           rhs=wslot[:, kc, c0:c0 + ncols],
                                           start=(first and kc == 0), stop=(last and kc == nk - 1)),
                  reads=[wslot.b, act.b], writes=[pst.b])

    def rope_evac(pst, table, dst_ap, dst_buf, lo, n, swp, copy_rows):
        ta, tb = rtmp
        sc.op("act", lambda e: e.copy(out=dst_ap[0:copy_rows, :], in_=pst[0:copy_rows, :]), reads=[pst.b], writes=[dst_buf])
        sc.op("dve", lambda e: e.tensor_tensor(out=ta[lo:lo + n, :], in0=pst[swp:swp + n, :], in1=table[swp:swp + n, :],
                                               op=ALU.mult), reads=[pst.b, table.b], writes=[ta.b])
        sc.op("dve", lambda e: e.tensor_tensor(out=tb[lo:lo + n, :], in0=pst[lo:lo + n, :], in1=table[lo:lo + n, :],
                                               op=ALU.mult), reads=[pst.b, table.b], writes=[tb.b])
        sc.op("dve", lambda e: e.tensor_tensor(out=dst_ap[lo:lo + n, :], in0=ta[lo:lo + n, :], in1=tb[lo:lo + n, :],
                                               op=ALU.add), reads=[ta.b, tb.b], writes=[dst_buf])

    def _body():
        sc.dma("pool", ident[:], ident_in, writes=[ident.b])
        sc.dma("sp", kposc[:], kpos_in, writes=[kposc.b])
        sc.dma("sp", cendc[:], cend_in, writes=[cendc.b])
        sc.dma("sp", jidx[:], jidx_in, writes=[jidx.b])
        sc.op("dve", lambda e: e.tensor_single_scalar(out=e0[:], in_=jidx[:], scalar=0.0, op=ALU.is_equal),
              reads=[jidx.b], writes=[e0.b])

        sc.dma("sp", h[:, 0:2, :], mem.rearrange("(s p) d -> p s d", p=128), writes=[h.b])
        norm_T(h, 2, load_g("g_mem"), memT)
        wk = load_w(xa_wkv, 0, 16, 0, 512)
        for hh in range(XA_H):
            pb = ps[hh % 2]
            for kc in range(16):
                sc.op("pe", lambda e: e.matmul(pb[:, 0:MEM], lhsT=wk[:, kc, hh * 128:(hh + 1) * 128],
                                               rhs=memT[:, kc, :], start=(kc == 0), stop=(kc == 15)),
                      reads=[wk.b, memT.b], writes=[pb.b])
            sc.op("act", lambda e: e.copy(out=kx[:, hh, :], in_=pb[:, 0:MEM]), reads=[pb.b], writes=[kx.b])
        wv = load_w(xa_wkv, 0, 16, 512, 512)
        sc.op("dve", lambda e: e.memset(vx[:], 1.0), writes=[vx.b])
        for mt in range(2):
            pb = ps[2 + mt]
            for kc in range(16):
                sc.op("pe", lambda e: e.matmul(pb[:, :], lhsT=memT[:, kc, mt * 128:(mt + 1) * 128],
                                               rhs=wv[:, kc, :], start=(kc == 0), stop=(kc == 15)),
                      reads=[wv.b, memT.b], writes=[pb.b])
            sc.op("dve", lambda e: e.tensor_copy(out=vx[:, mt, :, 0:128],
                                                 in_=pb[:, :].rearrange("p (h d) -> p h d", d=128)),
                  reads=[pb.b], writes=[vx.b])
        sc.barrier()
        ck(1)

        fam = [C_KC, C_KS, C_KW]
        for j in range(12):
            base = fam[j // 4] + (j % 4) * 96
            for a in (0, 8):
                sc.dma("pool", WK[:, a:a + 8, j * 120:j * 120 + 96],
                       w_in[a * 128:(a + 8) * 128, base:base + 96].rearrange("(k p) c -> p k c", p=128), writes=[WK.b])
        for (dst0, src0, n) in ((1440, C_VC, 256), (1696, C_CKV, 288)):
            for a in range(0, 16, 4):
                sc.dma("pool", WK[:, a:a + 4, dst0:dst0 + n],
                       w_in[a * 128:(a + 4) * 128, src0:src0 + n].rearrange("(k p) c -> p k c", p=128), writes=[WK.b])
        WKs = WK[:, :, 0:1440].rearrange("p k (j c) -> p k j c", c=120)
        sc.op("dve", lambda e: e.tensor_copy(out=WKs[:, :, :, 96:108], in_=WKs[:, :, :, 12:24]), writes=[WK.b])
        sc.op("dve", lambda e: e.tensor_copy(out=WKs[:, :, :, 108:120], in_=WKs[:, :, :, 0:12]), writes=[WK.b])
        wA = wsl[0]
        wAf = wA[:, :, :].rearrange("p k c -> p (k c)")
        WUK = wAf[:, 0:2048].rearrange("p (k c) -> p k c", c=1024)
        WUV = wAf[:, 2048:4096].rearrange("p (k c) -> p k c", c=1024)
        sc.dma("pool", WUK, w_uk.rearrange("(k p) c -> p k c", p=128), writes=[wA.b])
        sc.dma("pool", WUV, w_uv.rearrange("(k p) c -> p k c", p=128), writes=[wA.b])
        WV = wsl[1]
        for (dst0, src0) in ((0, C_VS), (256, C_VW)):
            for a in range(0, 16, 4):
                sc.dma("pool", WV[:, a:a + 4, dst0:dst0 + 256],
                       w_in[a * 128:(a + 4) * 128, src0:src0 + 256].rearrange("(k p) c -> p k c", p=128), writes=[WV.b])
        precast()
        a0 = 16 * WKC
        vst = carve("vst", a0, [4, 8, 65])
        vmst = carve("vmst", a0 + 2080, [4, 16, 65])
        ckvn = carve("ckvn", a0 + 6240, [256])
        krb = carve("krb", a0 + 6496, [32])
        ckvnT = carve("ckvnT", a0 + 6528, [2, 512])
        krT = carve("krT", a0 + 7552, [512])
        kstA = [carve(f"kstA{i}", a0 + 8064 + i * 512, [512]) for i in range(3)]
        rk = fcarve("rk", 0, [512])
        tkr = fcarve("tkr", 512, [4, 64])
        sc.dma("sp", gkv_b[:, :], g_kv_in.partition_broadcast(128), writes=[gkv_b.b])
        sc.op("dve", lambda e: e.memset(vst[:], 1.0), writes=[vst.b])
        sc.op("dve", lambda e: e.memset(vmst[:], 1.0), writes=[vmst.b])
        gmix = load_g("g_mix")
        kst_i = 0
        for t in range(NT):
            sc.dma("sp", h[:], x_full[t * 512:(t + 1) * 512, :].rearrange("(s p) d -> p s d", p=128), writes=[h.b])
            sc.dma("sp", rk[:, :], rkn_in[:, t * 512:(t + 1) * 512], writes=[rk.b])
            sc.dma("sp", tkr[:, :, :], tkr_in[t * 512:(t + 1) * 512, :].rearrange("(s p) c -> p s c", p=128), writes=[tkr.b])
            norm_T(h, 4, gmix, actT)
            for j in range(12):
                pb = ps[j % 2]
                mm_fm(pb, 120, WK, j * 120, actT, 16)
                ks_ = kstA[kst_i % 3]
                kst_i += 1
                rope_evac(pb, rk, ks_, ks_.b, 0, 24, 96, 96)
                sc.dma("sp", KN[j, :, t * 512:(t + 1) * 512], ks_[0:96, :], reads=[ks_.b])
            for j in range(2):
                pb = ps[j % 2]
                mm_fm(pb, 128, WK, 1440 + j * 128, actT, 16)
                ks_ = kstA[kst_i % 3]
                kst_i += 1
                sc.op("act", lambda e: e.copy(out=ks_[:, :], in_=pb[:, :]), reads=[pb.b], writes=[ks_.b])
                sc.dma("sp", VCT[j * 128:(j + 1) * 128, t * 512:(t + 1) * 512], ks_[:, :], reads=[ks_.b])
            for s in range(4):
                pa, pv = ps[2], ps[3]
                mm_tm(pa, 288, actT, s, WK, 1696, 16)
                mm_tm(pv, 512, actT, s, WV, 0, 16)
                sc.op("act", lambda e: e.copy(out=vst[:, s, :, 0:64],
                                              in_=pv[:, :].rearrange("p (j c) -> p j c", c=64)),
                      reads=[pv.b], writes=[vst.b])
                rstd_of(pa[:, 0:256], [pa.b], 256, 4 + s, ckvn[:, :], ckvn.b)
                sc.op("dve", lambda e: e.scalar_tensor_tensor(out=ckvn[:, :], in0=pa[:, 0:256], scalar=stat[:, 4 + s:5 + s],
                                                              in1=gkv_b[:, :], op0=ALU.mult, op1=ALU.mult),
                      reads=[pa.b, stat.b, gkv_b.b], writes=[ckvn.b])
                transposes(ckvn, 0, lambda j0, n: ckvnT[:, j0:j0 + n, s * 128:(s + 1) * 128], [ckvn.b], ckvnT.b, 2)
                ta, tb = rtmp
                sc.op("dve", lambda e: e.tensor_tensor(out=ta[:, 0:32], in0=pa[:, 256:288], in1=tkr[:, s, 0:32], op=ALU.mult),
                      reads=[pa.b, tkr.b], writes=[ta.b])
                sc.op("dve", lambda e: e.tensor_tensor(out=tb[:, 0:16], in0=pa[:, 272:288], in1=tkr[:, s, 32:48], op=ALU.mult),
                      reads=[pa.b, tkr.b], writes=[tb.b])
                sc.op("dve", lambda e: e.tensor_tensor(out=tb[:, 16:32], in0=pa[:, 256:272], in1=tkr[:, s, 48:64], op=ALU.mult),
                      reads=[pa.b, tkr.b], writes=[tb.b])
                sc.op("dve", lambda e: e.tensor_tensor(out=krb[:, :], in0=ta[:, 0:32], in1=tb[:, 0:32], op=ALU.add),
                      reads=[ta.b, tb.b], writes=[krb.b])
                transposes(krb, 0, lambda j0, n: krT[0:32, s * 128:(s + 1) * 128].rearrange("p (j c) -> p j c", j=1),
                           [krb.b], krT.b, 1, rows=128, width=32)
            sc.dma("sp", VSW[t * 4:(t + 1) * 4].rearrange("s p j c -> p s j c"), vst[:], reads=[vst.b])
            for hh in range(MLA_H):
                sc.dma("sp", KM[hh, 64:96, t * 512:(t + 1) * 512], krT[0:32, :], reads=[krT.b])
            for pr in range(8):
                pb = ps[pr % 2]
                for kc in range(2):
                    sc.op("pe", lambda e: e.matmul(pb[:, :], lhsT=WUK[:, kc, pr * 128:(pr + 1) * 128], rhs=ckvnT[:, kc, :],
                                                   start=(kc == 0), stop=(kc == 1)),
                          reads=[wA.b, ckvnT.b], writes=[pb.b])
                ks_ = kstA[kst_i % 3]
                kst_i += 1
                sc.op("act", lambda e: e.copy(out=ks_[:, :], in_=pb[:, :]), reads=[pb.b], writes=[ks_.b])
                sc.dma("sp", KM[2 * pr, 0:64, t * 512:(t + 1) * 512], ks_[0:64, :], reads=[ks_.b])
                sc.dma("sp", KM[2 * pr + 1, 0:64, t * 512:(t + 1) * 512], ks_[64:128, :], reads=[ks_.b])
            for s in range(4):
                for hf in range(2):
                    pb = ps[4 + hf]
                    for kc in range(2):
                        sc.op("pe", lambda e: e.matmul(pb[:, :], lhsT=ckvnT[:, kc, s * 128:(s + 1) * 128],
                                                       rhs=WUV[:, kc, hf * 512:(hf + 1) * 512],
                                                       start=(kc == 0), stop=(kc == 1)),
                              reads=[wA.b, ckvnT.b], writes=[pb.b])
                    sc.op("dve", lambda e: e.tensor_copy(out=vmst[:, s, hf * 8:(hf + 1) * 8, 0:64],
                                                         in_=pb[:, :].rearrange("p (j c) -> p j c", c=64)),
                          reads=[pb.b], writes=[vmst.b])
            sc.dma("sp", VM[t * 4:(t + 1) * 4].rearrange("s p j c -> p s j c"), vmst[:], reads=[vmst.b])
        sc.barrier()
        ck(2)

        kcs = carve("kcs", 0, [S])
        w1s = carve("w1s", S, [32, 96])
        w2s = carve("w2s", S + 3072, [96])
        poss = carve("poss", S + 3200, [96])
        posT = carve("posT", S + 3328, [32])
        gl = carve("gl", S + 3392, [512])
        ub = fcarve("ub", 0, [512])
        tb_ = fcarve("tb_", 512, [512])
        b1 = small
        sc.op("dve", lambda e: e.memset(kccT[:], 0.0), writes=[kccT.b])
        sc.op("dve", lambda e: e.memset(vcaug[:], 0.0), writes=[vcaug.b])
        sc.op("dve", lambda e: e.memset(vcaug[:, :, :, 64:65], 1.0), writes=[vcaug.b])
        for g in range(NSA_G):
            sc.dma("pool", vcaug[:, :, g, 65:193], msel_in, writes=[vcaug.b])
        for which in range(2):
            dd = NSA_DK if which == 0 else NSA_DV
            posi, w1i, w2i = (cmp_pos_k, cmp_w1_k, cmp_w2_k) if which == 0 else (cmp_pos_v, cmp_w1_v, cmp_w2_v)
            sc.dma("pool", w1s[0:dd, :, 0:dd], w1i.rearrange("(l d) e -> d l e", d=dd), writes=[w1s.b])
            sc.dma("pool", w2s[0:dd, 0:dd], w2i, writes=[w2s.b])
            sc.dma("pool", poss[0:32, 0:dd], posi, writes=[poss.b])
            transposes(poss, 0, lambda j0, n: posT[0:dd, 0:32].rearrange("p (j c) -> p j c", j=1), [poss.b], posT.b, 1,
                       rows=32, width=dd)
            pb = ps[0]
            for l in range(32):
                sc.op("pe", lambda e: e.matmul(pb[0:dd, 0:1], lhsT=w1s[0:dd, l, 0:dd], rhs=posT[0:dd, l:l + 1],
                                               start=(l == 0), stop=(l == 31)),
                      reads=[w1s.b, posT.b], writes=[pb.b])
            sc.op("dve", lambda e: e.tensor_copy(out=b1[0:dd, which:which + 1], in_=pb[0:dd, 0:1]), reads=[pb.b], writes=[b1.b])
            for g in range(NSA_G):
                if which == 0:
                    sc.dma("sp", kcs[0:96, :], KN[g, :, :], writes=[kcs.b])
                else:
                    sc.dma("sp", kcs[0:64, :], VCT[g * 64:(g + 1) * 64, :], writes=[kcs.b])
                for c0 in range(0, NC, 512):
                    n = min(512, NC - c0)
                    pb = ps[1]
                    for l in range(32):
                        lo = 16 * c0 + l
                        sc.op("pe", lambda e: e.matmul(pb[0:dd, 0:n], lhsT=w1s[0:dd, l, 0:dd],
                                                       rhs=kcs[0:dd, lo:lo + 16 * (n - 1) + 1:16],
                                                       start=(l == 0), stop=(l == 31)),
                              reads=[w1s.b, kcs.b], writes=[pb.b])
                    sc.op("act", lambda e: e.activation(out=ub[0:dd, 0:n], in_=pb[0:dd, 0:n], func=AF.Identity,
                                                        bias=b1[0:dd, which:which + 1]),
                          reads=[pb.b, b1.b], writes=[ub.b])
                    sc.op("dve", lambda e: e.tensor_tensor(out=tb_[0:dd, 0:n], in0=ub[0:dd, 0:n], in1=ub[0:dd, 0:n], op=ALU.mult),
                          reads=[ub.b], writes=[tb_.b])
                    sc.op("dve", lambda e: e.tensor_scalar(out=tb_[0:dd, 0:n], in0=tb_[0:dd, 0:n], scalar1=0.044715, scalar2=1.0,
                                                           op0=ALU.mult, op1=ALU.add), writes=[tb_.b])
                    sc.op("dve", lambda e: e.tensor_tensor(out=tb_[0:dd, 0:n], in0=tb_[0:dd, 0:n], in1=ub[0:dd, 0:n], op=ALU.mult),
                          reads=[ub.b], writes=[tb_.b])
                    sc.op("act", lambda e: e.activation(out=tb_[0:dd, 0:n], in_=tb_[0:dd, 0:n], func=AF.Sigmoid,
                                                        scale=1.5957691216057308), writes=[tb_.b])
                    sc.op("dve", lambda e: e.tensor_tensor(out=gl[0:dd, 0:n], in0=tb_[0:dd, 0:n], in1=ub[0:dd, 0:n], op=ALU.mult),
                          reads=[ub.b, tb_.b], writes=[gl.b])
                    if which == 0:
                        pc = ps[2]
                        sc.op("pe", lambda e: e.matmul(pc[0:96, 0:n], lhsT=w2s[0:96, 0:96], rhs=gl[0:96, 0:n], start=True, stop=True),
                              reads=[w2s.b, gl.b], writes=[pc.b])
                        sc.op("act", lambda e: e.copy(out=kccT[0:96, g, c0:c0 + n], in_=pc[0:96, 0:n]), reads=[pc.b], writes=[kccT.b])
                    else:
                        for cc in range(0, n, 128):
                            m = min(128, n - cc)
                            pc = ps[2 + (cc // 128) % 2]
                            sc.op("pe", lambda e: e.matmul(pc[0:m, 0:64], lhsT=gl[0:64, cc:cc + m], rhs=w2s[0:64, 0:64],
                                                           start=True, stop=True), reads=[w2s.b, gl.b], writes=[pc.b])
                            sc.op("act", lambda e: e.copy(out=vcaug[0:m, (c0 + cc) // 128, g, 0:64], in_=pc[0:m, 0:64]),
                                  reads=[pc.b], writes=[vcaug.b])
        sc.barrier()
        ck(3)
        if GS is not None:
            sc.dma("pool", GS[:, :], gsel_in, writes=[GS.b])

        for qt in range(NQT):
            mixer(qt)
            sc.barrier()
            ck(5)
            xattn_mlp(qt)
            sc.barrier()

    att_i = [0]
    kv_i = [0]
    sbanks = [ps[0], ps[1], ps[4]] + [View(psTs[i][:, :].bitcast(F32), f"psTf{i}", psTs[i].b) for i in range(2)]

    def attention(heads, scale, Ksrc, Vsrc, ktiles, mask_fn, acc_of, pre_kt=None):
        LOOK = 4
        first = {hd: True for hd in heads}
        blocks = [ktiles[b0:b0 + 12] for b0 in range(0, len(ktiles), 12)]
        slots = {}

        def load_block(bi):
            blk = blocks[bi]
            n = len(blk)
            kt0 = blk[0]
            ksl = Ksl[kv_i[0] % 2]
            vsl = Vsl[kv_i[0] % 2]
            kv_i[0] += 1
            sc.dma("sp", ksl[0:96, 0:n * 128], Ksrc[:, kt0 * 128:(kt0 + n) * 128], writes=[ksl.b])
            sc.dma("sp", vsl[:, 0:n, :], Vsrc[kt0:kt0 + n].rearrange("t p c -> p t c"), writes=[vsl.b])
            slots[bi] = (ksl, vsl)

        pending = []
        load_block(0)
        for bi, blk in enumerate(blocks):
            ksl, vsl = slots[bi]
            staged = 0
            for i, kt in enumerate(blk):
                if pre_kt is not None:
                    pre_kt(kt)
                for hi, hd in enumerate(heads):
                    if staged == LOOK and bi + 1 < len(blocks):
                        load_block(bi + 1)
                    staged += 1
                    sb = sbanks[att_i[0] % 5]
                    p_t = pT[att_i[0] % 6]
                    att_i[0] += 1
                    sc.op("pe", lambda e: e.matmul(sb[:, :], lhsT=ksl[0:96, i * 128:(i + 1) * 128], rhs=Q[0:96, hd, :],
                                                   start=True, stop=True), reads=[ksl.b, Q.b], writes=[sb.b])
                    sc.op("act", lambda e: e.activation(out=p_t[:, :], in_=sb[:, :], func=AF.Exp, scale=scale),
                          reads=[sb.b], writes=[p_t.b])
                    mk = mask_fn(kt)
                    if mk is not None:
                        sc.op("dve", lambda e: e.tensor_tensor(out=p_t[:, :], in0=p_t[:, :], in1=mk[:, :], op=ALU.mult),
                              reads=[mk.b], writes=[p_t.b])

                    def stage_b(p_t=p_t, vsl=vsl, i=i, hi=hi, hd=hd):
                        acc = acc_of(hi)
                        for s in range(4):
                            sc.op("pe", lambda e: e.matmul(acc[:, s * 65:(s + 1) * 65], lhsT=p_t[:, s * 128:(s + 1) * 128],
                                                           rhs=vsl[:, i, :], start=(first[hd] and s == 0), stop=False,
                                                           skip_group_check=True),
                                  reads=[p_t.b, vsl.b], writes=[acc.b])
                        first[hd] = False

                    pending.append(stage_b)
                    if len(pending) > LOOK:
                        pending.pop(0)()
            if staged <= LOOK and bi + 1 < len(blocks):
                while pending:
                    pending.pop(0)()
                load_block(bi + 1)
        while pending:
            pending.pop(0)()

    def nsa_finalize(acc, ncol, hd, hl, br, first_branch):
        accv = acc[:, 0:4 * ncol].rearrange("p (s c) -> p s c", c=ncol) if ncol == 65 else None
        for s in range(4):
            a = accv[:, s, :] if accv is not None else acc[s // 2][:, (s % 2) * ncol:(s % 2) * ncol + ncol]
            ab = acc.b if accv is not None else acc[s // 2].b
            c0 = 32 + s
            sc.op("dve", lambda e: e.tensor_scalar_max(out=small[:, c0:c0 + 1], in0=a[:, 64:65], scalar1=1e-30),
                  reads=[ab], writes=[small.b])
            sc.op("dve", lambda e: e.reciprocal(out=small[:, c0:c0 + 1], in_=small[:, c0:c0 + 1]), writes=[small.b])
            if ncol != 65:
                sc.op("dve", lambda e: e.tensor_copy(out=small[:, 40 + s:41 + s], in_=small[:, c0:c0 + 1]), writes=[small.b])
            sc.op("dve", lambda e: e.tensor_tensor(out=small[:, c0:c0 + 1], in0=small[:, c0:c0 + 1],
                                                   in1=gates[:, s, hd * 3 + br:hd * 3 + br + 1], op=ALU.mult),
                  reads=[gates.b], writes=[small.b])
            dst = tmpo[:, hl, s * 64:(s + 1) * 64]
            if first_branch:
                sc.op("dve", lambda e: e.tensor_scalar(out=dst, in0=a[:, 0:64], scalar1=small[:, c0:c0 + 1], scalar2=None,
                                                       op0=ALU.mult), reads=[ab, small.b], writes=[tmpo.b])
            else:
                sc.op("dve", lambda e: e.scalar_tensor_tensor(out=dst, in0=a[:, 0:64], scalar=small[:, c0:c0 + 1], in1=dst,
                                                              op0=ALU.mult, op1=ALU.add), reads=[ab, small.b], writes=[tmpo.b])

    def mixer(qt):
        k = qt
        nkt = 8 * k + 8
        zone0 = 8 * k
        nct = min(NCT, (2 * k + 1) // 4 + 1)
        sc.dma("sp", h[:], x_own[qt * 512:(qt + 1) * 512, :].rearrange("(s p) d -> p s d", p=128), writes=[h.b])
        norm_T(h, 4, load_g("g_mix"), actT)
        sc.dma("sp", qrow[:, :], qpos_in[qt:qt + 1, :].partition_broadcast(128), writes=[qrow.b])
        sc.dma("sp", curc[:], cur_in[qt], writes=[curc.b])
        for z in range(8):
            kt = zone0 + z
            sc.op("dve", lambda e: e.tensor_scalar(out=cm[z][:, :], in0=qrow[:, :], scalar1=kposc[:, kt:kt + 1], scalar2=None,
                                                   op0=ALU.is_ge), reads=[qrow.b, kposc.b], writes=[h.b])
        for z in range(12):
            kt = zone0 - 4 + z
            if kt < 0:
                continue
            sc.op("dve", lambda e: e.tensor_scalar(out=wm[z][:, :], in0=qrow[:, :], scalar1=kposc[:, kt:kt + 1], scalar2=511.0,
                                                   op0=ALU.subtract, op1=ALU.is_le), reads=[qrow.b, kposc.b], writes=[h.b])
            if z >= 4:
                sc.op("dve", lambda e: e.tensor_tensor(out=wm[z][:, :], in0=wm[z][:, :], in1=cm[z - 4][:, :], op=ALU.mult),
                      writes=[h.b])
        for ct in range(nct):
            sc.op("dve", lambda e: e.tensor_scalar(out=cmk[ct][:, :], in0=qrow[:, :], scalar1=cendc[:, ct:ct + 1], scalar2=None,
                                                   op0=ALU.is_ge), reads=[qrow.b, cendc.b], writes=[h.b])
        for s in range(4):
            A, Bc = tkA[s], tkB[s]
            sc.op("dve", lambda e: e.tensor_scalar(out=Bc[:], in0=jidx[:], scalar1=curc[:, s:s + 1], scalar2=None, op0=ALU.subtract),
                  reads=[jidx.b, curc.b], writes=[Bc.b])
            sc.op("dve", lambda e: e.tensor_single_scalar(out=A[:], in_=Bc[:], scalar=0.0, op=ALU.is_le), reads=[Bc.b], writes=[A.b])
            sc.op("dve", lambda e: e.tensor_single_scalar(out=Bc[:], in_=Bc[:], scalar=-1.0, op=ALU.is_ge), writes=[Bc.b])
            sc.op("dve", lambda e: e.tensor_tensor(out=Bc[:], in0=Bc[:], in1=A[:], op=ALU.mult), reads=[A.b], writes=[Bc.b])
            sc.op("dve", lambda e: e.tensor_tensor(out=Bc[:], in0=Bc[:], in1=e0[:], op=ALU.max), reads=[e0.b], writes=[Bc.b])
            sc.op("dve", lambda e: e.tensor_tensor(out=A[:], in0=A[:], in1=Bc[:], op=ALU.subtract), reads=[Bc.b], writes=[A.b])
            sc.op("dve", lambda e: e.scalar_tensor_tensor(out=Bc[:], in0=Bc[:], scalar=10001.0, in1=A[:], op0=ALU.mult, op1=ALU.add),
                  reads=[A.b], writes=[Bc.b])
            sc.op("dve", lambda e: e.tensor_scalar_add(out=Bc[:], in0=Bc[:], scalar1=-1.0), writes=[Bc.b])
        wg = load_w(w_in, 0, 16, C_GN, 48)
        for s in range(4):
            pb = ps[2 + s % 2]
            mm_tm(pb, 48, actT, s, wg, 0, 16)
            sc.op("act", lambda e: e.activation(out=gates[:, s, :], in_=pb[:, 0:48], func=AF.Sigmoid), reads=[pb.b], writes=[gates.b])
        sc.dma("sp", rq[:, :], rqn_in[qt], writes=[rq.b])
        for hb in range(4):
            wq = next_wslot()
            wqv = wq[:, :, 0:480].rearrange("p k (j c) -> p k j c", c=120)
            Ws, rb = wsrc(w_in)
            for a in range(16):
                sc.dma("pool", wqv[:, a, :, 0:96],
                       Ws[a * 128:(a + 1) * 128, C_Q + hb * 384:C_Q + (hb + 1) * 384].rearrange("p (j c) -> p j c", c=96),
                       reads=rb, writes=[wq.b])
            sc.op("dve", lambda e: e.tensor_copy(out=wqv[:, :, :, 96:108], in_=wqv[:, :, :, 12:24]), writes=[wq.b])
            sc.op("dve", lambda e: e.tensor_copy(out=wqv[:, :, :, 108:120], in_=wqv[:, :, :, 0:12]), writes=[wq.b])
            for j in range(4):
                hd = hb * 4 + j
                pb = ps[2 + j % 2]
                mm_fm(pb, 120, wq, j * 120, actT, 16)
                rope_evac(pb, rq, Q[:, hd, :], Q.b, 0, 24, 96, 96)
        ck(4)
        for g in range(NSA_G):
            gheads = [4 * g + j for j in range(4)]
            for hl, hd in enumerate(gheads):
                accs = [ps[2], ps[3]]
                for ct in range(nct):
                    sb = ps[ct % 2]
                    p_t = pT[ct % 3]
                    sc.op("pe", lambda e: e.matmul(sb[:, :], lhsT=kccT[0:96, g, ct * 128:(ct + 1) * 128], rhs=Q[0:96, hd, :],
                                                   start=True, stop=True), reads=[kccT.b, Q.b], writes=[sb.b])
                    sc.op("act", lambda e: e.activation(out=p_t[:, :], in_=sb[:, :], func=AF.Exp, scale=NSA_SCALE),
                          reads=[sb.b], writes=[p_t.b])
                    sc.op("dve", lambda e: e.tensor_tensor(out=p_t[:, :], in0=p_t[:, :], in1=cmk[ct][:, :], op=ALU.mult),
                          reads=[h.b], writes=[p_t.b])
                    for s in range(4):
                        acc = accs[s // 2]
                        sc.op("pe", lambda e: e.matmul(acc[:, (s % 2) * 196:(s % 2) * 196 + 193], lhsT=p_t[:, s * 128:(s + 1) * 128],
                                                       rhs=vcaug[:, ct, g, 0:193], start=(ct == 0 and s % 2 == 0), stop=False,
                                                       skip_group_check=True),
                              reads=[p_t.b, vcaug.b], writes=[acc.b])
                nsa_finalize(accs, 196, hd, hl, 0, True)
                for s in range(4):
                    acc = accs[s // 2]
                    src = acc[:, (s % 2) * 196 + 65:(s % 2) * 196 + 193]
                    if hl == 0:
                        sc.op("dve", lambda e: e.tensor_scalar(out=impacc[:, s, :], in0=src, scalar1=small[:, 40 + s:41 + s], scalar2=None,
                                                               op0=ALU.mult), reads=[acc.b, small.b], writes=[impacc.b])
                    else:
                        sc.op("dve", lambda e: e.scalar_tensor_tensor(out=impacc[:, s, :], in0=src, scalar=small[:, 40 + s:41 + s],
                                                                      in1=impacc[:, s, :], op0=ALU.mult, op1=ALU.add),
                              reads=[acc.b, small.b], writes=[impacc.b])
            for s in range(4):
                sc.op("dve", lambda e: e.tensor_tensor(out=scoreA[:, :], in0=impacc[:, s, :], in1=tkA[s][:], op=ALU.mult),
                      reads=[impacc.b, tkA[s].b], writes=[scoreA.b])
                sc.op("dve", lambda e: e.tensor_tensor(out=scoreA[:, :], in0=scoreA[:, :], in1=tkB[s][:], op=ALU.add),
                      reads=[tkB[s].b], writes=[scoreA.b])
                sc.op("dve", lambda e: e.max(out=small[:, 0:8], in_=scoreA[:, :]), reads=[scoreA.b], writes=[small.b])
                sc.op("dve", lambda e: e.match_replace(out=scoreB[:, :], in_to_replace=small[:, 0:8], in_values=scoreA[:, :],
                                                       imm_value=-2.0), reads=[scoreA.b, small.b], writes=[scoreB.b])
                sc.op("dve", lambda e: e.max(out=small[:, 8:16], in_=scoreB[:, :]), reads=[scoreB.b], writes=[small.b])
                sc.op("dve", lambda e: e.tensor_reduce(out=small[:, 16:17], in_=small[:, 8:16], axis=AX.X, op=ALU.min),
                      writes=[small.b])
                sc.op("dve", lambda e: e.tensor_scalar(out=sel01[:, :], in0=scoreA[:, :], scalar1=small[:, 16:17], scalar2=None,
                                                       op0=ALU.is_ge), reads=[scoreA.b, small.b], writes=[sel01.b])
                transposes(sel01, 0, lambda j0, n: selT[:, g, s * 128:(s + 1) * 128].rearrange("p (j c) -> p j c", j=1),
                           [sel01.b], selT.b, 1)
            for pr in range(2):
                pheads = gheads[2 * pr:2 * pr + 2]

                def pre_kt(kt):
                    mp = ps[5]
                    sm = smask[kt % 2]
                    sc.op("pe", lambda e: e.matmul(mp[:, :], lhsT=GS[:, kt * 128:(kt + 1) * 128], rhs=selT[:, g, :],
                                                   start=True, stop=True), reads=[GS.b, selT.b], writes=[mp.b])
                    if kt >= zone0:
                        sc.op("dve", lambda e: e.tensor_tensor(out=sm[:, :], in0=mp[:, :], in1=cm[kt - zone0][:, :], op=ALU.mult),
                              reads=[mp.b, h.b], writes=[sm.b])
                    else:
                        sc.op("dve", lambda e: e.tensor_copy(out=sm[:, :], in_=mp[:, :]), reads=[mp.b], writes=[sm.b])

                attention(pheads, NSA_SCALE, KN[4 + g], VSW[:, :, g, :], list(range(nkt)),
                          lambda kt: smask[kt % 2], lambda hi: ps[2 + hi], pre_kt)
                for hi, hd in enumerate(pheads):
                    nsa_finalize(ps[2 + hi], 65, hd, 2 * pr + hi, 1, False)
                wt = [kt for kt in range(zone0 - 4, zone0 + 8) if kt >= 0]
                attention(pheads, NSA_SCALE, KN[8 + g], VSW[:, :, 4 + g, :], wt,
                          lambda kt: wm[kt - zone0 + 4], lambda hi: ps[2 + hi])
                for hi, hd in enumerate(pheads):
                    nsa_finalize(ps[2 + hi], 65, hd, 2 * pr + hi, 2, False)
            for hl in range(4):
                sc.op("act", lambda e: e.copy(out=oa3[:, :, (4 * g + hl) * 64:(4 * g + hl + 1) * 64],
                                              in_=tmpo[:, hl, :].rearrange("p (s d) -> p s d", d=64)),
                      reads=[tmpo.b], writes=[oa.b])
        ck(6)
        wcq = load_w(w_in, 0, 16, C_CQ, 512)
        sc.dma("sp", gq_b[:, :], g_q_in.partition_broadcast(128), writes=[gbs.b])
        for s in range(4):
            pb = ps[s % 2]
            mm_tm(pb, 512, actT, s, wcq, 0, 16)
            rstd_of(pb[:, :], [pb.b], 512, 8 + s, cqn[:, :], cqn.b)
            sc.op("dve", lambda e: e.scalar_tensor_tensor(out=cqn[:, :], in0=pb[:, :], scalar=stat[:, 8 + s:9 + s], in1=gq_b[:, :],
                                                          op0=ALU.mult, op1=ALU.mult), reads=[pb.b, stat.b, gq_b.b], writes=[cqn.b])
            transposes(cqn, 0, lambda j0, n: cqnT[:, j0:j0 + n, s * 128:(s + 1) * 128], [cqn.b], cqnT.b, 4)
        wu = next_wslot()
        wuv_ = wu[:, :, :].rearrange("p k c -> p (k c)").rearrange("p (k j c) -> p k j c", k=4, c=128)
        Ws, rb = wsrc(w_uq)
        for kc in range(4):
            sc.dma("pool", wuv_[:, kc, :, 0:96], Ws[kc * 128:(kc + 1) * 128, :].rearrange("p (j c) -> p j c", c=96),
                   reads=rb, writes=[wu.b])
        sc.op("dve", lambda e: e.tensor_copy(out=wuv_[:, :, :, 96:112], in_=wuv_[:, :, :, 80:96]), writes=[wu.b])
        sc.op("dve", lambda e: e.tensor_copy(out=wuv_[:, :, :, 112:128], in_=wuv_[:, :, :, 64:80]), writes=[wu.b])
        sc.dma("sp", rq[:, :], rqm_in[qt], writes=[rq.b])
        for hd in range(MLA_H):
            pb = ps[2 + hd % 2]
            for kc in range(4):
                sc.op("pe", lambda e: e.matmul(pb[:, :], lhsT=wuv_[:, kc, hd, :], rhs=cqnT[:, kc, :], start=(kc == 0), stop=(kc == 3)),
                      reads=[wu.b, cqnT.b], writes=[pb.b])
            rope_evac(pb, rq, Q[:, hd, :], Q.b, 64, 32, 96, 96)
        for hd in range(MLA_H):
            acc = ps[2 + hd % 2]
            attention([hd], MLA_SCALE, KM[hd], VM[:, :, hd, :], list(range(nkt)),
                      lambda kt: (cm[kt - zone0] if kt >= zone0 else None), lambda hi: acc)
            accv = acc[:, 0:260].rearrange("p (s c) -> p s c", c=65)
            sc.op("dve", lambda e: e.tensor_scalar_max(out=small[:, 48:52], in0=accv[:, :, 64], scalar1=1e-30), reads=[acc.b], writes=[small.b])
            sc.op("dve", lambda e: e.reciprocal(out=small[:, 48:52], in_=small[:, 48:52]), writes=[small.b])
            for s in range(4):
                sc.op("dve", lambda e: e.tensor_scalar(out=ob3[:, s, hd * 64:(hd + 1) * 64], in0=accv[:, s, 0:64],
                                                       scalar1=small[:, 48 + s:49 + s], scalar2=None, op0=ALU.mult),
                      reads=[acc.b, small.b], writes=[ob.b])
        ck(7)
        for s in range(4):
            transposes(oa, s * 1024, lambda j0, n: oaT[:, j0:j0 + n, s * 128:(s + 1) * 128], [oa.b], oaT.b, 8)
            transposes(ob, s * 1024, lambda j0, n: obT[:, j0:j0 + n, s * 128:(s + 1) * 128], [ob.b], obT.b, 8)
        sc.dma("sp", h[:], x_own[qt * 512:(qt + 1) * 512, :].rearrange("(s p) d -> p s d", p=128), writes=[h.b])
        sa, sbb = rtmp
        sc.barrier()
        for cb in range(4):
            for which, (gcol, wo_src, oT) in enumerate(((C_GA, w_o_nsa, oaT), (C_GB, w_o_mla, obT))):
                wgt = load_w(w_in, 0, 16, gcol + cb * 512, 512)
                wo = load_w(wo_src, 0, 8, cb * 512, 512)
                for c in range(4):
                    ft = cb * 4 + c
                    pg, pp = ps[0], ps[1]
                    mm_fm(pg, 128, wgt, c * 128, actT, 16)
                    sc.op("act", lambda e: e.activation(out=sa[:, :], in_=pg[:, :], func=AF.Sigmoid), reads=[pg.b], writes=[sa.b])
                    mm_fm(pp, 128, wo, c * 128, oT, 8)
                    if which == 0:
                        sc.op("dve", lambda e: e.tensor_tensor(out=sbb_keep[ft % 4][:, :], in0=sa[:, :], in1=pp[:, :], op=ALU.mult),
                              reads=[sa.b, pp.b], writes=[sbb_keep[ft % 4].b])
                    else:
                        sc.op("dve", lambda e: e.tensor_tensor(out=sa[:, :], in0=sa[:, :], in1=pp[:, :], op=ALU.mult),
                              reads=[pp.b], writes=[sa.b])
                        sc.op("dve", lambda e: e.tensor_tensor(out=mixT[:, ft, :], in0=sa[:, :], in1=sbb_keep[ft % 4][:, :], op=ALU.add),
                              reads=[sa.b, sbb_keep[ft % 4].b], writes=[mixT.b])
        for cg in range(4):
            wo = load_w(w_out, 0, 16, cg * 512, 512)
            for s in range(4):
                pb = ps[s % 2]
                mm_tm(pb, 512, mixT, s, wo, 0, 16)
                sc.op("dve", lambda e: e.tensor_tensor(out=h[:, s, cg * 512:(cg + 1) * 512],
                                                       in0=h[:, s, cg * 512:(cg + 1) * 512], in1=pb[:, :], op=ALU.add),
                      reads=[pb.b], writes=[h.b])

    sbb_keep = [View(fsm[:, 512 * i:512 * (i + 1)], f"sbbk{i}") for i in range(4)]

    def xattn_mlp(qt):
        norm_T(h, 4, load_g("g_xattn"), actT)
        wq = load_w(xa_wq, 0, 16, 0, 512)
        for hh in range(XA_H):
            pb = ps[hh % 2]
            mm_fm(pb, 128, wq, hh * 128, actT, 16)
            sc.op("act", lambda e: e.copy(out=qx[:, hh, :], in_=pb[:, :]), reads=[pb.b], writes=[qx.b])
        for hh in range(XA_H):
            pvb = [ps[4], ps[5]]
            for mt in range(2):
                sb = ps[2 + mt]
                p_t = pTx[mt]
                sc.op("pe", lambda e: e.matmul(sb[:, :], lhsT=kx[:, hh, mt * 128:(mt + 1) * 128], rhs=qx[:, hh, :],
                                               start=True, stop=True), reads=[kx.b, qx.b], writes=[sb.b])
                sc.op("act", lambda e: e.activation(out=p_t[:, :], in_=sb[:, :], func=AF.Exp, scale=float(XA_D) ** -0.5),
                      reads=[sb.b], writes=[p_t.b])
                for s in range(4):
                    acc = pvb[s // 2]
                    sc.op("pe", lambda e: e.matmul(acc[:, (s % 2) * 132:(s % 2) * 132 + 129],
                                                   lhsT=p_t[:, s * 128:(s + 1) * 128], rhs=vx[:, mt, hh, 0:129],
                                                   start=(mt == 0 and s % 2 == 0), stop=(mt == 1), skip_group_check=True),
                          reads=[p_t.b, vx.b], writes=[acc.b])
            for half in range(2):
                acc = pvb[half]
                accv = acc[:, 0:264].rearrange("p (s c) -> p s c", c=132)
                sc.op("dve", lambda e: e.reciprocal(out=rden[:, half * 2:half * 2 + 2], in_=accv[:, :, 128]),
                      reads=[acc.b], writes=[rden.b])
                for s2 in range(2):
                    s = half * 2 + s2
                    sc.op("dve", lambda e: e.tensor_scalar(
                        out=ox[:, s * 512 + hh * 128: s * 512 + (hh + 1) * 128], in0=accv[:, s2, 0:128],
                        scalar1=rden[:, half * 2 + s2:half * 2 + s2 + 1], scalar2=None, op0=ALU.mult),
                        reads=[acc.b, rden.b], writes=[ox.b])
        for s in range(4):
            transposes(ox, s * 512, lambda j0, n: oxT[:, j0:j0 + n, s * 128:(s + 1) * 128], [ox.b], oxT.b, 4)
        for cg in range(4):
            wo = load_w(xa_wo, 0, 4, cg * 512, 512)
            for s in range(4):
                pb = ps[s % 2]
                mm_tm(pb, 512, oxT, s, wo, 0, 4)
                sc.op("dve", lambda e: e.tensor_tensor(out=h[:, s, cg * 512:(cg + 1) * 512],
                                                       in0=h[:, s, cg * 512:(cg + 1) * 512], in1=pb[:, :], op=ALU.add),
                      reads=[pb.b], writes=[h.b])
        sc.barrier()
        norm_T(h, 4, load_g("g_mlp"), actT)
        for blk in range(16):
            w1 = load_w(w_ff1, 0, 16, blk * 512, 512)
            for c in range(4):
                pb = ps[c % 2]
                rt = rtmp[c % 2]
                mm_fm(pb, 128, w1, c * 128, actT, 16)
                sc.op("act", lambda e: e.activation(out=rt[:], in_=pb[:, :], func=AF.Relu), reads=[pb.b], writes=[rt.b])
                sc.op("dve", lambda e: e.tensor_tensor(out=hid[:, blk * 4 + c, :], in0=rt[:], in1=rt[:], op=ALU.mult),
                      reads=[rt.b], writes=[hid.b])
        for cg in range(4):
            for kb in range(4):
                w2 = load_w(w_ff2, kb * 16, 16, cg * 512, 512)
                for s in range(4):
                    mm_tm(ps[2 + s], 512, hid, s, w2, 0, 16, kc0=kb * 16, first=(kb == 0), last=(kb == 3))
            for s in range(4):
                pb = ps[2 + s]
                sc.op("dve", lambda e: e.tensor_tensor(out=h[:, s, cg * 512:(cg + 1) * 512],
                                                       in0=h[:, s, cg * 512:(cg + 1) * 512], in1=pb[:, :], op=ALU.add),
                      reads=[pb.b], writes=[h.b])
        gfin = load_g("g_final")
        for s in range(4):
            rstd_of(h[:, s, :], [h.b], D, 8 + s, xnb[0][:], xnb[0].b)
            sc.op("dve", lambda e: e.scalar_tensor_tensor(out=h[:, s, :], in0=h[:, s, :], scalar=stat[:, 8 + s:9 + s],
                                                          in1=gfin[:], op0=ALU.mult, op1=ALU.mult),
                  reads=[stat.b, gfin.b], writes=[h.b])
        sc.dma("sp", out[qt * 512:(qt + 1) * 512, :].rearrange("(s p) d -> p s d", p=128), h[:], reads=[h.b])

    try:
        _body()
    except _Stop:
        sc.dma("sp", out[0:512, :].rearrange("(s p) d -> p s d", p=128), h[:], reads=[h.b])
    sc.finish()
    return nc


_CACHE = {}


def _rope_np(npos, dim):
    inv = (1.0 / (np.float32(ROPE_THETA) ** (np.arange(0, dim, 2, dtype=np.float32) / np.float32(dim)))).astype(np.float32)
    ang = np.arange(npos, dtype=np.float32)[:, None] * inv[None, :]
    return np.cos(ang).astype(np.float32), np.sin(ang).astype(np.float32)


def _consts(S):
    NKT = S // 128
    NC = S // 16 - 1
    NCT = (NC + 127) // 128
    ca, sa = _rope_np(S, NSA_ROT)
    cb, sb = _rope_np(S, MLA_ROPE)
    rkn = np.zeros((128, S), np.float32)
    rkn[0:12] = ca.T
    rkn[12:24] = ca.T
    rkn[96:108] = -sa.T
    rkn[108:120] = sa.T
    rqm = np.zeros((128, S), np.float32)
    rqm[64:80] = cb.T
    rqm[80:96] = cb.T
    rqm[96:112] = -sb.T
    rqm[112:128] = sb.T
    tkr = np.concatenate([cb, cb, -sb, sb], axis=1).astype(np.float32)
    kpos = (np.arange(NKT)[None, :] * 128 + np.arange(128)[:, None]).astype(np.float32)
    cend = (16 * (np.arange(NCT)[None, :] * 128 + np.arange(128)[:, None]) + 31).astype(np.float32)
    jidx = np.tile(np.arange(128, dtype=np.float32)[None, :], (128, 1))
    cs = np.arange(NCT * 128) * 16
    ce = cs + 32
    ss = np.arange(128) * 64
    se = ss + 64
    ov = np.clip(np.minimum(ce[:, None], se[None, :]) - np.maximum(cs[:, None], ss[None, :]), 0, None).astype(np.float32) / 32.0
    ov[NC:] = 0.0
    msel = np.ascontiguousarray(ov.reshape(NCT, 128, 128).transpose(1, 0, 2))
    gsel = (np.arange(128)[:, None] == (np.arange(S)[None, :] // 64)).astype(np.float32)
    return dict(rkn=rkn, rqm_full=rqm, tkr=tkr, kpos=kpos, cend=cend, jidx=jidx, msel=msel, gsel=gsel,
                ident=np.eye(128, dtype=np.float32))


def kernel(**inputs):
    x = np.asarray(inputs["x"], dtype=np.float32)
    B, S, _ = x.shape
    NQT = S // 1024
    ncores = 2 * B
    if S not in _CACHE:
        _CACHE[S] = (build_program(S), _consts(S))
    nc, cst = _CACHE[S]

    def sq(name):
        a = np.asarray(inputs[name], dtype=np.float32)
        return np.ascontiguousarray(a[0])

    shared = {
        "g_mix": sq("g_mix").reshape(1, D), "g_xattn": sq("g_xattn").reshape(1, D), "g_mem": sq("g_mem").reshape(1, D),
        "g_mlp": sq("g_mlp").reshape(1, D), "g_final": np.asarray(inputs["g_final"], np.float32).reshape(1, D),
        "mla_g_q": sq("mla_g_q").reshape(1, -1), "mla_g_kv": sq("mla_g_kv").reshape(1, -1),
    }
    for k in ("w_in", "cmp_pos_k", "cmp_w1_k", "cmp_w2_k", "cmp_pos_v", "cmp_w1_v", "cmp_w2_v", "mla_w_uq", "mla_w_uk",
              "mla_w_uv", "w_o_nsa", "w_o_mla", "w_out", "xa_wq", "xa_wkv", "xa_wo", "w_ff1", "w_ff2"):
        shared[k] = sq(k)
    for k in ("ident", "rkn", "tkr", "kpos", "cend", "jidx", "msel", "gsel"):
        shared[k] = cst[k]
    memv = np.asarray(inputs["mem"], dtype=np.float32)
    in_maps = []
    for c in range(ncores):
        b, r = c // 2, c % 2
        tiles = [2 * k + r for k in range(NQT)]
        pos = np.concatenate([np.arange(t * 512, (t + 1) * 512) for t in tiles])
        m = dict(shared)
        m["x_full"] = np.ascontiguousarray(x[b])
        m["x_own"] = np.ascontiguousarray(x[b][pos])
        m["mem"] = np.ascontiguousarray(memv[b])
        m["rqn"] = np.ascontiguousarray(cst["rkn"][:, pos].reshape(128, NQT, 512).transpose(1, 0, 2))
        m["rqm"] = np.ascontiguousarray(cst["rqm_full"][:, pos].reshape(128, NQT, 512).transpose(1, 0, 2))
        m["qpos"] = pos.astype(np.float32).reshape(NQT, 512)
        m["cur"] = np.ascontiguousarray((pos // 64).astype(np.float32).reshape(NQT, 4, 128).transpose(0, 2, 1))
        in_maps.append(m)
    res = run_bass_kernel_spmd(nc, in_maps, core_ids=list(range(ncores)))
    outp = np.empty((B, S, D), dtype=np.float32)
    for c in range(ncores):
        b, r = c // 2, c % 2
        o = res.results[c]["out"]
        for k in range(NQT):
            t = 2 * k + r
            outp[b, t * 512:(t + 1) * 512] = o[k * 512:(k + 1) * 512]
    return outp
```

```python
import numpy as np
import concourse.bass as bass
import concourse.mybir as mybir
from concourse.bass_utils import run_bass_kernel_spmd

F32 = mybir.dt.float32
BF16 = mybir.dt.bfloat16
ALU = mybir.AluOpType
AF = mybir.ActivationFunctionType
AX = mybir.AxisListType

D = 2048
DFF = 8192
MEM = 256
EPS = 1e-6
XA_H, XA_D = 4, 128
SEM_CH = 12000
N_DSEM = 32


class Buf:
    __slots__ = ("w", "r", "name", "excl")

    def __init__(self, name="", excl=False):
        self.w = None
        self.r = {}
        self.name = name
        self.excl = excl


class Sched:
    def __init__(self, nc):
        self.nc = nc
        self.eng = {}
        for name, h in (("pe", nc.tensor), ("act", nc.scalar), ("dve", nc.vector),
                        ("pool", nc.gpsimd), ("sp", nc.sync)):
            self.eng[name] = dict(h=h, n=0, sems=[], seen={}, seen_d={})
        self.dsem = [nc.semaphore(f"dsem{i}").__enter__() for i in range(N_DSEM)]
        self.dcnt = [0] * N_DSEM
        self.dnext = {"sp": 0, "pool": 0}
        self.dbase = {"sp": 0, "pool": N_DSEM // 2}
        self.n_wait = 0

    def _sem(self, en, chunk):
        e = self.eng[en]
        while len(e["sems"]) <= chunk:
            e["sems"].append(self.nc.semaphore(f"prog_{en}_{len(e['sems'])}").__enter__())
        return e["sems"][chunk]

    def _wait(self, en, tok):
        e = self.eng[en]
        if tok[0] == "e":
            _, e2, seq = tok
            if e2 == en and en == "pe":
                return
            if e["seen"].get(e2, 0) >= seq:
                return
            chunk = (seq - 1) // SEM_CH
            e["h"].wait_ge(self._sem(e2, chunk), seq - chunk * SEM_CH)
            e["seen"][e2] = seq
            self.n_wait += 1
        else:
            _, s, val = tok
            if e["seen_d"].get(s, 0) >= val:
                return
            e["h"].wait_ge(self.dsem[s], val)
            e["seen_d"][s] = val
            self.n_wait += 1

    def _deps(self, en, reads, writes):
        for b in reads:
            if b.w is not None:
                self._wait(en, b.w)
        for b in writes:
            if b.w is not None:
                self._wait(en, b.w)
            for t in b.r.values():
                self._wait(en, t)

    def op(self, en, fn, reads=(), writes=()):
        e = self.eng[en]
        if any(b.excl for b in reads):
            writes = list(writes) + [b for b in reads if b.excl]
            reads = [b for b in reads if not b.excl]
        self._deps(en, reads, writes)
        ins = fn(e["h"])
        e["n"] += 1
        seq = e["n"]
        chunk = (seq - 1) // SEM_CH
        ins.then_inc(self._sem(en, chunk), 1)
        tok = ("e", en, seq)
        for b in reads:
            b.r[en] = tok
        for b in writes:
            b.w = tok
            b.r = {}
        return tok

    def dma(self, qn, out, in_, reads=(), writes=()):
        e = self.eng[qn]
        self._deps(qn, reads, writes)
        s = self.dbase[qn] + self.dnext[qn]
        self.dnext[qn] = (self.dnext[qn] + 1) % (N_DSEM // 2)
        if self.dcnt[s] > 0:
            self._wait(qn, ("d", s, self.dcnt[s]))
        ins = e["h"].dma_start(out=out, in_=in_)
        self.dcnt[s] += 16
        ins.then_inc(self.dsem[s], 16)
        tok = ("d", s, self.dcnt[s])
        for b in reads:
            b.r[("d", s)] = tok
        for b in writes:
            b.w = tok
            b.r = {}
        return tok

    def barrier(self):
        toks = [("e", en, e["n"]) for en, e in self.eng.items() if e["n"] > 0]
        dt = [("d", s, c) for s, c in enumerate(self.dcnt) if c > 0]
        for en in self.eng:
            for t in toks:
                if t[1] != en:
                    self._wait(en, t)
            for t in dt:
                self._wait(en, t)

    def finish(self):
        for s, c in enumerate(self.dcnt):
            if c > 0:
                self._wait("sp", ("d", s, c))


class T:
    def __init__(self, nc, name, shape, dtype, psum=False):
        if psum:
            self.t = nc.psum_tensor(name, shape, dtype).__enter__()
        else:
            self.t = nc.sbuf_tensor(name, shape, dtype).__enter__()
        self.b = Buf(name, excl=psum)

    def __getitem__(self, idx):
        return self.t[idx]


NSA_H, NSA_G, NSA_DK, NSA_DV, NSA_ROT = 16, 4, 96, 64, 24
MLA_H, MLA_NOPE, MLA_ROPE, MLA_DV, MLA_QR, MLA_KVR = 16, 64, 32, 64, 512, 256
ROPE_THETA = 500000.0
C_Q, C_KC, C_VC, C_KS, C_VS, C_KW, C_VW, C_GN, C_CQ, C_CKV, C_KR, C_GA, C_GB = (
    0, 1536, 1920, 2176, 2560, 2816, 3200, 3456, 3504, 4016, 4272, 4304, 6352)
D_IN = 8400
NSA_SCALE = float(NSA_DK) ** -0.5
MLA_SCALE = float(MLA_NOPE + MLA_ROPE) ** -0.5
WKC = 1984


def build_program(S, dbg=None):
    NQT = S // 1024
    NOWN = NQT * 512
    NT = S // 512
    NKT = S // 128
    NC = S // 16 - 1
    NCT = (NC + 127) // 128
    nc = bass.Bass("TRN2", target_bir_lowering=False)
    sc = Sched(nc)

    def din(name, shape, dt=F32):
        return nc.dram_tensor(name, list(shape), dt, kind="ExternalInput").ap()

    x_full = din("x_full", [S, D])
    x_own = din("x_own", [NOWN, D])
    mem = din("mem", [MEM, D])
    gvecs = {k: din(k, [1, D]) for k in ("g_mix", "g_xattn", "g_mem", "g_mlp", "g_final")}
    g_q_in = din("mla_g_q", [1, MLA_QR])
    g_kv_in = din("mla_g_kv", [1, MLA_KVR])
    w_in = din("w_in", [D, D_IN])
    cmp_pos_k = din("cmp_pos_k", [32, NSA_DK])
    cmp_w1_k = din("cmp_w1_k", [32 * NSA_DK, NSA_DK])
    cmp_w2_k = din("cmp_w2_k", [NSA_DK, NSA_DK])
    cmp_pos_v = din("cmp_pos_v", [32, NSA_DV])
    cmp_w1_v = din("cmp_w1_v", [32 * NSA_DV, NSA_DV])
    cmp_w2_v = din("cmp_w2_v", [NSA_DV, NSA_DV])
    w_uq = din("mla_w_uq", [MLA_QR, MLA_H * 96])
    w_uk = din("mla_w_uk", [MLA_KVR, MLA_H * 64])
    w_uv = din("mla_w_uv", [MLA_KVR, MLA_H * 64])
    w_o_nsa = din("w_o_nsa", [1024, D])
    w_o_mla = din("w_o_mla", [1024, D])
    w_out = din("w_out", [D, D])
    xa_wq = din("xa_wq", [D, 512])
    xa_wkv = din("xa_wkv", [D, 1024])
    xa_wo = din("xa_wo", [512, D])
    w_ff1 = din("w_ff1", [D, DFF])
    w_ff2 = din("w_ff2", [DFF, D])
    ident_in = din("ident", [128, 128])
    rkn_in = din("rkn", [128, S])
    tkr_in = din("tkr", [S, 64])
    rqn_in = din("rqn", [NQT, 128, 512])
    rqm_in = din("rqm", [NQT, 128, 512])
    qpos_in = din("qpos", [NQT, 512])
    cur_in = din("cur", [NQT, 128, 4])
    kpos_in = din("kpos", [128, NKT])
    cend_in = din("cend", [128, NCT])
    jidx_in = din("jidx", [128, 128])
    msel_in = din("msel", [128, NCT, 128])
    gsel_in = din("gsel", [128, S])
    out = nc.dram_tensor("out", [NOWN, D], F32, kind="ExternalOutput").ap()

    KN = nc.dram_tensor("KN", [12, 96, S], BF16).ap()
    KM = nc.dram_tensor("KM", [MLA_H, 96, S], BF16).ap()
    VSW = nc.dram_tensor("VSW", [NKT, 128, 8, 65], BF16).ap()
    VM = nc.dram_tensor("VM", [NKT, 128, MLA_H, 65], BF16).ap()
    VCT = nc.dram_tensor("VCT", [256, S], BF16).ap()
    WB = {}
    for nm, src in (("w_in", w_in), ("mla_w_uq", w_uq), ("w_o_nsa", w_o_nsa), ("w_o_mla", w_o_mla), ("w_out", w_out),
                    ("xa_wq", xa_wq), ("xa_wo", xa_wo), ("w_ff1", w_ff1), ("w_ff2", w_ff2), ("gsel", gsel_in)):
        WB[id(src)] = (nc.dram_tensor(nm + "_bf", list(src.shape), BF16).ap(), Buf(nm + "_bf"), src)
    use_bf = [False]

    def wsrc(W):
        if use_bf[0] and id(W) in WB:
            return WB[id(W)][0], [WB[id(W)][1]]
        return W, []

    def precast():
        for wb, buf, src in WB.values():
            R = src.shape[0]
            step = min(R, 256)
            for r0 in range(0, R, step):
                sc.dma("pool", wb[r0:r0 + step, :], src[r0:r0 + step, :], writes=[buf])
        use_bf[0] = True

    ident = T(nc, "identb", [128, 128], BF16)
    gbs = T(nc, "gbs", [128, D], F32)
    h = T(nc, "h", [128, 4, D], F32)
    xnb = [T(nc, "xnb0", [128, D], BF16)]
    stat = T(nc, "stat", [128, 16], F32)
    actT = T(nc, "actT", [128, 16, 512], BF16)
    wsl = [T(nc, f"wsl{i}", [128, 16, 512], BF16) for i in range(2)]
    rtmp = [T(nc, f"rtmp{i}", [128, 512], F32) for i in range(2)]
    kx = T(nc, "kx_sb", [128, XA_H, MEM], BF16)
    vx = T(nc, "vx", [128, 2, XA_H, 132], BF16)
    rden = T(nc, "rden", [128, 8], F32)
    kposc = T(nc, "kposc", [128, NKT], F32)
    cendc = T(nc, "cendc", [128, NCT], F32)
    kccT = T(nc, "kccT", [128, NSA_G, NCT * 128], BF16)
    vcaug = T(nc, "vcaug", [128, NCT, NSA_G, 196], BF16)
    ARENA = 41472
    arena = T(nc, "arena", [128, ARENA], BF16)
    fsm = T(nc, "fsm", [128, 3072], F32)
    ps = [T(nc, f"ps{i}", [128, 512], F32, psum=True) for i in range(6)]
    psTs = [T(nc, f"psT{i}", [128, 1024], BF16, psum=True) for i in range(2)]

    class View:
        def __init__(self, ap, name, buf=None):
            self.ap = ap
            self.b = buf if buf is not None else Buf(name)

        def __getitem__(self, idx):
            return self.ap[idx]

    def shaped(ap, shape):
        if len(shape) == 2:
            ap = ap.rearrange("p (a b) -> p a b", b=shape[1])
        elif len(shape) == 3:
            ap = ap.rearrange("p (a b c) -> p a b c", b=shape[1], c=shape[2])
        return ap

    def carve(name, off, shape, buf=None):
        n = int(np.prod(shape))
        assert off + n <= ARENA, (name, off, n)
        return View(shaped(arena[:, off:off + n], shape), name, buf)

    hbf = h[:, :, :].rearrange("p s d -> p (s d)").bitcast(BF16)

    def carve_h(name, off, shape):
        n = int(np.prod(shape))
        assert off + n <= 16384
        return View(shaped(hbf[:, off:off + n], shape), name, h.b)

    def fcarve(name, off, shape):
        n = int(np.prod(shape))
        assert off + n <= 3072
        return View(shaped(fsm[:, off:off + n], shape), name)

    hid = carve("hid", 0, [64, 512])
    qx = carve("qx", 0, [XA_H, 512])
    pTx = [carve(f"pTx{i}", 2048 + i * 512, [512]) for i in range(2)]
    ox = carve("ox", 3072, [4 * 512])
    oxT = carve("oxT", 5120, [4, 512])
    memT = carve("memT", 8192, [16, MEM])
    WK = carve("WK", 0, [16, WKC])
    Qb = Buf("Q")
    Q = carve("Q", 0, [16, 512], Qb)
    oaT = carve("oaT", 0, [8, 512], Qb)
    obT = carve("obT", 4096, [8, 512], Qb)
    Ob = Buf("O")
    oa = carve("oa", 8192, [4096], Ob)
    ob = carve("ob", 12288, [4096], Ob)
    oa3 = View(oa[:, :].rearrange("p (s c) -> p s c", c=1024), "oa3", Ob)
    ob3 = View(ob[:, :].rearrange("p (s c) -> p s c", c=1024), "ob3", Ob)
    mixT = carve("mixT", 8192, [16, 512], Ob)
    Ksl = [carve(f"Ksl{i}", 16384 + i * 1536, [1536]) for i in range(2)]
    Vsl = [carve(f"Vsl{i}", 19456 + i * 784, [12, 65]) for i in range(2)]
    pT = [carve(f"pT{i}", 29344 + i * 512, [512]) for i in range(6)]
    smask = [carve(f"smask{i}", 22560 + i * 512, [512]) for i in range(2)]
    selT = carve("selT", 23584, [NSA_G, 512])
    cqn = carve("cqn", 25632, [512])
    cqnT = carve("cqnT", 26144, [4, 512])
    sel01 = carve("sel01", 28192, [128])
    kst = [carve(f"kst{i}", 28320 + i * 512, [512]) for i in range(2)]
    GSb = Buf("GS_W3")
    GS = carve("GS", ARENA - S, [S], GSb)
    W3 = carve("W3", ARENA - 8192, [16, 512], GSb)
    wsl3 = [wsl[0], wsl[1], W3]
    assert 32416 <= ARENA - S
    cm = [carve_h(f"cm{z}", z * 512, [512]) for z in range(8)]
    wm = [carve_h(f"wm{z}", 4096 + z * 512, [512]) for z in range(12)]
    cmk = [carve_h(f"cmk{z}", 10240 + z * 512, [512]) for z in range(4)]
    qrow = fcarve("qrow", 0, [512])
    rq = fcarve("rq", 512, [512])
    gates = fcarve("gates", 1024, [4, 48])
    tmpo = fcarve("tmpo", 1216, [4, 256])
    impacc = fcarve("impacc", 2240, [4, 128])
    scoreA = fcarve("scoreA", 2752, [128])
    scoreB = fcarve("scoreB", 2880, [128])
    small = T(nc, "small_sb", [128, 64], F32)
    jidx = T(nc, "jidx_sb", [128, 128], F32)
    e0 = T(nc, "e0", [128, 128], F32)
    tkA = [View(h[:, 3, s * 256:s * 256 + 128], f"tkA{s}", h.b) for s in range(4)]
    tkB = [View(h[:, 3, s * 256 + 128:s * 256 + 256], f"tkB{s}", h.b) for s in range(4)]
    curc = T(nc, "curc", [128, 4], F32)
    gq_b = View(gbs[:, 0:MLA_QR], "gq_b", gbs.b)
    gkv_b = fcarve("gkv_b", 768, [MLA_KVR])

    def load_g(name):
        sc.dma("sp", gbs[:], gvecs[name].partition_broadcast(128), writes=[gbs.b])
        return gbs

    wslot_i = [0]

    def next_wslot():
        slot = wsl3[wslot_i[0] % 3]
        wslot_i[0] += 1
        return slot

    def load_w(W, k0, nk, c0, ncols):
        slot = next_wslot()
        Ws, rb = wsrc(W)
        src = Ws[k0 * 128:(k0 + nk) * 128, c0:c0 + ncols].rearrange("(k p) c -> p k c", p=128)
        for a in range(0, nk, 4):
            n = min(4, nk - a)
            sc.dma("pool", slot[:, a:a + n, 0:ncols], src[:, a:a + n, :], reads=rb, writes=[slot.b])
        return slot

    import os
    KSTOP = int(os.environ.get("KSTOP", "0"))

    class _Stop(Exception):
        pass

    def ck(n):
        if KSTOP == n:
            raise _Stop()

    tr_i = [0]

    def transposes(src_t, src_cols, dst_fn, reads, dst_buf, nchunk, rows=128, width=128):
        for j0 in range(0, nchunk, 4):
            n = min(4, nchunk - j0)
            psT = psTs[tr_i[0] % 2]
            tr_i[0] += 1
            for j in range(n):
                sc.op("pe", lambda e: e.transpose(
                    out=psT[0:width, j * 128: j * 128 + rows],
                    in_=src_t[0:rows, src_cols + (j0 + j) * width: src_cols + (j0 + j + 1) * width],
                    identity=ident[0:rows, 0:rows]), reads=list(reads) + [ident.b], writes=[psT.b])
            srcv = psT[0:width, 0:n * 128].rearrange("p (j c) -> p j c", c=128)[:, :, 0:rows]
            if tr_i[0] % 2:
                sc.op("act", lambda e: e.copy(out=dst_fn(j0, n), in_=srcv), writes=[psT.b, dst_buf])
            else:
                sc.op("dve", lambda e: e.tensor_copy(out=dst_fn(j0, n), in_=srcv), writes=[psT.b, dst_buf])

    def rstd_of(src_ap, src_bufs, width, col, junk_ap, junk_buf):
        sc.op("dve", lambda e: e.memset(stat[:, col:col + 1], 0.0), writes=[stat.b])
        sc.op("act", lambda e: e.activation(out=junk_ap, in_=src_ap, func=AF.Square, scale=float(width) ** -0.5,
                                            accum_out=stat[:, col:col + 1]),
              reads=src_bufs, writes=[junk_buf, stat.b])
        sc.op("dve", lambda e: e.tensor_scalar_add(out=stat[:, col:col + 1], in0=stat[:, col:col + 1], scalar1=EPS),
              writes=[stat.b])
        sc.op("act", lambda e: e.sqrt(out=stat[:, col:col + 1], in_=stat[:, col:col + 1]), writes=[stat.b])
        sc.op("dve", lambda e: e.reciprocal(out=stat[:, col:col + 1], in_=stat[:, col:col + 1]), writes=[stat.b])

    def norm_T(src, nsub, g, dst):
        for s in range(nsub):
            xb = xnb[0]
            rstd_of(src[:, s, :], [src.b], D, s, xb[:], xb.b)
            sc.op("dve", lambda e: e.scalar_tensor_tensor(out=xb[:], in0=src[:, s, :], scalar=stat[:, s:s + 1],
                                                          in1=g[:], op0=ALU.mult, op1=ALU.mult),
                  reads=[src.b, stat.b, g.b], writes=[xb.b])
            transposes(xb, 0, lambda j0, n: dst[:, j0:j0 + n, s * 128:(s + 1) * 128], [xb.b], dst.b, 16)

    def mm_fm(pst, M, wslot, c0, act, nk):
        for kc in range(nk):
            sc.op("pe", lambda e: e.matmul(pst[0:M, :], lhsT=wslot[:, kc, c0:c0 + M], rhs=act[:, kc, :],
                                           start=(kc == 0), stop=(kc == nk - 1)),
                  reads=[wslot.b, act.b], writes=[pst.b])

    def mm_tm(pst, ncols, act, s, wslot, c0, nk, kc0=0, first=True, last=True):
        for kc in range(nk):
            sc.op("pe", lambda e: e.matmul(pst[:, 0:ncols], lhsT=act[:, kc0 + kc, s * 128:(s + 1) * 128],
                                           rhs=wslot[:, kc, c0:c0 + ncols],
                                           start=(first and kc == 0), stop=(last and kc == nk - 1)),
                  reads=[wslot.b, act.b], writes=[pst.b])

    def rope_evac(pst, table, dst_ap, dst_buf, lo, n, swp, copy_rows):
        ta, tb = rtmp
        sc.op("act", lambda e: e.copy(out=dst_ap[0:copy_rows, :], in_=pst[0:copy_rows, :]), reads=[pst.b], writes=[dst_buf])
        sc.op("dve", lambda e: e.tensor_tensor(out=ta[lo:lo + n, :], in0=pst[swp:swp + n, :], in1=table[swp:swp + n, :],
                                               op=ALU.mult), reads=[pst.b, table.b], writes=[ta.b])
        sc.op("dve", lambda e: e.tensor_tensor(out=tb[lo:lo + n, :], in0=pst[lo:lo + n, :], in1=table[lo:lo + n, :],
                                               op=ALU.mult), reads=[pst.b, table.b], writes=[tb.b])
        sc.op("dve", lambda e: e.tensor_tensor(out=dst_ap[lo:lo + n, :], in0=ta[lo:lo + n, :], in1=tb[lo:lo + n, :],
                                               op=ALU.add), reads=[ta.b, tb.b], writes=[dst_buf])

    def _body():
        sc.dma("pool", ident[:], ident_in, writes=[ident.b])
        sc.dma("sp", kposc[:], kpos_in, writes=[kposc.b])
        sc.dma("sp", cendc[:], cend_in, writes=[cendc.b])
        sc.dma("sp", jidx[:], jidx_in, writes=[jidx.b])
        sc.op("dve", lambda e: e.tensor_single_scalar(out=e0[:], in_=jidx[:], scalar=0.0, op=ALU.is_equal),
              reads=[jidx.b], writes=[e0.b])

        sc.dma("sp", h[:, 0:2, :], mem.rearrange("(s p) d -> p s d", p=128), writes=[h.b])
        norm_T(h, 2, load_g("g_mem"), memT)
        wk = load_w(xa_wkv, 0, 16, 0, 512)
        for hh in range(XA_H):
            pb = ps[hh % 2]
            for kc in range(16):
                sc.op("pe", lambda e: e.matmul(pb[:, 0:MEM], lhsT=wk[:, kc, hh * 128:(hh + 1) * 128],
                                               rhs=memT[:, kc, :], start=(kc == 0), stop=(kc == 15)),
                      reads=[wk.b, memT.b], writes=[pb.b])
            sc.op("act", lambda e: e.copy(out=kx[:, hh, :], in_=pb[:, 0:MEM]), reads=[pb.b], writes=[kx.b])
        wv = load_w(xa_wkv, 0, 16, 512, 512)
        sc.op("dve", lambda e: e.memset(vx[:], 1.0), writes=[vx.b])
        for mt in range(2):
            pb = ps[2 + mt]
            for kc in range(16):
                sc.op("pe", lambda e: e.matmul(pb[:, :], lhsT=memT[:, kc, mt * 128:(mt + 1) * 128],
                                               rhs=wv[:, kc, :], start=(kc == 0), stop=(kc == 15)),
                      reads=[wv.b, memT.b], writes=[pb.b])
            sc.op("dve", lambda e: e.tensor_copy(out=vx[:, mt, :, 0:128],
                                                 in_=pb[:, :].rearrange("p (h d) -> p h d", d=128)),
                  reads=[pb.b], writes=[vx.b])
        sc.barrier()
        ck(1)

        fam = [C_KC, C_KS, C_KW]
        for j in range(12):
            base = fam[j // 4] + (j % 4) * 96
            for a in (0, 8):
                sc.dma("pool", WK[:, a:a + 8, j * 120:j * 120 + 96],
                       w_in[a * 128:(a + 8) * 128, base:base + 96].rearrange("(k p) c -> p k c", p=128), writes=[WK.b])
        for (dst0, src0, n) in ((1440, C_VC, 256), (1696, C_CKV, 288)):
            for a in range(0, 16, 4):
                sc.dma("pool", WK[:, a:a + 4, dst0:dst0 + n],
                       w_in[a * 128:(a + 4) * 128, src0:src0 + n].rearrange("(k p) c -> p k c", p=128), writes=[WK.b])
        WKs = WK[:, :, 0:1440].rearrange("p k (j c) -> p k j c", c=120)
        sc.op("dve", lambda e: e.tensor_copy(out=WKs[:, :, :, 96:108], in_=WKs[:, :, :, 12:24]), writes=[WK.b])
        sc.op("dve", lambda e: e.tensor_copy(out=WKs[:, :, :, 108:120], in_=WKs[:, :, :, 0:12]), writes=[WK.b])
        wA = wsl[0]
        wAf = wA[:, :, :].rearrange("p k c -> p (k c)")
        WUK = wAf[:, 0:2048].rearrange("p (k c) -> p k c", c=1024)
        WUV = wAf[:, 2048:4096].rearrange("p (k c) -> p k c", c=1024)
        sc.dma("pool", WUK, w_uk.rearrange("(k p) c -> p k c", p=128), writes=[wA.b])
        sc.dma("pool", WUV, w_uv.rearrange("(k p) c -> p k c", p=128), writes=[wA.b])
        WV = wsl[1]
        for (dst0, src0) in ((0, C_VS), (256, C_VW)):
            for a in range(0, 16, 4):
                sc.dma("pool", WV[:, a:a + 4, dst0:dst0 + 256],
                       w_in[a * 128:(a + 4) * 128, src0:src0 + 256].rearrange("(k p) c -> p k c", p=128), writes=[WV.b])
        precast()
        a0 = 16 * WKC
        vst = carve("vst", a0, [4, 8, 65])
        vmst = carve("vmst", a0 + 2080, [4, 16, 65])
        ckvn = carve("ckvn", a0 + 6240, [256])
        krb = carve("krb", a0 + 6496, [32])
        ckvnT = carve("ckvnT", a0 + 6528, [2, 512])
        krT = carve("krT", a0 + 7552, [512])
        kstA = [carve(f"kstA{i}", a0 + 8064 + i * 512, [512]) for i in range(3)]
        rk = fcarve("rk", 0, [512])
        tkr = fcarve("tkr", 512, [4, 64])
        sc.dma("sp", gkv_b[:, :], g_kv_in.partition_broadcast(128), writes=[gkv_b.b])
        sc.op("dve", lambda e: e.memset(vst[:], 1.0), writes=[vst.b])
        sc.op("dve", lambda e: e.memset(vmst[:], 1.0), writes=[vmst.b])
        gmix = load_g("g_mix")
        kst_i = 0
        for t in range(NT):
            sc.dma("sp", h[:], x_full[t * 512:(t + 1) * 512, :].rearrange("(s p) d -> p s d", p=128), writes=[h.b])
            sc.dma("sp", rk[:, :], rkn_in[:, t * 512:(t + 1) * 512], writes=[rk.b])
            sc.dma("sp", tkr[:, :, :], tkr_in[t * 512:(t + 1) * 512, :].rearrange("(s p) c -> p s c", p=128), writes=[tkr.b])
            norm_T(h, 4, gmix, actT)
            for j in range(12):
                pb = ps[j % 2]
                mm_fm(pb, 120, WK, j * 120, actT, 16)
                ks_ = kstA[kst_i % 3]
                kst_i += 1
                rope_evac(pb, rk, ks_, ks_.b, 0, 24, 96, 96)
                sc.dma("sp", KN[j, :, t * 512:(t + 1) * 512], ks_[0:96, :], reads=[ks_.b])
            for j in range(2):
                pb = ps[j % 2]
                mm_fm(pb, 128, WK, 1440 + j * 128, actT, 16)
                ks_ = kstA[kst_i % 3]
                kst_i += 1
                sc.op("act", lambda e: e.copy(out=ks_[:, :], in_=pb[:, :]), reads=[pb.b], writes=[ks_.b])
                sc.dma("sp", VCT[j * 128:(j + 1) * 128, t * 512:(t + 1) * 512], ks_[:, :], reads=[ks_.b])
            for s in range(4):
                pa, pv = ps[2], ps[3]
                mm_tm(pa, 288, actT, s, WK, 1696, 16)
                mm_tm(pv, 512, actT, s, WV, 0, 16)
                sc.op("act", lambda e: e.copy(out=vst[:, s, :, 0:64],
                                              in_=pv[:, :].rearrange("p (j c) -> p j c", c=64)),
                      reads=[pv.b], writes=[vst.b])
                rstd_of(pa[:, 0:256], [pa.b], 256, 4 + s, ckvn[:, :], ckvn.b)
                sc.op("dve", lambda e: e.scalar_tensor_tensor(out=ckvn[:, :], in0=pa[:, 0:256], scalar=stat[:, 4 + s:5 + s],
                                                              in1=gkv_b[:, :], op0=ALU.mult, op1=ALU.mult),
                      reads=[pa.b, stat.b, gkv_b.b], writes=[ckvn.b])
                transposes(ckvn, 0, lambda j0, n: ckvnT[:, j0:j0 + n, s * 128:(s + 1) * 128], [ckvn.b], ckvnT.b, 2)
                ta, tb = rtmp
                sc.op("dve", lambda e: e.tensor_tensor(out=ta[:, 0:32], in0=pa[:, 256:288], in1=tkr[:, s, 0:32], op=ALU.mult),
                      reads=[pa.b, tkr.b], writes=[ta.b])
                sc.op("dve", lambda e: e.tensor_tensor(out=tb[:, 0:16], in0=pa[:, 272:288], in1=tkr[:, s, 32:48], op=ALU.mult),
                      reads=[pa.b, tkr.b], writes=[tb.b])
                sc.op("dve", lambda e: e.tensor_tensor(out=tb[:, 16:32], in0=pa[:, 256:272], in1=tkr[:, s, 48:64], op=ALU.mult),
                      reads=[pa.b, tkr.b], writes=[tb.b])
                sc.op("dve", lambda e: e.tensor_tensor(out=krb[:, :], in0=ta[:, 0:32], in1=tb[:, 0:32], op=ALU.add),
                      reads=[ta.b, tb.b], writes=[krb.b])
                transposes(krb, 0, lambda j0, n: krT[0:32, s * 128:(s + 1) * 128].rearrange("p (j c) -> p j c", j=1),
                           [krb.b], krT.b, 1, rows=128, width=32)
            sc.dma("sp", VSW[t * 4:(t + 1) * 4].rearrange("s p j c -> p s j c"), vst[:], reads=[vst.b])
            for hh in range(MLA_H):
                sc.dma("sp", KM[hh, 64:96, t * 512:(t + 1) * 512], krT[0:32, :], reads=[krT.b])
            for pr in range(8):
                pb = ps[pr % 2]
                for kc in range(2):
                    sc.op("pe", lambda e: e.matmul(pb[:, :], lhsT=WUK[:, kc, pr * 128:(pr + 1) * 128], rhs=ckvnT[:, kc, :],
                                                   start=(kc == 0), stop=(kc == 1)),
                          reads=[wA.b, ckvnT.b], writes=[pb.b])
                ks_ = kstA[kst_i % 3]
                kst_i += 1
                sc.op("act", lambda e: e.copy(out=ks_[:, :], in_=pb[:, :]), reads=[pb.b], writes=[ks_.b])
                sc.dma("sp", KM[2 * pr, 0:64, t * 512:(t + 1) * 512], ks_[0:64, :], reads=[ks_.b])
                sc.dma("sp", KM[2 * pr + 1, 0:64, t * 512:(t + 1) * 512], ks_[64:128, :], reads=[ks_.b])
            for s in range(4):
                for hf in range(2):
                    pb = ps[4 + hf]
                    for kc in range(2):
                        sc.op("pe", lambda e: e.matmul(pb[:, :], lhsT=ckvnT[:, kc, s * 128:(s + 1) * 128],
                                                       rhs=WUV[:, kc, hf * 512:(hf + 1) * 512],
                                                       start=(kc == 0), stop=(kc == 1)),
                              reads=[wA.b, ckvnT.b], writes=[pb.b])
                    sc.op("dve", lambda e: e.tensor_copy(out=vmst[:, s, hf * 8:(hf + 1) * 8, 0:64],
                                                         in_=pb[:, :].rearrange("p (j c) -> p j c", c=64)),
                          reads=[pb.b], writes=[vmst.b])
            sc.dma("sp", VM[t * 4:(t + 1) * 4].rearrange("s p j c -> p s j c"), vmst[:], reads=[vmst.b])
        sc.barrier()
        ck(2)

        kcs = carve("kcs", 0, [S])
        w1s = carve("w1s", S, [32, 96])
        w2s = carve("w2s", S + 3072, [96])
        poss = carve("poss", S + 3200, [96])
        posT = carve("posT", S + 3328, [32])
        gl = carve("gl", S + 3392, [512])
        ub = fcarve("ub", 0, [512])
        tb_ = fcarve("tb_", 512, [512])
        b1 = small
        sc.op("dve", lambda e: e.memset(kccT[:], 0.0), writes=[kccT.b])
        sc.op("dve", lambda e: e.memset(vcaug[:], 0.0), writes=[vcaug.b])
        sc.op("dve", lambda e: e.memset(vcaug[:, :, :, 64:65], 1.0), writes=[vcaug.b])
        for g in range(NSA_G):
            sc.dma("pool", vcaug[:, :, g, 65:193], msel_in, writes=[vcaug.b])
        for which in range(2):
            dd = NSA_DK if which == 0 else NSA_DV
            posi, w1i, w2i = (cmp_pos_k, cmp_w1_k, cmp_w2_k) if which == 0 else (cmp_pos_v, cmp_w1_v, cmp_w2_v)
            sc.dma("pool", w1s[0:dd, :, 0:dd], w1i.rearrange("(l d) e -> d l e", d=dd), writes=[w1s.b])
            sc.dma("pool", w2s[0:dd, 0:dd], w2i, writes=[w2s.b])
            sc.dma("pool", poss[0:32, 0:dd], posi, writes=[poss.b])
            transposes(poss, 0, lambda j0, n: posT[0:dd, 0:32].rearrange("p (j c) -> p j c", j=1), [poss.b], posT.b, 1,
                       rows=32, width=dd)
            pb = ps[0]
            for l in range(32):
                sc.op("pe", lambda e: e.matmul(pb[0:dd, 0:1], lhsT=w1s[0:dd, l, 0:dd], rhs=posT[0:dd, l:l + 1],
                                               start=(l == 0), stop=(l == 31)),
                      reads=[w1s.b, posT.b], writes=[pb.b])
            sc.op("dve", lambda e: e.tensor_copy(out=b1[0:dd, which:which + 1], in_=pb[0:dd, 0:1]), reads=[pb.b], writes=[b1.b])
            for g in range(NSA_G):
                if which == 0:
                    sc.dma("sp", kcs[0:96, :], KN[g, :, :], writes=[kcs.b])
                else:
                    sc.dma("sp", kcs[0:64, :], VCT[g * 64:(g + 1) * 64, :], writes=[kcs.b])
                for c0 in range(0, NC, 512):
                    n = min(512, NC - c0)
                    pb = ps[1]
                    for l in range(32):
                        lo = 16 * c0 + l
                        sc.op("pe", lambda e: e.matmul(pb[0:dd, 0:n], lhsT=w1s[0:dd, l, 0:dd],
                                                       rhs=kcs[0:dd, lo:lo + 16 * (n - 1) + 1:16],
                                                       start=(l == 0), stop=(l == 31)),
                              reads=[w1s.b, kcs.b], writes=[pb.b])
                    sc.op("act", lambda e: e.activation(out=ub[0:dd, 0:n], in_=pb[0:dd, 0:n], func=AF.Identity,
                                                        bias=b1[0:dd, which:which + 1]),
                          reads=[pb.b, b1.b], writes=[ub.b])
                    sc.op("dve", lambda e: e.tensor_tensor(out=tb_[0:dd, 0:n], in0=ub[0:dd, 0:n], in1=ub[0:dd, 0:n], op=ALU.mult),
                          reads=[ub.b], writes=[tb_.b])
                    sc.op("dve", lambda e: e.tensor_scalar(out=tb_[0:dd, 0:n], in0=tb_[0:dd, 0:n], scalar1=0.044715, scalar2=1.0,
                                                           op0=ALU.mult, op1=ALU.add), writes=[tb_.b])
                    sc.op("dve", lambda e: e.tensor_tensor(out=tb_[0:dd, 0:n], in0=tb_[0:dd, 0:n], in1=ub[0:dd, 0:n], op=ALU.mult),
                          reads=[ub.b], writes=[tb_.b])
                    sc.op("act", lambda e: e.activation(out=tb_[0:dd, 0:n], in_=tb_[0:dd, 0:n], func=AF.Sigmoid,
                                                        scale=1.5957691216057308), writes=[tb_.b])
                    sc.op("dve", lambda e: e.tensor_tensor(out=gl[0:dd, 0:n], in0=tb_[0:dd, 0:n], in1=ub[0:dd, 0:n], op=ALU.mult),
                          reads=[ub.b, tb_.b], writes=[gl.b])
                    if which == 0:
                        pc = ps[2]
                        sc.op("pe", lambda e: e.matmul(pc[0:96, 0:n], lhsT=w2s[0:96, 0:96], rhs=gl[0:96, 0:n], start=True, stop=True),
                              reads=[w2s.b, gl.b], writes=[pc.b])
                        sc.op("act", lambda e: e.copy(out=kccT[0:96, g, c0:c0 + n], in_=pc[0:96, 0:n]), reads=[pc.b], writes=[kccT.b])
                    else:
                        for cc in range(0, n, 128):
                            m = min(128, n - cc)
                            pc = ps[2 + (cc // 128) % 2]
                            sc.op("pe", lambda e: e.matmul(pc[0:m, 0:64], lhsT=gl[0:64, cc:cc + m], rhs=w2s[0:64, 0:64],
                                                           start=True, stop=True), reads=[w2s.b, gl.b], writes=[pc.b])
                            sc.op("act", lambda e: e.copy(out=vcaug[0:m, (c0 + cc) // 128, g, 0:64], in_=pc[0:m, 0:64]),
                                  reads=[pc.b], writes=[vcaug.b])
        sc.barrier()
        ck(3)

        for qt in range(NQT):
            mixer(qt)
            sc.barrier()
            ck(5)
            xattn_mlp(qt)
            sc.barrier()

    att_i = [0]
    kv_i = [0]
    sbanks = [ps[0], ps[1], ps[4]] + [View(psTs[i][:, :].bitcast(F32), f"psTf{i}", psTs[i].b) for i in range(2)]

    def attention(heads, scale, Ksrc, Vsrc, ktiles, mask_fn, acc_of, pre_kt=None):
        LOOK = 4
        first = {hd: True for hd in heads}
        blocks = [ktiles[b0:b0 + 12] for b0 in range(0, len(ktiles), 12)]
        slots = {}

        def load_block(bi):
            blk = blocks[bi]
            n = len(blk)
            kt0 = blk[0]
            ksl = Ksl[kv_i[0] % 2]
            vsl = Vsl[kv_i[0] % 2]
            kv_i[0] += 1
            sc.dma("sp", ksl[0:96, 0:n * 128], Ksrc[:, kt0 * 128:(kt0 + n) * 128], writes=[ksl.b])
            sc.dma("sp", vsl[:, 0:n, :], Vsrc[kt0:kt0 + n].rearrange("t p c -> p t c"), writes=[vsl.b])
            slots[bi] = (ksl, vsl)

        pending = []
        load_block(0)
        for bi, blk in enumerate(blocks):
            ksl, vsl = slots[bi]
            staged = 0
            for i, kt in enumerate(blk):
                if pre_kt is not None:
                    pre_kt(kt)
                for hi, hd in enumerate(heads):
                    if staged == LOOK and bi + 1 < len(blocks):
                        load_block(bi + 1)
                    staged += 1
                    sb = sbanks[att_i[0] % 5]
                    p_t = pT[att_i[0] % 6]
                    att_i[0] += 1
                    sc.op("pe", lambda e: e.matmul(sb[:, :], lhsT=ksl[0:96, i * 128:(i + 1) * 128], rhs=Q[0:96, hd, :],
                                                   start=True, stop=True), reads=[ksl.b, Q.b], writes=[sb.b])
                    sc.op("act", lambda e: e.activation(out=p_t[:, :], in_=sb[:, :], func=AF.Exp, scale=scale),
                          reads=[sb.b], writes=[p_t.b])
                    mk = mask_fn(kt)
                    if mk is not None:
                        meng = "pool" if att_i[0] % 3 == 0 else "dve"
                        sc.op(meng, lambda e: e.tensor_tensor(out=p_t[:, :], in0=p_t[:, :], in1=mk[:, :], op=ALU.mult),
                              reads=[mk.b], writes=[p_t.b])

                    def stage_b(p_t=p_t, vsl=vsl, i=i, hi=hi, hd=hd):
                        acc = acc_of(hi)
                        for s in range(4):
                            sc.op("pe", lambda e: e.matmul(acc[:, s * 65:(s + 1) * 65], lhsT=p_t[:, s * 128:(s + 1) * 128],
                                                           rhs=vsl[:, i, :], start=(first[hd] and s == 0), stop=False,
                                                           skip_group_check=True),
                                  reads=[p_t.b, vsl.b], writes=[acc.b])
                        first[hd] = False

                    pending.append(stage_b)
                    if len(pending) > LOOK:
                        pending.pop(0)()
            if staged <= LOOK and bi + 1 < len(blocks):
                while pending:
                    pending.pop(0)()
                load_block(bi + 1)
        while pending:
            pending.pop(0)()

    def nsa_finalize(acc, ncol, hd, hl, br, first_branch):
        accv = acc[:, 0:4 * ncol].rearrange("p (s c) -> p s c", c=ncol) if ncol == 65 else None
        for s in range(4):
            a = accv[:, s, :] if accv is not None else acc[s // 2][:, (s % 2) * ncol:(s % 2) * ncol + ncol]
            ab = acc.b if accv is not None else acc[s // 2].b
            c0 = 32 + s
            sc.op("dve", lambda e: e.tensor_scalar_max(out=small[:, c0:c0 + 1], in0=a[:, 64:65], scalar1=1e-30),
                  reads=[ab], writes=[small.b])
            sc.op("dve", lambda e: e.reciprocal(out=small[:, c0:c0 + 1], in_=small[:, c0:c0 + 1]), writes=[small.b])
            if ncol != 65:
                sc.op("dve", lambda e: e.tensor_copy(out=small[:, 40 + s:41 + s], in_=small[:, c0:c0 + 1]), writes=[small.b])
            sc.op("dve", lambda e: e.tensor_tensor(out=small[:, c0:c0 + 1], in0=small[:, c0:c0 + 1],
                                                   in1=gates[:, s, hd * 3 + br:hd * 3 + br + 1], op=ALU.mult),
                  reads=[gates.b], writes=[small.b])
            dst = tmpo[:, hl, s * 64:(s + 1) * 64]
            if first_branch:
                sc.op("dve", lambda e: e.tensor_scalar(out=dst, in0=a[:, 0:64], scalar1=small[:, c0:c0 + 1], scalar2=None,
                                                       op0=ALU.mult), reads=[ab, small.b], writes=[tmpo.b])
            else:
                sc.op("dve", lambda e: e.scalar_tensor_tensor(out=dst, in0=a[:, 0:64], scalar=small[:, c0:c0 + 1], in1=dst,
                                                              op0=ALU.mult, op1=ALU.add), reads=[ab, small.b], writes=[tmpo.b])

    def mixer(qt):
        k = qt
        nkt = 8 * k + 8
        zone0 = 8 * k
        nct = min(NCT, (2 * k + 1) // 4 + 1)
        sc.dma("sp", h[:], x_own[qt * 512:(qt + 1) * 512, :].rearrange("(s p) d -> p s d", p=128), writes=[h.b])
        norm_T(h, 4, load_g("g_mix"), actT)
        sc.dma("sp", qrow[:, :], qpos_in[qt:qt + 1, :].partition_broadcast(128), writes=[qrow.b])
        sc.dma("sp", curc[:], cur_in[qt], writes=[curc.b])
        for z in range(8):
            kt = zone0 + z
            sc.op("dve", lambda e: e.tensor_scalar(out=cm[z][:, :], in0=qrow[:, :], scalar1=kposc[:, kt:kt + 1], scalar2=None,
                                                   op0=ALU.is_ge), reads=[qrow.b, kposc.b], writes=[h.b])
        for z in range(12):
            kt = zone0 - 4 + z
            if kt < 0:
                continue
            sc.op("dve", lambda e: e.tensor_scalar(out=wm[z][:, :], in0=qrow[:, :], scalar1=kposc[:, kt:kt + 1], scalar2=511.0,
                                                   op0=ALU.subtract, op1=ALU.is_le), reads=[qrow.b, kposc.b], writes=[h.b])
            if z >= 4:
                sc.op("dve", lambda e: e.tensor_tensor(out=wm[z][:, :], in0=wm[z][:, :], in1=cm[z - 4][:, :], op=ALU.mult),
                      writes=[h.b])
        for ct in range(nct):
            sc.op("dve", lambda e: e.tensor_scalar(out=cmk[ct][:, :], in0=qrow[:, :], scalar1=cendc[:, ct:ct + 1], scalar2=None,
                                                   op0=ALU.is_ge), reads=[qrow.b, cendc.b], writes=[h.b])
        for s in range(4):
            A, Bc = tkA[s], tkB[s]
            sc.op("dve", lambda e: e.tensor_scalar(out=Bc[:], in0=jidx[:], scalar1=curc[:, s:s + 1], scalar2=None, op0=ALU.subtract),
                  reads=[jidx.b, curc.b], writes=[Bc.b])
            sc.op("dve", lambda e: e.tensor_single_scalar(out=A[:], in_=Bc[:], scalar=0.0, op=ALU.is_le), reads=[Bc.b], writes=[A.b])
            sc.op("dve", lambda e: e.tensor_single_scalar(out=Bc[:], in_=Bc[:], scalar=-1.0, op=ALU.is_ge), writes=[Bc.b])
            sc.op("dve", lambda e: e.tensor_tensor(out=Bc[:], in0=Bc[:], in1=A[:], op=ALU.mult), reads=[A.b], writes=[Bc.b])
            sc.op("dve", lambda e: e.tensor_tensor(out=Bc[:], in0=Bc[:], in1=e0[:], op=ALU.max), reads=[e0.b], writes=[Bc.b])
            sc.op("dve", lambda e: e.tensor_tensor(out=A[:], in0=A[:], in1=Bc[:], op=ALU.subtract), reads=[Bc.b], writes=[A.b])
            sc.op("dve", lambda e: e.scalar_tensor_tensor(out=Bc[:], in0=Bc[:], scalar=10001.0, in1=A[:], op0=ALU.mult, op1=ALU.add),
                  reads=[A.b], writes=[Bc.b])
            sc.op("dve", lambda e: e.tensor_scalar_add(out=Bc[:], in0=Bc[:], scalar1=-1.0), writes=[Bc.b])
        wg = load_w(w_in, 0, 16, C_GN, 48)
        for s in range(4):
            pb = ps[2 + s % 2]
            mm_tm(pb, 48, actT, s, wg, 0, 16)
            sc.op("act", lambda e: e.activation(out=gates[:, s, :], in_=pb[:, 0:48], func=AF.Sigmoid), reads=[pb.b], writes=[gates.b])
        sc.dma("sp", rq[:, :], rqn_in[qt], writes=[rq.b])
        for hb in range(4):
            wq = next_wslot()
            wqv = wq[:, :, 0:480].rearrange("p k (j c) -> p k j c", c=120)
            Ws, rb = wsrc(w_in)
            for a in range(16):
                sc.dma("pool", wqv[:, a, :, 0:96],
                       Ws[a * 128:(a + 1) * 128, C_Q + hb * 384:C_Q + (hb + 1) * 384].rearrange("p (j c) -> p j c", c=96),
                       reads=rb, writes=[wq.b])
            sc.op("dve", lambda e: e.tensor_copy(out=wqv[:, :, :, 96:108], in_=wqv[:, :, :, 12:24]), writes=[wq.b])
            sc.op("dve", lambda e: e.tensor_copy(out=wqv[:, :, :, 108:120], in_=wqv[:, :, :, 0:12]), writes=[wq.b])
            for j in range(4):
                hd = hb * 4 + j
                pb = ps[2 + j % 2]
                mm_fm(pb, 120, wq, j * 120, actT, 16)
                rope_evac(pb, rq, Q[:, hd, :], Q.b, 0, 24, 96, 96)
        ck(4)
        Gs_, rb_ = wsrc(gsel_in)
        sc.dma("pool", GS[:, :], Gs_, reads=rb_, writes=[GS.b])
        for g in range(NSA_G):
            gheads = [4 * g + j for j in range(4)]
            for hl, hd in enumerate(gheads):
                accs = [ps[2], ps[3]]
                for ct in range(nct):
                    sb = ps[ct % 2]
                    p_t = pT[ct % 3]
                    sc.op("pe", lambda e: e.matmul(sb[:, :], lhsT=kccT[0:96, g, ct * 128:(ct + 1) * 128], rhs=Q[0:96, hd, :],
                                                   start=True, stop=True), reads=[kccT.b, Q.b], writes=[sb.b])
                    sc.op("act", lambda e: e.activation(out=p_t[:, :], in_=sb[:, :], func=AF.Exp, scale=NSA_SCALE),
                          reads=[sb.b], writes=[p_t.b])
                    sc.op("dve", lambda e: e.tensor_tensor(out=p_t[:, :], in0=p_t[:, :], in1=cmk[ct][:, :], op=ALU.mult),
                          reads=[h.b], writes=[p_t.b])
                    for s in range(4):
                        acc = accs[s // 2]
                        sc.op("pe", lambda e: e.matmul(acc[:, (s % 2) * 196:(s % 2) * 196 + 193], lhsT=p_t[:, s * 128:(s + 1) * 128],
                                                       rhs=vcaug[:, ct, g, 0:193], start=(ct == 0 and s % 2 == 0), stop=False,
                                                       skip_group_check=True),
                              reads=[p_t.b, vcaug.b], writes=[acc.b])
                nsa_finalize(accs, 196, hd, hl, 0, True)
                for s in range(4):
                    acc = accs[s // 2]
                    src = acc[:, (s % 2) * 196 + 65:(s % 2) * 196 + 193]
                    if hl == 0:
                        sc.op("dve", lambda e: e.tensor_scalar(out=impacc[:, s, :], in0=src, scalar1=small[:, 40 + s:41 + s], scalar2=None,
                                                               op0=ALU.mult), reads=[acc.b, small.b], writes=[impacc.b])
                    else:
                        sc.op("dve", lambda e: e.scalar_tensor_tensor(out=impacc[:, s, :], in0=src, scalar=small[:, 40 + s:41 + s],
                                                                      in1=impacc[:, s, :], op0=ALU.mult, op1=ALU.add),
                              reads=[acc.b, small.b], writes=[impacc.b])
            for s in range(4):
                sc.op("dve", lambda e: e.tensor_tensor(out=scoreA[:, :], in0=impacc[:, s, :], in1=tkA[s][:], op=ALU.mult),
                      reads=[impacc.b, tkA[s].b], writes=[scoreA.b])
                sc.op("dve", lambda e: e.tensor_tensor(out=scoreA[:, :], in0=scoreA[:, :], in1=tkB[s][:], op=ALU.add),
                      reads=[tkB[s].b], writes=[scoreA.b])
                sc.op("dve", lambda e: e.max(out=small[:, 0:8], in_=scoreA[:, :]), reads=[scoreA.b], writes=[small.b])
                sc.op("dve", lambda e: e.match_replace(out=scoreB[:, :], in_to_replace=small[:, 0:8], in_values=scoreA[:, :],
                                                       imm_value=-2.0), reads=[scoreA.b, small.b], writes=[scoreB.b])
                sc.op("dve", lambda e: e.max(out=small[:, 8:16], in_=scoreB[:, :]), reads=[scoreB.b], writes=[small.b])
                sc.op("dve", lambda e: e.tensor_reduce(out=small[:, 16:17], in_=small[:, 8:16], axis=AX.X, op=ALU.min),
                      writes=[small.b])
                sc.op("dve", lambda e: e.tensor_scalar(out=sel01[:, :], in0=scoreA[:, :], scalar1=small[:, 16:17], scalar2=None,
                                                       op0=ALU.is_ge), reads=[scoreA.b, small.b], writes=[sel01.b])
                transposes(sel01, 0, lambda j0, n: selT[:, g, s * 128:(s + 1) * 128].rearrange("p (j c) -> p j c", j=1),
                           [sel01.b], selT.b, 1)
            for pr in range(2):
                pheads = gheads[2 * pr:2 * pr + 2]

                def pre_kt(kt):
                    mp = ps[5]
                    sm = smask[kt % 2]
                    sc.op("pe", lambda e: e.matmul(mp[:, :], lhsT=GS[:, kt * 128:(kt + 1) * 128], rhs=selT[:, g, :],
                                                   start=True, stop=True), reads=[GS.b, selT.b], writes=[mp.b])
                    if kt >= zone0:
                        sc.op("dve", lambda e: e.tensor_tensor(out=sm[:, :], in0=mp[:, :], in1=cm[kt - zone0][:, :], op=ALU.mult),
                              reads=[mp.b, h.b], writes=[sm.b])
                    else:
                        sc.op("act", lambda e: e.copy(out=sm[:, :], in_=mp[:, :]), reads=[mp.b], writes=[sm.b])

                attention(pheads, NSA_SCALE, KN[4 + g], VSW[:, :, g, :], list(range(nkt)),
                          lambda kt: smask[kt % 2], lambda hi: ps[2 + hi], pre_kt)
                for hi, hd in enumerate(pheads):
                    nsa_finalize(ps[2 + hi], 65, hd, 2 * pr + hi, 1, False)
                wt = [kt for kt in range(zone0 - 4, zone0 + 8) if kt >= 0]
                attention(pheads, NSA_SCALE, KN[8 + g], VSW[:, :, 4 + g, :], wt,
                          lambda kt: wm[kt - zone0 + 4], lambda hi: ps[2 + hi])
                for hi, hd in enumerate(pheads):
                    nsa_finalize(ps[2 + hi], 65, hd, 2 * pr + hi, 2, False)
            for hl in range(4):
                sc.op("act", lambda e: e.copy(out=oa3[:, :, (4 * g + hl) * 64:(4 * g + hl + 1) * 64],
                                              in_=tmpo[:, hl, :].rearrange("p (s d) -> p s d", d=64)),
                      reads=[tmpo.b], writes=[oa.b])
        ck(6)
        wcq = load_w(w_in, 0, 16, C_CQ, 512)
        sc.dma("sp", gq_b[:, :], g_q_in.partition_broadcast(128), writes=[gbs.b])
        for s in range(4):
            pb = ps[s % 2]
            mm_tm(pb, 512, actT, s, wcq, 0, 16)
            rstd_of(pb[:, :], [pb.b], 512, 8 + s, cqn[:, :], cqn.b)
            sc.op("dve", lambda e: e.scalar_tensor_tensor(out=cqn[:, :], in0=pb[:, :], scalar=stat[:, 8 + s:9 + s], in1=gq_b[:, :],
                                                          op0=ALU.mult, op1=ALU.mult), reads=[pb.b, stat.b, gq_b.b], writes=[cqn.b])
            transposes(cqn, 0, lambda j0, n: cqnT[:, j0:j0 + n, s * 128:(s + 1) * 128], [cqn.b], cqnT.b, 4)
        wu = next_wslot()
        wuv_ = wu[:, :, :].rearrange("p k c -> p (k c)").rearrange("p (k j c) -> p k j c", k=4, c=128)
        Ws, rb = wsrc(w_uq)
        for kc in range(4):
            sc.dma("pool", wuv_[:, kc, :, 0:96], Ws[kc * 128:(kc + 1) * 128, :].rearrange("p (j c) -> p j c", c=96),
                   reads=rb, writes=[wu.b])
        sc.op("dve", lambda e: e.tensor_copy(out=wuv_[:, :, :, 96:112], in_=wuv_[:, :, :, 80:96]), writes=[wu.b])
        sc.op("dve", lambda e: e.tensor_copy(out=wuv_[:, :, :, 112:128], in_=wuv_[:, :, :, 64:80]), writes=[wu.b])
        sc.dma("sp", rq[:, :], rqm_in[qt], writes=[rq.b])
        for hd in range(MLA_H):
            pb = ps[2 + hd % 2]
            for kc in range(4):
                sc.op("pe", lambda e: e.matmul(pb[:, :], lhsT=wuv_[:, kc, hd, :], rhs=cqnT[:, kc, :], start=(kc == 0), stop=(kc == 3)),
                      reads=[wu.b, cqnT.b], writes=[pb.b])
            rope_evac(pb, rq, Q[:, hd, :], Q.b, 64, 32, 96, 96)
        for hd in range(MLA_H):
            acc = ps[2 + hd % 2]
            attention([hd], MLA_SCALE, KM[hd], VM[:, :, hd, :], list(range(nkt)),
                      lambda kt: (cm[kt - zone0] if kt >= zone0 else None), lambda hi: acc)
            accv = acc[:, 0:260].rearrange("p (s c) -> p s c", c=65)
            sc.op("dve", lambda e: e.tensor_scalar_max(out=small[:, 48:52], in0=accv[:, :, 64], scalar1=1e-30), reads=[acc.b], writes=[small.b])
            sc.op("dve", lambda e: e.reciprocal(out=small[:, 48:52], in_=small[:, 48:52]), writes=[small.b])
            for s in range(4):
                sc.op("dve", lambda e: e.tensor_scalar(out=ob3[:, s, hd * 64:(hd + 1) * 64], in0=accv[:, s, 0:64],
                                                       scalar1=small[:, 48 + s:49 + s], scalar2=None, op0=ALU.mult),
                      reads=[acc.b, small.b], writes=[ob.b])
        ck(7)
        for s in range(4):
            transposes(oa, s * 1024, lambda j0, n: oaT[:, j0:j0 + n, s * 128:(s + 1) * 128], [oa.b], oaT.b, 8)
            transposes(ob, s * 1024, lambda j0, n: obT[:, j0:j0 + n, s * 128:(s + 1) * 128], [ob.b], obT.b, 8)
        sc.dma("sp", h[:], x_own[qt * 512:(qt + 1) * 512, :].rearrange("(s p) d -> p s d", p=128), writes=[h.b])
        sa, sbb = rtmp
        sc.barrier()
        for cb in range(4):
            for which, (gcol, wo_src, oT) in enumerate(((C_GA, w_o_nsa, oaT), (C_GB, w_o_mla, obT))):
                wgt = load_w(w_in, 0, 16, gcol + cb * 512, 512)
                wo = load_w(wo_src, 0, 8, cb * 512, 512)
                for c in range(4):
                    ft = cb * 4 + c
                    pg, pp = ps[0], ps[1]
                    mm_fm(pg, 128, wgt, c * 128, actT, 16)
                    sc.op("act", lambda e: e.activation(out=sa[:, :], in_=pg[:, :], func=AF.Sigmoid), reads=[pg.b], writes=[sa.b])
                    mm_fm(pp, 128, wo, c * 128, oT, 8)
                    if which == 0:
                        sc.op("dve", lambda e: e.tensor_tensor(out=sbb_keep[ft % 4][:, :], in0=sa[:, :], in1=pp[:, :], op=ALU.mult),
                              reads=[sa.b, pp.b], writes=[sbb_keep[ft % 4].b])
                    else:
                        sc.op("dve", lambda e: e.tensor_tensor(out=sa[:, :], in0=sa[:, :], in1=pp[:, :], op=ALU.mult),
                              reads=[pp.b], writes=[sa.b])
                        sc.op("dve", lambda e: e.tensor_tensor(out=mixT[:, ft, :], in0=sa[:, :], in1=sbb_keep[ft % 4][:, :], op=ALU.add),
                              reads=[sa.b, sbb_keep[ft % 4].b], writes=[mixT.b])
        for cg in range(4):
            wo = load_w(w_out, 0, 16, cg * 512, 512)
            for s in range(4):
                pb = ps[s % 2]
                mm_tm(pb, 512, mixT, s, wo, 0, 16)
                sc.op("dve", lambda e: e.tensor_tensor(out=h[:, s, cg * 512:(cg + 1) * 512],
                                                       in0=h[:, s, cg * 512:(cg + 1) * 512], in1=pb[:, :], op=ALU.add),
                      reads=[pb.b], writes=[h.b])

    sbb_keep = [View(fsm[:, 512 * i:512 * (i + 1)], f"sbbk{i}") for i in range(4)]

    def xattn_mlp(qt):
        norm_T(h, 4, load_g("g_xattn"), actT)
        wq = load_w(xa_wq, 0, 16, 0, 512)
        for hh in range(XA_H):
            pb = ps[hh % 2]
            mm_fm(pb, 128, wq, hh * 128, actT, 16)
            sc.op("act", lambda e: e.copy(out=qx[:, hh, :], in_=pb[:, :]), reads=[pb.b], writes=[qx.b])
        for hh in range(XA_H):
            pvb = [ps[4], ps[5]]
            for mt in range(2):
                sb = ps[2 + mt]
                p_t = pTx[mt]
                sc.op("pe", lambda e: e.matmul(sb[:, :], lhsT=kx[:, hh, mt * 128:(mt + 1) * 128], rhs=qx[:, hh, :],
                                               start=True, stop=True), reads=[kx.b, qx.b], writes=[sb.b])
                sc.op("act", lambda e: e.activation(out=p_t[:, :], in_=sb[:, :], func=AF.Exp, scale=float(XA_D) ** -0.5),
                      reads=[sb.b], writes=[p_t.b])
                for s in range(4):
                    acc = pvb[s // 2]
                    sc.op("pe", lambda e: e.matmul(acc[:, (s % 2) * 132:(s % 2) * 132 + 129],
                                                   lhsT=p_t[:, s * 128:(s + 1) * 128], rhs=vx[:, mt, hh, 0:129],
                                                   start=(mt == 0 and s % 2 == 0), stop=(mt == 1), skip_group_check=True),
                          reads=[p_t.b, vx.b], writes=[acc.b])
            for half in range(2):
                acc = pvb[half]
                accv = acc[:, 0:264].rearrange("p (s c) -> p s c", c=132)
                sc.op("dve", lambda e: e.reciprocal(out=rden[:, half * 2:half * 2 + 2], in_=accv[:, :, 128]),
                      reads=[acc.b], writes=[rden.b])
                for s2 in range(2):
                    s = half * 2 + s2
                    sc.op("dve", lambda e: e.tensor_scalar(
                        out=ox[:, s * 512 + hh * 128: s * 512 + (hh + 1) * 128], in0=accv[:, s2, 0:128],
                        scalar1=rden[:, half * 2 + s2:half * 2 + s2 + 1], scalar2=None, op0=ALU.mult),
                        reads=[acc.b, rden.b], writes=[ox.b])
        for s in range(4):
            transposes(ox, s * 512, lambda j0, n: oxT[:, j0:j0 + n, s * 128:(s + 1) * 128], [ox.b], oxT.b, 4)
        for cg in range(4):
            wo = load_w(xa_wo, 0, 4, cg * 512, 512)
            for s in range(4):
                pb = ps[s % 2]
                mm_tm(pb, 512, oxT, s, wo, 0, 4)
                sc.op("dve", lambda e: e.tensor_tensor(out=h[:, s, cg * 512:(cg + 1) * 512],
                                                       in0=h[:, s, cg * 512:(cg + 1) * 512], in1=pb[:, :], op=ALU.add),
                      reads=[pb.b], writes=[h.b])
        sc.barrier()
        norm_T(h, 4, load_g("g_mlp"), actT)
        for blk in range(16):
            w1 = load_w(w_ff1, 0, 16, blk * 512, 512)
            for c in range(4):
                pb = ps[c % 2]
                rt = rtmp[c % 2]
                mm_fm(pb, 128, w1, c * 128, actT, 16)
                sc.op("act", lambda e: e.activation(out=rt[:], in_=pb[:, :], func=AF.Relu), reads=[pb.b], writes=[rt.b])
                sc.op("dve", lambda e: e.tensor_tensor(out=hid[:, blk * 4 + c, :], in0=rt[:], in1=rt[:], op=ALU.mult),
                      reads=[rt.b], writes=[hid.b])
        for cg in range(4):
            for kb in range(4):
                w2 = load_w(w_ff2, kb * 16, 16, cg * 512, 512)
                for s in range(4):
                    mm_tm(ps[2 + s], 512, hid, s, w2, 0, 16, kc0=kb * 16, first=(kb == 0), last=(kb == 3))
            for s in range(4):
                pb = ps[2 + s]
                sc.op("dve", lambda e: e.tensor_tensor(out=h[:, s, cg * 512:(cg + 1) * 512],
                                                       in0=h[:, s, cg * 512:(cg + 1) * 512], in1=pb[:, :], op=ALU.add),
                      reads=[pb.b], writes=[h.b])
        gfin = load_g("g_final")
        for s in range(4):
            rstd_of(h[:, s, :], [h.b], D, 8 + s, xnb[0][:], xnb[0].b)
            sc.op("dve", lambda e: e.scalar_tensor_tensor(out=h[:, s, :], in0=h[:, s, :], scalar=stat[:, 8 + s:9 + s],
                                                          in1=gfin[:], op0=ALU.mult, op1=ALU.mult),
                  reads=[stat.b, gfin.b], writes=[h.b])
        sc.dma("sp", out[qt * 512:(qt + 1) * 512, :].rearrange("(s p) d -> p s d", p=128), h[:], reads=[h.b])

    try:
        _body()
    except _Stop:
        sc.dma("sp", out[0:512, :].rearrange("(s p) d -> p s d", p=128), h[:], reads=[h.b])
    sc.finish()
    return nc


_CACHE = {}


def _rope_np(npos, dim):
    inv = (1.0 / (np.float32(ROPE_THETA) ** (np.arange(0, dim, 2, dtype=np.float32) / np.float32(dim)))).astype(np.float32)
    ang = np.arange(npos, dtype=np.float32)[:, None] * inv[None, :]
    return np.cos(ang).astype(np.float32), np.sin(ang).astype(np.float32)


def _consts(S):
    NKT = S // 128
    NC = S // 16 - 1
    NCT = (NC + 127) // 128
    ca, sa = _rope_np(S, NSA_ROT)
    cb, sb = _rope_np(S, MLA_ROPE)
    rkn = np.zeros((128, S), np.float32)
    rkn[0:12] = ca.T
    rkn[12:24] = ca.T
    rkn[96:108] = -sa.T
    rkn[108:120] = sa.T
    rqm = np.zeros((128, S), np.float32)
    rqm[64:80] = cb.T
    rqm[80:96] = cb.T
    rqm[96:112] = -sb.T
    rqm[112:128] = sb.T
    tkr = np.concatenate([cb, cb, -sb, sb], axis=1).astype(np.float32)
    kpos = (np.arange(NKT)[None, :] * 128 + np.arange(128)[:, None]).astype(np.float32)
    cend = (16 * (np.arange(NCT)[None, :] * 128 + np.arange(128)[:, None]) + 31).astype(np.float32)
    jidx = np.tile(np.arange(128, dtype=np.float32)[None, :], (128, 1))
    cs = np.arange(NCT * 128) * 16
    ce = cs + 32
    ss = np.arange(128) * 64
    se = ss + 64
    ov = np.clip(np.minimum(ce[:, None], se[None, :]) - np.maximum(cs[:, None], ss[None, :]), 0, None).astype(np.float32) / 32.0
    ov[NC:] = 0.0
    msel = np.ascontiguousarray(ov.reshape(NCT, 128, 128).transpose(1, 0, 2))
    gsel = (np.arange(128)[:, None] == (np.arange(S)[None, :] // 64)).astype(np.float32)
    return dict(rkn=rkn, rqm_full=rqm, tkr=tkr, kpos=kpos, cend=cend, jidx=jidx, msel=msel, gsel=gsel,
                ident=np.eye(128, dtype=np.float32))


def kernel(**inputs):
    x = np.asarray(inputs["x"], dtype=np.float32)
    B, S, _ = x.shape
    NQT = S // 1024
    ncores = 2 * B
    if S not in _CACHE:
        _CACHE[S] = (build_program(S), _consts(S))
    nc, cst = _CACHE[S]

    def sq(name):
        a = np.asarray(inputs[name], dtype=np.float32)
        return np.ascontiguousarray(a[0])

    shared = {
        "g_mix": sq("g_mix").reshape(1, D), "g_xattn": sq("g_xattn").reshape(1, D), "g_mem": sq("g_mem").reshape(1, D),
        "g_mlp": sq("g_mlp").reshape(1, D), "g_final": np.asarray(inputs["g_final"], np.float32).reshape(1, D),
        "mla_g_q": sq("mla_g_q").reshape(1, -1), "mla_g_kv": sq("mla_g_kv").reshape(1, -1),
    }
    for k in ("w_in", "cmp_pos_k", "cmp_w1_k", "cmp_w2_k", "cmp_pos_v", "cmp_w1_v", "cmp_w2_v", "mla_w_uq", "mla_w_uk",
              "mla_w_uv", "w_o_nsa", "w_o_mla", "w_out", "xa_wq", "xa_wkv", "xa_wo", "w_ff1", "w_ff2"):
        shared[k] = sq(k)
    for k in ("ident", "rkn", "tkr", "kpos", "cend", "jidx", "msel", "gsel"):
        shared[k] = cst[k]
    memv = np.asarray(inputs["mem"], dtype=np.float32)
    in_maps = []
    for c in range(ncores):
        b, r = c // 2, c % 2
        tiles = [2 * k + r for k in range(NQT)]
        pos = np.concatenate([np.arange(t * 512, (t + 1) * 512) for t in tiles])
        m = dict(shared)
        m["x_full"] = np.ascontiguousarray(x[b])
        m["x_own"] = np.ascontiguousarray(x[b][pos])
        m["mem"] = np.ascontiguousarray(memv[b])
        m["rqn"] = np.ascontiguousarray(cst["rkn"][:, pos].reshape(128, NQT, 512).transpose(1, 0, 2))
        m["rqm"] = np.ascontiguousarray(cst["rqm_full"][:, pos].reshape(128, NQT, 512).transpose(1, 0, 2))
        m["qpos"] = pos.astype(np.float32).reshape(NQT, 512)
        m["cur"] = np.ascontiguousarray((pos // 64).astype(np.float32).reshape(NQT, 4, 128).transpose(0, 2, 1))
        in_maps.append(m)
    res = run_bass_kernel_spmd(nc, in_maps, core_ids=list(range(ncores)))
    outp = np.empty((B, S, D), dtype=np.float32)
    for c in range(ncores):
        b, r = c // 2, c % 2
        o = res.results[c]["out"]
        for k in range(NQT):
            t = 2 * k + r
            outp[b, t * 512:(t + 1) * 512] = o[k * 512:(k + 1) * 512]
    return outp
```

```python
import numpy as np
import concourse.bass as bass
import concourse.mybir as mybir
from concourse.bass_utils import run_bass_kernel_spmd

F32 = mybir.dt.float32
BF16 = mybir.dt.bfloat16
ALU = mybir.AluOpType
AF = mybir.ActivationFunctionType
AX = mybir.AxisListType

D = 2048
DFF = 8192
MEM = 256
EPS = 1e-6
XA_H, XA_D = 4, 128
SEM_CH = 12000
N_DSEM = 32


class Buf:
    __slots__ = ("w", "r", "name", "excl")

    def __init__(self, name="", excl=False):
        self.w = None
        self.r = {}
        self.name = name
        self.excl = excl


class Sched:
    def __init__(self, nc):
        self.nc = nc
        self.eng = {}
        for name, h in (("pe", nc.tensor), ("act", nc.scalar), ("dve", nc.vector),
                        ("pool", nc.gpsimd), ("sp", nc.sync)):
            self.eng[name] = dict(h=h, n=0, sems=[], seen={}, seen_d={})
        self.dsem = [nc.semaphore(f"dsem{i}").__enter__() for i in range(N_DSEM)]
        self.dcnt = [0] * N_DSEM
        self.dnext = {"sp": 0, "pool": 0}
        self.dbase = {"sp": 0, "pool": N_DSEM // 2}
        self.n_wait = 0

    def _sem(self, en, chunk):
        e = self.eng[en]
        while len(e["sems"]) <= chunk:
            e["sems"].append(self.nc.semaphore(f"prog_{en}_{len(e['sems'])}").__enter__())
        return e["sems"][chunk]

    def _wait(self, en, tok):
        e = self.eng[en]
        if tok[0] == "e":
            _, e2, seq = tok
            if e2 == en and en == "pe":
                return
            if e["seen"].get(e2, 0) >= seq:
                return
            chunk = (seq - 1) // SEM_CH
            e["h"].wait_ge(self._sem(e2, chunk), seq - chunk * SEM_CH)
            e["seen"][e2] = seq
            self.n_wait += 1
        else:
            _, s, val = tok
            if e["seen_d"].get(s, 0) >= val:
                return
            e["h"].wait_ge(self.dsem[s], val)
            e["seen_d"][s] = val
            self.n_wait += 1

    def _deps(self, en, reads, writes):
        for b in reads:
            if b.w is not None:
                self._wait(en, b.w)
        for b in writes:
            if b.w is not None:
                self._wait(en, b.w)
            for t in b.r.values():
                self._wait(en, t)

    def op(self, en, fn, reads=(), writes=()):
        e = self.eng[en]
        if any(b.excl for b in reads):
            writes = list(writes) + [b for b in reads if b.excl]
            reads = [b for b in reads if not b.excl]
        self._deps(en, reads, writes)
        ins = fn(e["h"])
        e["n"] += 1
        seq = e["n"]
        chunk = (seq - 1) // SEM_CH
        ins.then_inc(self._sem(en, chunk), 1)
        tok = ("e", en, seq)
        for b in reads:
            b.r[en] = tok
        for b in writes:
            b.w = tok
            b.r = {}
        return tok

    def dma(self, qn, out, in_, reads=(), writes=()):
        e = self.eng[qn]
        self._deps(qn, reads, writes)
        s = self.dbase[qn] + self.dnext[qn]
        self.dnext[qn] = (self.dnext[qn] + 1) % (N_DSEM // 2)
        if self.dcnt[s] > 0:
            self._wait(qn, ("d", s, self.dcnt[s]))
        ins = e["h"].dma_start(out=out, in_=in_)
        self.dcnt[s] += 16
        ins.then_inc(self.dsem[s], 16)
        tok = ("d", s, self.dcnt[s])
        for b in reads:
            b.r[("d", s)] = tok
        for b in writes:
            b.w = tok
            b.r = {}
        return tok

    def barrier(self):
        toks = [("e", en, e["n"]) for en, e in self.eng.items() if e["n"] > 0]
        dt = [("d", s, c) for s, c in enumerate(self.dcnt) if c > 0]
        for en in self.eng:
            for t in toks:
                if t[1] != en:
                    self._wait(en, t)
            for t in dt:
                self._wait(en, t)

    def finish(self):
        for s, c in enumerate(self.dcnt):
            if c > 0:
                self._wait("sp", ("d", s, c))


class T:
    def __init__(self, nc, name, shape, dtype, psum=False):
        if psum:
            self.t = nc.psum_tensor(name, shape, dtype).__enter__()
        else:
            self.t = nc.sbuf_tensor(name, shape, dtype).__enter__()
        self.b = Buf(name, excl=psum)

    def __getitem__(self, idx):
        return self.t[idx]


NSA_H, NSA_G, NSA_DK, NSA_DV, NSA_ROT = 16, 4, 96, 64, 24
MLA_H, MLA_NOPE, MLA_ROPE, MLA_DV, MLA_QR, MLA_KVR = 16, 64, 32, 64, 512, 256
ROPE_THETA = 500000.0
C_Q, C_KC, C_VC, C_KS, C_VS, C_KW, C_VW, C_GN, C_CQ, C_CKV, C_KR, C_GA, C_GB = (
    0, 1536, 1920, 2176, 2560, 2816, 3200, 3456, 3504, 4016, 4272, 4304, 6352)
D_IN = 8400
NSA_SCALE = float(NSA_DK) ** -0.5
MLA_SCALE = float(MLA_NOPE + MLA_ROPE) ** -0.5
WKC = 1984


def build_program(S, dbg=None):
    NQT = S // 1024
    NOWN = NQT * 512
    NT = S // 512
    NKT = S // 128
    NC = S // 16 - 1
    NCT = (NC + 127) // 128
    nc = bass.Bass("TRN2", target_bir_lowering=False)
    sc = Sched(nc)

    def din(name, shape, dt=F32):
        return nc.dram_tensor(name, list(shape), dt, kind="ExternalInput").ap()

    x_full = din("x_full", [S, D])
    x_own = din("x_own", [NOWN, D])
    mem = din("mem", [MEM, D])
    gvecs = {k: din(k, [1, D]) for k in ("g_mix", "g_xattn", "g_mem", "g_mlp", "g_final")}
    g_q_in = din("mla_g_q", [1, MLA_QR])
    g_kv_in = din("mla_g_kv", [1, MLA_KVR])
    w_in = din("w_in", [D, D_IN])
    cmp_pos_k = din("cmp_pos_k", [32, NSA_DK])
    cmp_w1_k = din("cmp_w1_k", [32 * NSA_DK, NSA_DK])
    cmp_w2_k = din("cmp_w2_k", [NSA_DK, NSA_DK])
    cmp_pos_v = din("cmp_pos_v", [32, NSA_DV])
    cmp_w1_v = din("cmp_w1_v", [32 * NSA_DV, NSA_DV])
    cmp_w2_v = din("cmp_w2_v", [NSA_DV, NSA_DV])
    w_uq = din("mla_w_uq", [MLA_QR, MLA_H * 96])
    w_uk = din("mla_w_uk", [MLA_KVR, MLA_H * 64])
    w_uv = din("mla_w_uv", [MLA_KVR, MLA_H * 64])
    w_o_nsa = din("w_o_nsa", [1024, D])
    w_o_mla = din("w_o_mla", [1024, D])
    w_out = din("w_out", [D, D])
    xa_wq = din("xa_wq", [D, 512])
    xa_wkv = din("xa_wkv", [D, 1024])
    xa_wo = din("xa_wo", [512, D])
    w_ff1 = din("w_ff1", [D, DFF])
    w_ff2 = din("w_ff2", [DFF, D])
    ident_in = din("ident", [128, 128])
    rkn_in = din("rkn", [128, S])
    tkr_in = din("tkr", [S, 64])
    rqn_in = din("rqn", [NQT, 128, 512])
    rqm_in = din("rqm", [NQT, 128, 512])
    qpos_in = din("qpos", [NQT, 512])
    cur_in = din("cur", [NQT, 128, 4])
    kpos_in = din("kpos", [128, NKT])
    cend_in = din("cend", [128, NCT])
    jidx_in = din("jidx", [128, 128])
    msel_in = din("msel", [128, NCT, 128])
    gsel_in = din("gsel", [128, S])
    out = nc.dram_tensor("out", [NOWN, D], F32, kind="ExternalOutput").ap()

    KN = nc.dram_tensor("KN", [12, 96, S], BF16).ap()
    KM = nc.dram_tensor("KM", [MLA_H, 96, S], BF16).ap()
    VSW = nc.dram_tensor("VSW", [NKT, 128, 8, 65], BF16).ap()
    VM = nc.dram_tensor("VM", [NKT, 128, MLA_H, 65], BF16).ap()
    VCT = nc.dram_tensor("VCT", [256, S], BF16).ap()
    WB = {}
    for nm, src in (("w_in", w_in), ("mla_w_uq", w_uq), ("w_o_nsa", w_o_nsa), ("w_o_mla", w_o_mla), ("w_out", w_out),
                    ("xa_wq", xa_wq), ("xa_wo", xa_wo), ("w_ff1", w_ff1), ("w_ff2", w_ff2), ("gsel", gsel_in)):
        WB[id(src)] = (nc.dram_tensor(nm + "_bf", list(src.shape), BF16).ap(), Buf(nm + "_bf"), src)
    use_bf = [False]

    def wsrc(W):
        if use_bf[0] and id(W) in WB:
            return WB[id(W)][0], [WB[id(W)][1]]
        return W, []

    def precast():
        for wb, buf, src in WB.values():
            R = src.shape[0]
            step = min(R, 256)
            for r0 in range(0, R, step):
                sc.dma("pool", wb[r0:r0 + step, :], src[r0:r0 + step, :], writes=[buf])
        use_bf[0] = True

    ident = T(nc, "identb", [128, 128], BF16)
    gbs = T(nc, "gbs", [128, D], F32)
    h = T(nc, "h", [128, 4, D], F32)
    xnb = [T(nc, "xnb0", [128, D], BF16)]
    stat = T(nc, "stat", [128, 16], F32)
    actT = T(nc, "actT", [128, 16, 512], BF16)
    wsl = [T(nc, f"wsl{i}", [128, 16, 512], BF16) for i in range(2)]
    rtmp = [T(nc, f"rtmp{i}", [128, 512], F32) for i in range(2)]
    kx = T(nc, "kx_sb", [128, XA_H, MEM], BF16)
    vx = T(nc, "vx", [128, 2, XA_H, 132], BF16)
    rden = T(nc, "rden", [128, 8], F32)
    kposc = T(nc, "kposc", [128, NKT], F32)
    cendc = T(nc, "cendc", [128, NCT], F32)
    kccT = T(nc, "kccT", [128, NSA_G, NCT * 128], BF16)
    vcaug = T(nc, "vcaug", [128, NCT, NSA_G, 196], BF16)
    ARENA = 41472
    arena = T(nc, "arena", [128, ARENA], BF16)
    fsm = T(nc, "fsm", [128, 3072], F32)
    ps = [T(nc, f"ps{i}", [128, 512], F32, psum=True) for i in range(6)]
    psTs = [T(nc, f"psT{i}", [128, 1024], BF16, psum=True) for i in range(2)]

    class View:
        def __init__(self, ap, name, buf=None):
            self.ap = ap
            self.b = buf if buf is not None else Buf(name)

        def __getitem__(self, idx):
            return self.ap[idx]

    def shaped(ap, shape):
        if len(shape) == 2:
            ap = ap.rearrange("p (a b) -> p a b", b=shape[1])
        elif len(shape) == 3:
            ap = ap.rearrange("p (a b c) -> p a b c", b=shape[1], c=shape[2])
        return ap

    def carve(name, off, shape, buf=None):
        n = int(np.prod(shape))
        assert off + n <= ARENA, (name, off, n)
        return View(shaped(arena[:, off:off + n], shape), name, buf)

    hbf = h[:, :, :].rearrange("p s d -> p (s d)").bitcast(BF16)

    def carve_h(name, off, shape):
        n = int(np.prod(shape))
        assert off + n <= 16384
        return View(shaped(hbf[:, off:off + n], shape), name, h.b)

    def fcarve(name, off, shape):
        n = int(np.prod(shape))
        assert off + n <= 3072
        return View(shaped(fsm[:, off:off + n], shape), name)

    hid = carve("hid", 0, [64, 512])
    qx = carve("qx", 0, [XA_H, 512])
    pTx = [carve(f"pTx{i}", 2048 + i * 512, [512]) for i in range(2)]
    ox = carve("ox", 3072, [4 * 512])
    oxT = carve("oxT", 5120, [4, 512])
    memT = carve("memT", 8192, [16, MEM])
    WK = carve("WK", 0, [16, WKC])
    Qb = Buf("Q")
    Q = carve("Q", 0, [16, 512], Qb)
    oaT = carve("oaT", 0, [8, 512], Qb)
    obT = carve("obT", 4096, [8, 512], Qb)
    Ob = Buf("O")
    oa = carve("oa", 8192, [4096], Ob)
    ob = carve("ob", 12288, [4096], Ob)
    oa3 = View(oa[:, :].rearrange("p (s c) -> p s c", c=1024), "oa3", Ob)
    ob3 = View(ob[:, :].rearrange("p (s c) -> p s c", c=1024), "ob3", Ob)
    mixT = carve("mixT", 8192, [16, 512], Ob)
    Ksl = [carve(f"Ksl{i}", 16384 + i * 1536, [1536]) for i in range(2)]
    Vsl = [carve(f"Vsl{i}", 19456 + i * 784, [12, 65]) for i in range(2)]
    pT = [carve(f"pT{i}", 28320 + i * 512, [512]) for i in range(5)]
    smask = [carve(f"smask{i}", 30880 + i * 512, [512]) for i in range(4)]
    selT = carve("selT", 23584, [NSA_G, 512])
    cqn = carve("cqn", 25632, [512])
    cqnT = carve("cqnT", 26144, [4, 512])
    sel01 = carve("sel01", 28192, [128])
    GSb = Buf("GS_W3")
    GS = carve("GS", ARENA - S, [S], GSb)
    W3 = carve("W3", ARENA - 8192, [16, 512], GSb)
    wsl3 = [wsl[0], wsl[1], W3]
    assert 32928 <= ARENA - S
    cm = [carve_h(f"cm{z}", z * 512, [512]) for z in range(8)]
    wm = [carve_h(f"wm{z}", 4096 + z * 512, [512]) for z in range(12)]
    cmk = [carve_h(f"cmk{z}", 10240 + z * 512, [512]) for z in range(4)]
    qrow = fcarve("qrow", 0, [512])
    rq = fcarve("rq", 512, [512])
    gates = fcarve("gates", 1024, [4, 48])
    tmpo = fcarve("tmpo", 1216, [4, 256])
    impacc = fcarve("impacc", 2240, [4, 128])
    scoreA = fcarve("scoreA", 2752, [128])
    scoreB = fcarve("scoreB", 2880, [128])
    small = T(nc, "small_sb", [128, 64], F32)
    jidx = T(nc, "jidx_sb", [128, 128], F32)
    e0 = T(nc, "e0", [128, 128], F32)
    tkA = [View(h[:, 3, s * 256:s * 256 + 128], f"tkA{s}", h.b) for s in range(4)]
    tkB = [View(h[:, 3, s * 256 + 128:s * 256 + 256], f"tkB{s}", h.b) for s in range(4)]
    curc = T(nc, "curc", [128, 4], F32)
    gq_b = View(gbs[:, 0:MLA_QR], "gq_b", gbs.b)
    gkv_b = fcarve("gkv_b", 768, [MLA_KVR])

    def load_g(name):
        sc.dma("sp", gbs[:], gvecs[name].partition_broadcast(128), writes=[gbs.b])
        return gbs

    wslot_i = [0]

    def next_wslot():
        slot = wsl3[wslot_i[0] % 3]
        wslot_i[0] += 1
        return slot

    def load_w(W, k0, nk, c0, ncols):
        slot = next_wslot()
        Ws, rb = wsrc(W)
        src = Ws[k0 * 128:(k0 + nk) * 128, c0:c0 + ncols].rearrange("(k p) c -> p k c", p=128)
        qs = ("pool", "sp") if (use_bf[0] and id(W) in WB) else ("pool", "pool")
        for ai, a in enumerate(range(0, nk, 4)):
            n = min(4, nk - a)
            sc.dma(qs[ai % 2], slot[:, a:a + n, 0:ncols], src[:, a:a + n, :], reads=rb, writes=[slot.b])
        return slot

    import os
    KSTOP = int(os.environ.get("KSTOP", "0"))

    class _Stop(Exception):
        pass

    def ck(n):
        if KSTOP == n:
            raise _Stop()

    tr_i = [0]

    def transposes(src_t, src_cols, dst_fn, reads, dst_buf, nchunk, rows=128, width=128):
        for j0 in range(0, nchunk, 4):
            n = min(4, nchunk - j0)
            psT = psTs[tr_i[0] % 2]
            tr_i[0] += 1
            for j in range(n):
                sc.op("pe", lambda e: e.transpose(
                    out=psT[0:width, j * 128: j * 128 + rows],
                    in_=src_t[0:rows, src_cols + (j0 + j) * width: src_cols + (j0 + j + 1) * width],
                    identity=ident[0:rows, 0:rows]), reads=list(reads) + [ident.b], writes=[psT.b])
            srcv = psT[0:width, 0:n * 128].rearrange("p (j c) -> p j c", c=128)[:, :, 0:rows]
            if tr_i[0] % 2:
                sc.op("act", lambda e: e.copy(out=dst_fn(j0, n), in_=srcv), writes=[psT.b, dst_buf])
            else:
                sc.op("dve", lambda e: e.tensor_copy(out=dst_fn(j0, n), in_=srcv), writes=[psT.b, dst_buf])

    def rstd_of(src_ap, src_bufs, width, col, junk_ap, junk_buf):
        sc.op("dve", lambda e: e.memset(stat[:, col:col + 1], 0.0), writes=[stat.b])
        sc.op("act", lambda e: e.activation(out=junk_ap, in_=src_ap, func=AF.Square, scale=float(width) ** -0.5,
                                            accum_out=stat[:, col:col + 1]),
              reads=src_bufs, writes=[junk_buf, stat.b])
        sc.op("dve", lambda e: e.tensor_scalar_add(out=stat[:, col:col + 1], in0=stat[:, col:col + 1], scalar1=EPS),
              writes=[stat.b])
        sc.op("act", lambda e: e.sqrt(out=stat[:, col:col + 1], in_=stat[:, col:col + 1]), writes=[stat.b])
        sc.op("dve", lambda e: e.reciprocal(out=stat[:, col:col + 1], in_=stat[:, col:col + 1]), writes=[stat.b])

    def norm_T(src, nsub, g, dst):
        for s in range(nsub):
            xb = xnb[0]
            rstd_of(src[:, s, :], [src.b], D, s, xb[:], xb.b)
            sc.op("dve", lambda e: e.scalar_tensor_tensor(out=xb[:], in0=src[:, s, :], scalar=stat[:, s:s + 1],
                                                          in1=g[:], op0=ALU.mult, op1=ALU.mult),
                  reads=[src.b, stat.b, g.b], writes=[xb.b])
            transposes(xb, 0, lambda j0, n: dst[:, j0:j0 + n, s * 128:(s + 1) * 128], [xb.b], dst.b, 16)

    def mm_fm(pst, M, wslot, c0, act, nk):
        for kc in range(nk):
            sc.op("pe", lambda e: e.matmul(pst[0:M, :], lhsT=wslot[:, kc, c0:c0 + M], rhs=act[:, kc, :],
                                           start=(kc == 0), stop=(kc == nk - 1)),
                  reads=[wslot.b, act.b], writes=[pst.b])

    def mm_tm(pst, ncols, act, s, wslot, c0, nk, kc0=0, first=True, last=True):
        for kc in range(nk):
            sc.op("pe", lambda e: e.matmul(pst[:, 0:ncols], lhsT=act[:, kc0 + kc, s * 128:(s + 1) * 128],
                                           rhs=wslot[:, kc, c0:c0 + ncols],
                                           start=(first and kc == 0), stop=(last and kc == nk - 1)),
                  reads=[wslot.b, act.b], writes=[pst.b])

    def rope_evac(pst, table, dst_ap, dst_buf, lo, n, swp, copy_rows):
        ta, tb = rtmp
        sc.op("act", lambda e: e.copy(out=dst_ap[0:copy_rows, :], in_=pst[0:copy_rows, :]), reads=[pst.b], writes=[dst_buf])
        sc.op("dve", lambda e: e.tensor_tensor(out=ta[lo:lo + n, :], in0=pst[swp:swp + n, :], in1=table[swp:swp + n, :],
                                               op=ALU.mult), reads=[pst.b, table.b], writes=[ta.b])
        sc.op("dve", lambda e: e.tensor_tensor(out=tb[lo:lo + n, :], in0=pst[lo:lo + n, :], in1=table[lo:lo + n, :],
                                               op=ALU.mult), reads=[pst.b, table.b], writes=[tb.b])
        sc.op("dve", lambda e: e.tensor_tensor(out=dst_ap[lo:lo + n, :], in0=ta[lo:lo + n, :], in1=tb[lo:lo + n, :],
                                               op=ALU.add), reads=[ta.b, tb.b], writes=[dst_buf])

    def _body():
        sc.dma("pool", ident[:], ident_in, writes=[ident.b])
        sc.dma("sp", kposc[:], kpos_in, writes=[kposc.b])
        sc.dma("sp", cendc[:], cend_in, writes=[cendc.b])
        sc.dma("sp", jidx[:], jidx_in, writes=[jidx.b])
        sc.op("dve", lambda e: e.tensor_single_scalar(out=e0[:], in_=jidx[:], scalar=0.0, op=ALU.is_equal),
              reads=[jidx.b], writes=[e0.b])

        sc.dma("sp", h[:, 0:2, :], mem.rearrange("(s p) d -> p s d", p=128), writes=[h.b])
        norm_T(h, 2, load_g("g_mem"), memT)
        wk = load_w(xa_wkv, 0, 16, 0, 512)
        for hh in range(XA_H):
            pb = ps[hh % 2]
            for kc in range(16):
                sc.op("pe", lambda e: e.matmul(pb[:, 0:MEM], lhsT=wk[:, kc, hh * 128:(hh + 1) * 128],
                                               rhs=memT[:, kc, :], start=(kc == 0), stop=(kc == 15)),
                      reads=[wk.b, memT.b], writes=[pb.b])
            sc.op("act", lambda e: e.copy(out=kx[:, hh, :], in_=pb[:, 0:MEM]), reads=[pb.b], writes=[kx.b])
        wv = load_w(xa_wkv, 0, 16, 512, 512)
        sc.op("dve", lambda e: e.memset(vx[:], 1.0), writes=[vx.b])
        for mt in range(2):
            pb = ps[2 + mt]
            for kc in range(16):
                sc.op("pe", lambda e: e.matmul(pb[:, :], lhsT=memT[:, kc, mt * 128:(mt + 1) * 128],
                                               rhs=wv[:, kc, :], start=(kc == 0), stop=(kc == 15)),
                      reads=[wv.b, memT.b], writes=[pb.b])
            sc.op("dve", lambda e: e.tensor_copy(out=vx[:, mt, :, 0:128],
                                                 in_=pb[:, :].rearrange("p (h d) -> p h d", d=128)),
                  reads=[pb.b], writes=[vx.b])
        sc.barrier()
        ck(1)

        fam = [C_KC, C_KS, C_KW]
        for j in range(12):
            base = fam[j // 4] + (j % 4) * 96
            for a in (0, 8):
                sc.dma("pool", WK[:, a:a + 8, j * 120:j * 120 + 96],
                       w_in[a * 128:(a + 8) * 128, base:base + 96].rearrange("(k p) c -> p k c", p=128), writes=[WK.b])
        for (dst0, src0, n) in ((1440, C_VC, 256), (1696, C_CKV, 288)):
            for a in range(0, 16, 4):
                sc.dma("pool", WK[:, a:a + 4, dst0:dst0 + n],
                       w_in[a * 128:(a + 4) * 128, src0:src0 + n].rearrange("(k p) c -> p k c", p=128), writes=[WK.b])
        WKs = WK[:, :, 0:1440].rearrange("p k (j c) -> p k j c", c=120)
        sc.op("dve", lambda e: e.tensor_copy(out=WKs[:, :, :, 96:108], in_=WKs[:, :, :, 12:24]), writes=[WK.b])
        sc.op("dve", lambda e: e.tensor_copy(out=WKs[:, :, :, 108:120], in_=WKs[:, :, :, 0:12]), writes=[WK.b])
        wA = wsl[0]
        wAf = wA[:, :, :].rearrange("p k c -> p (k c)")
        WUK = wAf[:, 0:2048].rearrange("p (k c) -> p k c", c=1024)
        WUV = wAf[:, 2048:4096].rearrange("p (k c) -> p k c", c=1024)
        sc.dma("pool", WUK, w_uk.rearrange("(k p) c -> p k c", p=128), writes=[wA.b])
        sc.dma("pool", WUV, w_uv.rearrange("(k p) c -> p k c", p=128), writes=[wA.b])
        WV = wsl[1]
        for (dst0, src0) in ((0, C_VS), (256, C_VW)):
            for a in range(0, 16, 4):
                sc.dma("pool", WV[:, a:a + 4, dst0:dst0 + 256],
                       w_in[a * 128:(a + 4) * 128, src0:src0 + 256].rearrange("(k p) c -> p k c", p=128), writes=[WV.b])
        precast()
        a0 = 16 * WKC
        vst = carve("vst", a0, [4, 8, 65])
        vmst = carve("vmst", a0 + 2080, [4, 16, 65])
        ckvn = carve("ckvn", a0 + 6240, [256])
        krb = carve("krb", a0 + 6496, [32])
        ckvnT = carve("ckvnT", a0 + 6528, [2, 512])
        krT = carve("krT", a0 + 7552, [512])
        kstA = [carve(f"kstA{i}", a0 + 8064 + i * 512, [512]) for i in range(3)]
        rk = fcarve("rk", 0, [512])
        tkr = fcarve("tkr", 512, [4, 64])
        sc.dma("sp", gkv_b[:, :], g_kv_in.partition_broadcast(128), writes=[gkv_b.b])
        sc.op("dve", lambda e: e.memset(vst[:], 1.0), writes=[vst.b])
        sc.op("dve", lambda e: e.memset(vmst[:], 1.0), writes=[vmst.b])
        gmix = load_g("g_mix")
        kst_i = 0
        for t in range(NT):
            sc.dma("sp", h[:], x_full[t * 512:(t + 1) * 512, :].rearrange("(s p) d -> p s d", p=128), writes=[h.b])
            sc.dma("sp", rk[:, :], rkn_in[:, t * 512:(t + 1) * 512], writes=[rk.b])
            sc.dma("sp", tkr[:, :, :], tkr_in[t * 512:(t + 1) * 512, :].rearrange("(s p) c -> p s c", p=128), writes=[tkr.b])
            norm_T(h, 4, gmix, actT)
            for j in range(12):
                pb = ps[j % 2]
                mm_fm(pb, 120, WK, j * 120, actT, 16)
                ks_ = kstA[kst_i % 3]
                kst_i += 1
                rope_evac(pb, rk, ks_, ks_.b, 0, 24, 96, 96)
                sc.dma("sp", KN[j, :, t * 512:(t + 1) * 512], ks_[0:96, :], reads=[ks_.b])
            for j in range(2):
                pb = ps[j % 2]
                mm_fm(pb, 128, WK, 1440 + j * 128, actT, 16)
                ks_ = kstA[kst_i % 3]
                kst_i += 1
                sc.op("act", lambda e: e.copy(out=ks_[:, :], in_=pb[:, :]), reads=[pb.b], writes=[ks_.b])
                sc.dma("sp", VCT[j * 128:(j + 1) * 128, t * 512:(t + 1) * 512], ks_[:, :], reads=[ks_.b])
            for s in range(4):
                pa, pv = ps[2], ps[3]
                mm_tm(pa, 288, actT, s, WK, 1696, 16)
                mm_tm(pv, 512, actT, s, WV, 0, 16)
                sc.op("act", lambda e: e.copy(out=vst[:, s, :, 0:64],
                                              in_=pv[:, :].rearrange("p (j c) -> p j c", c=64)),
                      reads=[pv.b], writes=[vst.b])
                rstd_of(pa[:, 0:256], [pa.b], 256, 4 + s, ckvn[:, :], ckvn.b)
                sc.op("dve", lambda e: e.scalar_tensor_tensor(out=ckvn[:, :], in0=pa[:, 0:256], scalar=stat[:, 4 + s:5 + s],
                                                              in1=gkv_b[:, :], op0=ALU.mult, op1=ALU.mult),
                      reads=[pa.b, stat.b, gkv_b.b], writes=[ckvn.b])
                transposes(ckvn, 0, lambda j0, n: ckvnT[:, j0:j0 + n, s * 128:(s + 1) * 128], [ckvn.b], ckvnT.b, 2)
                ta, tb = rtmp
                sc.op("dve", lambda e: e.tensor_tensor(out=ta[:, 0:32], in0=pa[:, 256:288], in1=tkr[:, s, 0:32], op=ALU.mult),
                      reads=[pa.b, tkr.b], writes=[ta.b])
                sc.op("dve", lambda e: e.tensor_tensor(out=tb[:, 0:16], in0=pa[:, 272:288], in1=tkr[:, s, 32:48], op=ALU.mult),
                      reads=[pa.b, tkr.b], writes=[tb.b])
                sc.op("dve", lambda e: e.tensor_tensor(out=tb[:, 16:32], in0=pa[:, 256:272], in1=tkr[:, s, 48:64], op=ALU.mult),
                      reads=[pa.b, tkr.b], writes=[tb.b])
                sc.op("dve", lambda e: e.tensor_tensor(out=krb[:, :], in0=ta[:, 0:32], in1=tb[:, 0:32], op=ALU.add),
                      reads=[ta.b, tb.b], writes=[krb.b])
                transposes(krb, 0, lambda j0, n: krT[0:32, s * 128:(s + 1) * 128].rearrange("p (j c) -> p j c", j=1),
                           [krb.b], krT.b, 1, rows=128, width=32)
            sc.dma("sp", VSW[t * 4:(t + 1) * 4].rearrange("s p j c -> p s j c"), vst[:], reads=[vst.b])
            for hh in range(MLA_H):
                sc.dma("sp", KM[hh, 64:96, t * 512:(t + 1) * 512], krT[0:32, :], reads=[krT.b])
            for pr in range(8):
                pb = ps[pr % 2]
                for kc in range(2):
                    sc.op("pe", lambda e: e.matmul(pb[:, :], lhsT=WUK[:, kc, pr * 128:(pr + 1) * 128], rhs=ckvnT[:, kc, :],
                                                   start=(kc == 0), stop=(kc == 1)),
                          reads=[wA.b, ckvnT.b], writes=[pb.b])
                ks_ = kstA[kst_i % 3]
                kst_i += 1
                sc.op("act", lambda e: e.copy(out=ks_[:, :], in_=pb[:, :]), reads=[pb.b], writes=[ks_.b])
                sc.dma("sp", KM[2 * pr, 0:64, t * 512:(t + 1) * 512], ks_[0:64, :], reads=[ks_.b])
                sc.dma("sp", KM[2 * pr + 1, 0:64, t * 512:(t + 1) * 512], ks_[64:128, :], reads=[ks_.b])
            for s in range(4):
                for hf in range(2):
                    pb = ps[4 + hf]
                    for kc in range(2):
                        sc.op("pe", lambda e: e.matmul(pb[:, :], lhsT=ckvnT[:, kc, s * 128:(s + 1) * 128],
                                                       rhs=WUV[:, kc, hf * 512:(hf + 1) * 512],
                                                       start=(kc == 0), stop=(kc == 1)),
                              reads=[wA.b, ckvnT.b], writes=[pb.b])
                    sc.op("dve", lambda e: e.tensor_copy(out=vmst[:, s, hf * 8:(hf + 1) * 8, 0:64],
                                                         in_=pb[:, :].rearrange("p (j c) -> p j c", c=64)),
                          reads=[pb.b], writes=[vmst.b])
            sc.dma("sp", VM[t * 4:(t + 1) * 4].rearrange("s p j c -> p s j c"), vmst[:], reads=[vmst.b])
        sc.barrier()
        ck(2)

        kcs = carve("kcs", 0, [S])
        w1s = carve("w1s", S, [32, 96])
        w2s = carve("w2s", S + 3072, [96])
        poss = carve("poss", S + 3200, [96])
        posT = carve("posT", S + 3328, [32])
        gl = carve("gl", S + 3392, [512])
        ub = fcarve("ub", 0, [512])
        tb_ = fcarve("tb_", 512, [512])
        b1 = small
        sc.op("dve", lambda e: e.memset(kccT[:], 0.0), writes=[kccT.b])
        sc.op("dve", lambda e: e.memset(vcaug[:], 0.0), writes=[vcaug.b])
        sc.op("dve", lambda e: e.memset(vcaug[:, :, :, 64:65], 1.0), writes=[vcaug.b])
        for g in range(NSA_G):
            sc.dma("pool", vcaug[:, :, g, 65:193], msel_in, writes=[vcaug.b])
        for which in range(2):
            dd = NSA_DK if which == 0 else NSA_DV
            posi, w1i, w2i = (cmp_pos_k, cmp_w1_k, cmp_w2_k) if which == 0 else (cmp_pos_v, cmp_w1_v, cmp_w2_v)
            sc.dma("pool", w1s[0:dd, :, 0:dd], w1i.rearrange("(l d) e -> d l e", d=dd), writes=[w1s.b])
            sc.dma("pool", w2s[0:dd, 0:dd], w2i, writes=[w2s.b])
            sc.dma("pool", poss[0:32, 0:dd], posi, writes=[poss.b])
            transposes(poss, 0, lambda j0, n: posT[0:dd, 0:32].rearrange("p (j c) -> p j c", j=1), [poss.b], posT.b, 1,
                       rows=32, width=dd)
            pb = ps[0]
            for l in range(32):
                sc.op("pe", lambda e: e.matmul(pb[0:dd, 0:1], lhsT=w1s[0:dd, l, 0:dd], rhs=posT[0:dd, l:l + 1],
                                               start=(l == 0), stop=(l == 31)),
                      reads=[w1s.b, posT.b], writes=[pb.b])
            sc.op("dve", lambda e: e.tensor_copy(out=b1[0:dd, which:which + 1], in_=pb[0:dd, 0:1]), reads=[pb.b], writes=[b1.b])
            for g in range(NSA_G):
                if which == 0:
                    sc.dma("sp", kcs[0:96, :], KN[g, :, :], writes=[kcs.b])
                else:
                    sc.dma("sp", kcs[0:64, :], VCT[g * 64:(g + 1) * 64, :], writes=[kcs.b])
                for c0 in range(0, NC, 512):
                    n = min(512, NC - c0)
                    pb = ps[1]
                    for l in range(32):
                        lo = 16 * c0 + l
                        sc.op("pe", lambda e: e.matmul(pb[0:dd, 0:n], lhsT=w1s[0:dd, l, 0:dd],
                                                       rhs=kcs[0:dd, lo:lo + 16 * (n - 1) + 1:16],
                                                       start=(l == 0), stop=(l == 31)),
                              reads=[w1s.b, kcs.b], writes=[pb.b])
                    sc.op("act", lambda e: e.activation(out=ub[0:dd, 0:n], in_=pb[0:dd, 0:n], func=AF.Identity,
                                                        bias=b1[0:dd, which:which + 1]),
                          reads=[pb.b, b1.b], writes=[ub.b])
                    sc.op("dve", lambda e: e.tensor_tensor(out=tb_[0:dd, 0:n], in0=ub[0:dd, 0:n], in1=ub[0:dd, 0:n], op=ALU.mult),
                          reads=[ub.b], writes=[tb_.b])
                    sc.op("dve", lambda e: e.tensor_scalar(out=tb_[0:dd, 0:n], in0=tb_[0:dd, 0:n], scalar1=0.044715, scalar2=1.0,
                                                           op0=ALU.mult, op1=ALU.add), writes=[tb_.b])
                    sc.op("dve", lambda e: e.tensor_tensor(out=tb_[0:dd, 0:n], in0=tb_[0:dd, 0:n], in1=ub[0:dd, 0:n], op=ALU.mult),
                          reads=[ub.b], writes=[tb_.b])
                    sc.op("act", lambda e: e.activation(out=tb_[0:dd, 0:n], in_=tb_[0:dd, 0:n], func=AF.Sigmoid,
                                                        scale=1.5957691216057308), writes=[tb_.b])
                    sc.op("dve", lambda e: e.tensor_tensor(out=gl[0:dd, 0:n], in0=tb_[0:dd, 0:n], in1=ub[0:dd, 0:n], op=ALU.mult),
                          reads=[ub.b, tb_.b], writes=[gl.b])
                    if which == 0:
                        pc = ps[2]
                        sc.op("pe", lambda e: e.matmul(pc[0:96, 0:n], lhsT=w2s[0:96, 0:96], rhs=gl[0:96, 0:n], start=True, stop=True),
                              reads=[w2s.b, gl.b], writes=[pc.b])
                        sc.op("act", lambda e: e.copy(out=kccT[0:96, g, c0:c0 + n], in_=pc[0:96, 0:n]), reads=[pc.b], writes=[kccT.b])
                    else:
                        for cc in range(0, n, 128):
                            m = min(128, n - cc)
                            pc = ps[2 + (cc // 128) % 2]
                            sc.op("pe", lambda e: e.matmul(pc[0:m, 0:64], lhsT=gl[0:64, cc:cc + m], rhs=w2s[0:64, 0:64],
                                                           start=True, stop=True), reads=[w2s.b, gl.b], writes=[pc.b])
                            sc.op("act", lambda e: e.copy(out=vcaug[0:m, (c0 + cc) // 128, g, 0:64], in_=pc[0:m, 0:64]),
                                  reads=[pc.b], writes=[vcaug.b])
        sc.barrier()
        ck(3)

        for qt in range(NQT):
            mixer(qt)
            sc.barrier()
            ck(5)
            xattn_mlp(qt)
            sc.barrier()

    att_i = [0]
    kv_i = [0]
    psTf = [View(psTs[i][:, :].bitcast(F32), f"psTf{i}", psTs[i].b) for i in range(2)]
    sbanks = [ps[0], ps[1], ps[4], psTf[0]]
    mebanks = [ps[5], psTf[1]]

    def attention(heads, scale, Ksrc, Vsrc, ktiles, mask_fn, acc_of, pre_kt=None):
        LOOK = 3
        PK = 2
        pre_done = [0]
        kt_index = {kt: n for n, kt in enumerate(ktiles)}
        first = {hd: True for hd in heads}
        blocks = [ktiles[b0:b0 + 12] for b0 in range(0, len(ktiles), 12)]
        slots = {}

        def load_block(bi):
            blk = blocks[bi]
            n = len(blk)
            kt0 = blk[0]
            ksl = Ksl[kv_i[0] % 2]
            vsl = Vsl[kv_i[0] % 2]
            kv_i[0] += 1
            sc.dma("sp", ksl[0:96, 0:n * 128], Ksrc[:, kt0 * 128:(kt0 + n) * 128], writes=[ksl.b])
            sc.dma("sp", vsl[:, 0:n, :], Vsrc[kt0:kt0 + n].rearrange("t p c -> p t c"), writes=[vsl.b])
            slots[bi] = (ksl, vsl)

        pending = []
        load_block(0)
        for bi, blk in enumerate(blocks):
            ksl, vsl = slots[bi]
            staged = 0
            for i, kt in enumerate(blk):
                while pre_kt is not None and pre_done[0] <= min(kt_index[kt] + PK, len(ktiles) - 1):
                    pre_kt(ktiles[pre_done[0]])
                    pre_done[0] += 1
                for hi, hd in enumerate(heads):
                    if staged == LOOK and bi + 1 < len(blocks):
                        load_block(bi + 1)
                    staged += 1
                    sb = sbanks[att_i[0] % 4]
                    p_t = pT[att_i[0] % 5]
                    att_i[0] += 1
                    sc.op("pe", lambda e: e.matmul(sb[:, :], lhsT=ksl[0:96, i * 128:(i + 1) * 128], rhs=Q[0:96, hd, :],
                                                   start=True, stop=True), reads=[ksl.b, Q.b], writes=[sb.b])
                    sc.op("act", lambda e: e.activation(out=p_t[:, :], in_=sb[:, :], func=AF.Exp, scale=scale),
                          reads=[sb.b], writes=[p_t.b])
                    mk = mask_fn(kt)
                    if mk is not None:
                        meng = "pool" if att_i[0] % 3 == 0 else "dve"
                        sc.op(meng, lambda e: e.tensor_tensor(out=p_t[:, :], in0=p_t[:, :], in1=mk[:, :], op=ALU.mult),
                              reads=[mk.b], writes=[p_t.b])

                    def stage_b(p_t=p_t, vsl=vsl, i=i, hi=hi, hd=hd):
                        acc = acc_of(hi)
                        for s in range(4):
                            sc.op("pe", lambda e: e.matmul(acc[:, s * 65:(s + 1) * 65], lhsT=p_t[:, s * 128:(s + 1) * 128],
                                                           rhs=vsl[:, i, :], start=(first[hd] and s == 0), stop=False,
                                                           skip_group_check=True),
                                  reads=[p_t.b, vsl.b], writes=[acc.b])
                        first[hd] = False

                    pending.append(stage_b)
                    if len(pending) > LOOK:
                        pending.pop(0)()
            if staged <= LOOK and bi + 1 < len(blocks):
                while pending:
                    pending.pop(0)()
                load_block(bi + 1)
        while pending:
            pending.pop(0)()

    def nsa_finalize(acc, ncol, hd, hl, br, first_branch):
        accv = acc[:, 0:4 * ncol].rearrange("p (s c) -> p s c", c=ncol) if ncol == 65 else None
        for s in range(4):
            a = accv[:, s, :] if accv is not None else acc[s // 2][:, (s % 2) * ncol:(s % 2) * ncol + ncol]
            ab = acc.b if accv is not None else acc[s // 2].b
            c0 = 32 + s
            sc.op("dve", lambda e: e.tensor_scalar_max(out=small[:, c0:c0 + 1], in0=a[:, 64:65], scalar1=1e-30),
                  reads=[ab], writes=[small.b])
            sc.op("dve", lambda e: e.reciprocal(out=small[:, c0:c0 + 1], in_=small[:, c0:c0 + 1]), writes=[small.b])
            if ncol != 65:
                sc.op("dve", lambda e: e.tensor_copy(out=small[:, 40 + s:41 + s], in_=small[:, c0:c0 + 1]), writes=[small.b])
            sc.op("dve", lambda e: e.tensor_tensor(out=small[:, c0:c0 + 1], in0=small[:, c0:c0 + 1],
                                                   in1=gates[:, s, hd * 3 + br:hd * 3 + br + 1], op=ALU.mult),
                  reads=[gates.b], writes=[small.b])
            dst = tmpo[:, hl, s * 64:(s + 1) * 64]
            if first_branch:
                sc.op("dve", lambda e: e.tensor_scalar(out=dst, in0=a[:, 0:64], scalar1=small[:, c0:c0 + 1], scalar2=None,
                                                       op0=ALU.mult), reads=[ab, small.b], writes=[tmpo.b])
            else:
                sc.op("dve", lambda e: e.scalar_tensor_tensor(out=dst, in0=a[:, 0:64], scalar=small[:, c0:c0 + 1], in1=dst,
                                                              op0=ALU.mult, op1=ALU.add), reads=[ab, small.b], writes=[tmpo.b])

    def mixer(qt):
        k = qt
        nkt = 8 * k + 8
        zone0 = 8 * k
        nct = min(NCT, (2 * k + 1) // 4 + 1)
        sc.dma("sp", h[:], x_own[qt * 512:(qt + 1) * 512, :].rearrange("(s p) d -> p s d", p=128), writes=[h.b])
        norm_T(h, 4, load_g("g_mix"), actT)
        sc.dma("sp", qrow[:, :], qpos_in[qt:qt + 1, :].partition_broadcast(128), writes=[qrow.b])
        sc.dma("sp", curc[:], cur_in[qt], writes=[curc.b])
        for z in range(8):
            kt = zone0 + z
            sc.op("dve", lambda e: e.tensor_scalar(out=cm[z][:, :], in0=qrow[:, :], scalar1=kposc[:, kt:kt + 1], scalar2=None,
                                                   op0=ALU.is_ge), reads=[qrow.b, kposc.b], writes=[h.b])
        for z in range(12):
            kt = zone0 - 4 + z
            if kt < 0:
                continue
            sc.op("dve", lambda e: e.tensor_scalar(out=wm[z][:, :], in0=qrow[:, :], scalar1=kposc[:, kt:kt + 1], scalar2=511.0,
                                                   op0=ALU.subtract, op1=ALU.is_le), reads=[qrow.b, kposc.b], writes=[h.b])
            if z >= 4:
                sc.op("dve", lambda e: e.tensor_tensor(out=wm[z][:, :], in0=wm[z][:, :], in1=cm[z - 4][:, :], op=ALU.mult),
                      writes=[h.b])
        for ct in range(nct):
            sc.op("dve", lambda e: e.tensor_scalar(out=cmk[ct][:, :], in0=qrow[:, :], scalar1=cendc[:, ct:ct + 1], scalar2=None,
                                                   op0=ALU.is_ge), reads=[qrow.b, cendc.b], writes=[h.b])
        for s in range(4):
            A, Bc = tkA[s], tkB[s]
            sc.op("dve", lambda e: e.tensor_scalar(out=Bc[:], in0=jidx[:], scalar1=curc[:, s:s + 1], scalar2=None, op0=ALU.subtract),
                  reads=[jidx.b, curc.b], writes=[Bc.b])
            sc.op("dve", lambda e: e.tensor_single_scalar(out=A[:], in_=Bc[:], scalar=0.0, op=ALU.is_le), reads=[Bc.b], writes=[A.b])
            sc.op("dve", lambda e: e.tensor_single_scalar(out=Bc[:], in_=Bc[:], scalar=-1.0, op=ALU.is_ge), writes=[Bc.b])
            sc.op("dve", lambda e: e.tensor_tensor(out=Bc[:], in0=Bc[:], in1=A[:], op=ALU.mult), reads=[A.b], writes=[Bc.b])
            sc.op("dve", lambda e: e.tensor_tensor(out=Bc[:], in0=Bc[:], in1=e0[:], op=ALU.max), reads=[e0.b], writes=[Bc.b])
            sc.op("dve", lambda e: e.tensor_tensor(out=A[:], in0=A[:], in1=Bc[:], op=ALU.subtract), reads=[Bc.b], writes=[A.b])
            sc.op("dve", lambda e: e.scalar_tensor_tensor(out=Bc[:], in0=Bc[:], scalar=10001.0, in1=A[:], op0=ALU.mult, op1=ALU.add),
                  reads=[A.b], writes=[Bc.b])
            sc.op("dve", lambda e: e.tensor_scalar_add(out=Bc[:], in0=Bc[:], scalar1=-1.0), writes=[Bc.b])
        wg = load_w(w_in, 0, 16, C_GN, 48)
        for s in range(4):
            pb = ps[2 + s % 2]
            mm_tm(pb, 48, actT, s, wg, 0, 16)
            sc.op("act", lambda e: e.activation(out=gates[:, s, :], in_=pb[:, 0:48], func=AF.Sigmoid), reads=[pb.b], writes=[gates.b])
        sc.dma("sp", rq[:, :], rqn_in[qt], writes=[rq.b])
        for hb in range(4):
            wq = next_wslot()
            wqv = wq[:, :, 0:480].rearrange("p k (j c) -> p k j c", c=120)
            Ws, rb = wsrc(w_in)
            for a in range(16):
                sc.dma("pool", wqv[:, a, :, 0:96],
                       Ws[a * 128:(a + 1) * 128, C_Q + hb * 384:C_Q + (hb + 1) * 384].rearrange("p (j c) -> p j c", c=96),
                       reads=rb, writes=[wq.b])
            sc.op("dve", lambda e: e.tensor_copy(out=wqv[:, :, :, 96:108], in_=wqv[:, :, :, 12:24]), writes=[wq.b])
            sc.op("dve", lambda e: e.tensor_copy(out=wqv[:, :, :, 108:120], in_=wqv[:, :, :, 0:12]), writes=[wq.b])
            for j in range(4):
                hd = hb * 4 + j
                pb = ps[2 + j % 2]
                mm_fm(pb, 120, wq, j * 120, actT, 16)
                rope_evac(pb, rq, Q[:, hd, :], Q.b, 0, 24, 96, 96)
        ck(4)
        Gs_, rb_ = wsrc(gsel_in)
        sc.dma("pool", GS[:, :], Gs_, reads=rb_, writes=[GS.b])
        for g in range(NSA_G):
            gheads = [4 * g + j for j in range(4)]
            for hl, hd in enumerate(gheads):
                accs = [ps[2], ps[3]]
                for ct in range(nct):
                    sb = ps[ct % 2]
                    p_t = pT[ct % 3]
                    sc.op("pe", lambda e: e.matmul(sb[:, :], lhsT=kccT[0:96, g, ct * 128:(ct + 1) * 128], rhs=Q[0:96, hd, :],
                                                   start=True, stop=True), reads=[kccT.b, Q.b], writes=[sb.b])
                    sc.op("act", lambda e: e.activation(out=p_t[:, :], in_=sb[:, :], func=AF.Exp, scale=NSA_SCALE),
                          reads=[sb.b], writes=[p_t.b])
                    sc.op("dve", lambda e: e.tensor_tensor(out=p_t[:, :], in0=p_t[:, :], in1=cmk[ct][:, :], op=ALU.mult),
                          reads=[h.b], writes=[p_t.b])
                    for s in range(4):
                        acc = accs[s // 2]
                        sc.op("pe", lambda e: e.matmul(acc[:, (s % 2) * 196:(s % 2) * 196 + 193], lhsT=p_t[:, s * 128:(s + 1) * 128],
                                                       rhs=vcaug[:, ct, g, 0:193], start=(ct == 0 and s % 2 == 0), stop=False,
                                                       skip_group_check=True),
                              reads=[p_t.b, vcaug.b], writes=[acc.b])
                nsa_finalize(accs, 196, hd, hl, 0, True)
                for s in range(4):
                    acc = accs[s // 2]
                    src = acc[:, (s % 2) * 196 + 65:(s % 2) * 196 + 193]
                    if hl == 0:
                        sc.op("dve", lambda e: e.tensor_scalar(out=impacc[:, s, :], in0=src, scalar1=small[:, 40 + s:41 + s], scalar2=None,
                                                               op0=ALU.mult), reads=[acc.b, small.b], writes=[impacc.b])
                    else:
                        sc.op("dve", lambda e: e.scalar_tensor_tensor(out=impacc[:, s, :], in0=src, scalar=small[:, 40 + s:41 + s],
                                                                      in1=impacc[:, s, :], op0=ALU.mult, op1=ALU.add),
                              reads=[acc.b, small.b], writes=[impacc.b])
            for s in range(4):
                sc.op("dve", lambda e: e.tensor_tensor(out=scoreA[:, :], in0=impacc[:, s, :], in1=tkA[s][:], op=ALU.mult),
                      reads=[impacc.b, tkA[s].b], writes=[scoreA.b])
                sc.op("dve", lambda e: e.tensor_tensor(out=scoreA[:, :], in0=scoreA[:, :], in1=tkB[s][:], op=ALU.add),
                      reads=[tkB[s].b], writes=[scoreA.b])
                sc.op("dve", lambda e: e.max(out=small[:, 0:8], in_=scoreA[:, :]), reads=[scoreA.b], writes=[small.b])
                sc.op("dve", lambda e: e.match_replace(out=scoreB[:, :], in_to_replace=small[:, 0:8], in_values=scoreA[:, :],
                                                       imm_value=-2.0), reads=[scoreA.b, small.b], writes=[scoreB.b])
                sc.op("dve", lambda e: e.max(out=small[:, 8:16], in_=scoreB[:, :]), reads=[scoreB.b], writes=[small.b])
                sc.op("dve", lambda e: e.tensor_reduce(out=small[:, 16:17], in_=small[:, 8:16], axis=AX.X, op=ALU.min),
                      writes=[small.b])
                sc.op("dve", lambda e: e.tensor_scalar(out=sel01[:, :], in0=scoreA[:, :], scalar1=small[:, 16:17], scalar2=None,
                                                       op0=ALU.is_ge), reads=[scoreA.b, small.b], writes=[sel01.b])
                transposes(sel01, 0, lambda j0, n: selT[:, g, s * 128:(s + 1) * 128].rearrange("p (j c) -> p j c", j=1),
                           [sel01.b], selT.b, 1)
            for pr in range(2):
                pheads = gheads[2 * pr:2 * pr + 2]

                def pre_kt(kt):
                    mp = mebanks[kt % 2]
                    sm = smask[kt % 4]
                    sc.op("pe", lambda e: e.matmul(mp[:, :], lhsT=GS[:, kt * 128:(kt + 1) * 128], rhs=selT[:, g, :],
                                                   start=True, stop=True), reads=[GS.b, selT.b], writes=[mp.b])
                    if kt >= zone0:
                        sc.op("dve", lambda e: e.tensor_tensor(out=sm[:, :], in0=mp[:, :], in1=cm[kt - zone0][:, :], op=ALU.mult),
                              reads=[mp.b, h.b], writes=[sm.b])
                    elif kt % 4 < 2:
                        sc.op("act", lambda e: e.copy(out=sm[:, :], in_=mp[:, :]), reads=[mp.b], writes=[sm.b])
                    else:
                        sc.op("dve", lambda e: e.tensor_copy(out=sm[:, :], in_=mp[:, :]), reads=[mp.b], writes=[sm.b])

                attention(pheads, NSA_SCALE, KN[4 + g], VSW[:, :, g, :], list(range(nkt)),
                          lambda kt: smask[kt % 4], lambda hi: ps[2 + hi], pre_kt)
                for hi, hd in enumerate(pheads):
                    nsa_finalize(ps[2 + hi], 65, hd, 2 * pr + hi, 1, False)
                wt = [kt for kt in range(zone0 - 4, zone0 + 8) if kt >= 0]
                attention(pheads, NSA_SCALE, KN[8 + g], VSW[:, :, 4 + g, :], wt,
                          lambda kt: wm[kt - zone0 + 4], lambda hi: ps[2 + hi])
                for hi, hd in enumerate(pheads):
                    nsa_finalize(ps[2 + hi], 65, hd, 2 * pr + hi, 2, False)
            for hl in range(4):
                sc.op("act", lambda e: e.copy(out=oa3[:, :, (4 * g + hl) * 64:(4 * g + hl + 1) * 64],
                                              in_=tmpo[:, hl, :].rearrange("p (s d) -> p s d", d=64)),
                      reads=[tmpo.b], writes=[oa.b])
        ck(6)
        wcq = load_w(w_in, 0, 16, C_CQ, 512)
        sc.dma("sp", gq_b[:, :], g_q_in.partition_broadcast(128), writes=[gbs.b])
        for s in range(4):
            pb = ps[s % 2]
            mm_tm(pb, 512, actT, s, wcq, 0, 16)
            rstd_of(pb[:, :], [pb.b], 512, 8 + s, cqn[:, :], cqn.b)
            sc.op("dve", lambda e: e.scalar_tensor_tensor(out=cqn[:, :], in0=pb[:, :], scalar=stat[:, 8 + s:9 + s], in1=gq_b[:, :],
                                                          op0=ALU.mult, op1=ALU.mult), reads=[pb.b, stat.b, gq_b.b], writes=[cqn.b])
            transposes(cqn, 0, lambda j0, n: cqnT[:, j0:j0 + n, s * 128:(s + 1) * 128], [cqn.b], cqnT.b, 4)
        wu = next_wslot()
        wuv_ = wu[:, :, :].rearrange("p k c -> p (k c)").rearrange("p (k j c) -> p k j c", k=4, c=128)
        Ws, rb = wsrc(w_uq)
        for kc in range(4):
            sc.dma("pool", wuv_[:, kc, :, 0:96], Ws[kc * 128:(kc + 1) * 128, :].rearrange("p (j c) -> p j c", c=96),
                   reads=rb, writes=[wu.b])
        sc.op("dve", lambda e: e.tensor_copy(out=wuv_[:, :, :, 96:112], in_=wuv_[:, :, :, 80:96]), writes=[wu.b])
        sc.op("dve", lambda e: e.tensor_copy(out=wuv_[:, :, :, 112:128], in_=wuv_[:, :, :, 64:80]), writes=[wu.b])
        sc.dma("sp", rq[:, :], rqm_in[qt], writes=[rq.b])
        for hd in range(MLA_H):
            pb = ps[2 + hd % 2]
            for kc in range(4):
                sc.op("pe", lambda e: e.matmul(pb[:, :], lhsT=wuv_[:, kc, hd, :], rhs=cqnT[:, kc, :], start=(kc == 0), stop=(kc == 3)),
                      reads=[wu.b, cqnT.b], writes=[pb.b])
            rope_evac(pb, rq, Q[:, hd, :], Q.b, 64, 32, 96, 96)
        for hd in range(MLA_H):
            acc = ps[2 + hd % 2]
            attention([hd], MLA_SCALE, KM[hd], VM[:, :, hd, :], list(range(nkt)),
                      lambda kt: (cm[kt - zone0] if kt >= zone0 else None), lambda hi: acc)
            accv = acc[:, 0:260].rearrange("p (s c) -> p s c", c=65)
            sc.op("dve", lambda e: e.tensor_scalar_max(out=small[:, 48:52], in0=accv[:, :, 64], scalar1=1e-30), reads=[acc.b], writes=[small.b])
            sc.op("dve", lambda e: e.reciprocal(out=small[:, 48:52], in_=small[:, 48:52]), writes=[small.b])
            for s in range(4):
                sc.op("dve", lambda e: e.tensor_scalar(out=ob3[:, s, hd * 64:(hd + 1) * 64], in0=accv[:, s, 0:64],
                                                       scalar1=small[:, 48 + s:49 + s], scalar2=None, op0=ALU.mult),
                      reads=[acc.b, small.b], writes=[ob.b])
        ck(7)
        for s in range(4):
            transposes(oa, s * 1024, lambda j0, n: oaT[:, j0:j0 + n, s * 128:(s + 1) * 128], [oa.b], oaT.b, 8)
            transposes(ob, s * 1024, lambda j0, n: obT[:, j0:j0 + n, s * 128:(s + 1) * 128], [ob.b], obT.b, 8)
        sc.dma("sp", h[:], x_own[qt * 512:(qt + 1) * 512, :].rearrange("(s p) d -> p s d", p=128), writes=[h.b])
        sa, sbb = rtmp
        sc.barrier()
        for cb in range(4):
            for which, (gcol, wo_src, oT) in enumerate(((C_GA, w_o_nsa, oaT), (C_GB, w_o_mla, obT))):
                wgt = load_w(w_in, 0, 16, gcol + cb * 512, 512)
                wo = load_w(wo_src, 0, 8, cb * 512, 512)
                for c in range(4):
                    ft = cb * 4 + c
                    pg, pp = ps[0], ps[1]
                    mm_fm(pg, 128, wgt, c * 128, actT, 16)
                    sc.op("act", lambda e: e.activation(out=sa[:, :], in_=pg[:, :], func=AF.Sigmoid), reads=[pg.b], writes=[sa.b])
                    mm_fm(pp, 128, wo, c * 128, oT, 8)
                    if which == 0:
                        sc.op("dve", lambda e: e.tensor_tensor(out=sbb_keep[ft % 4][:, :], in0=sa[:, :], in1=pp[:, :], op=ALU.mult),
                              reads=[sa.b, pp.b], writes=[sbb_keep[ft % 4].b])
                    else:
                        sc.op("dve", lambda e: e.tensor_tensor(out=sa[:, :], in0=sa[:, :], in1=pp[:, :], op=ALU.mult),
                              reads=[pp.b], writes=[sa.b])
                        sc.op("dve", lambda e: e.tensor_tensor(out=mixT[:, ft, :], in0=sa[:, :], in1=sbb_keep[ft % 4][:, :], op=ALU.add),
                              reads=[sa.b, sbb_keep[ft % 4].b], writes=[mixT.b])
        for cg in range(4):
            wo = load_w(w_out, 0, 16, cg * 512, 512)
            for s in range(4):
                pb = ps[s % 2]
                mm_tm(pb, 512, mixT, s, wo, 0, 16)
                sc.op("dve", lambda e: e.tensor_tensor(out=h[:, s, cg * 512:(cg + 1) * 512],
                                                       in0=h[:, s, cg * 512:(cg + 1) * 512], in1=pb[:, :], op=ALU.add),
                      reads=[pb.b], writes=[h.b])

    sbb_keep = [View(fsm[:, 512 * i:512 * (i + 1)], f"sbbk{i}") for i in range(4)]

    def xattn_mlp(qt):
        norm_T(h, 4, load_g("g_xattn"), actT)
        wq = load_w(xa_wq, 0, 16, 0, 512)
        for hh in range(XA_H):
            pb = ps[hh % 2]
            mm_fm(pb, 128, wq, hh * 128, actT, 16)
            sc.op("act", lambda e: e.copy(out=qx[:, hh, :], in_=pb[:, :]), reads=[pb.b], writes=[qx.b])
        for hh in range(XA_H):
            pvb = [ps[4], ps[5]]
            for mt in range(2):
                sb = ps[2 + mt]
                p_t = pTx[mt]
                sc.op("pe", lambda e: e.matmul(sb[:, :], lhsT=kx[:, hh, mt * 128:(mt + 1) * 128], rhs=qx[:, hh, :],
                                               start=True, stop=True), reads=[kx.b, qx.b], writes=[sb.b])
                sc.op("act", lambda e: e.activation(out=p_t[:, :], in_=sb[:, :], func=AF.Exp, scale=float(XA_D) ** -0.5),
                      reads=[sb.b], writes=[p_t.b])
                for s in range(4):
                    acc = pvb[s // 2]
                    sc.op("pe", lambda e: e.matmul(acc[:, (s % 2) * 132:(s % 2) * 132 + 129],
                                                   lhsT=p_t[:, s * 128:(s + 1) * 128], rhs=vx[:, mt, hh, 0:129],
                                                   start=(mt == 0 and s % 2 == 0), stop=(mt == 1), skip_group_check=True),
                          reads=[p_t.b, vx.b], writes=[acc.b])
            for half in range(2):
                acc = pvb[half]
                accv = acc[:, 0:264].rearrange("p (s c) -> p s c", c=132)
                sc.op("dve", lambda e: e.reciprocal(out=rden[:, half * 2:half * 2 + 2], in_=accv[:, :, 128]),
                      reads=[acc.b], writes=[rden.b])
                for s2 in range(2):
                    s = half * 2 + s2
                    sc.op("dve", lambda e: e.tensor_scalar(
                        out=ox[:, s * 512 + hh * 128: s * 512 + (hh + 1) * 128], in0=accv[:, s2, 0:128],
                        scalar1=rden[:, half * 2 + s2:half * 2 + s2 + 1], scalar2=None, op0=ALU.mult),
                        reads=[acc.b, rden.b], writes=[ox.b])
        for s in range(4):
            transposes(ox, s * 512, lambda j0, n: oxT[:, j0:j0 + n, s * 128:(s + 1) * 128], [ox.b], oxT.b, 4)
        for cg in range(4):
            wo = load_w(xa_wo, 0, 4, cg * 512, 512)
            for s in range(4):
                pb = ps[s % 2]
                mm_tm(pb, 512, oxT, s, wo, 0, 4)
                sc.op("dve", lambda e: e.tensor_tensor(out=h[:, s, cg * 512:(cg + 1) * 512],
                                                       in0=h[:, s, cg * 512:(cg + 1) * 512], in1=pb[:, :], op=ALU.add),
                      reads=[pb.b], writes=[h.b])
        sc.barrier()
        norm_T(h, 4, load_g("g_mlp"), actT)
        for blk in range(16):
            w1 = load_w(w_ff1, 0, 16, blk * 512, 512)
            for c in range(4):
                pb = ps[c % 2]
                rt = rtmp[c % 2]
                mm_fm(pb, 128, w1, c * 128, actT, 16)
                sc.op("act", lambda e: e.activation(out=rt[:], in_=pb[:, :], func=AF.Relu), reads=[pb.b], writes=[rt.b])
                sc.op("dve", lambda e: e.tensor_tensor(out=hid[:, blk * 4 + c, :], in0=rt[:], in1=rt[:], op=ALU.mult),
                      reads=[rt.b], writes=[hid.b])
        for cg in range(4):
            for kb in range(4):
                w2 = load_w(w_ff2, kb * 16, 16, cg * 512, 512)
                for s in range(4):
                    mm_tm(ps[2 + s], 512, hid, s, w2, 0, 16, kc0=kb * 16, first=(kb == 0), last=(kb == 3))
            for s in range(4):
                pb = ps[2 + s]
                sc.op("dve", lambda e: e.tensor_tensor(out=h[:, s, cg * 512:(cg + 1) * 512],
                                                       in0=h[:, s, cg * 512:(cg + 1) * 512], in1=pb[:, :], op=ALU.add),
                      reads=[pb.b], writes=[h.b])
        gfin = load_g("g_final")
        for s in range(4):
            rstd_of(h[:, s, :], [h.b], D, 8 + s, xnb[0][:], xnb[0].b)
            sc.op("dve", lambda e: e.scalar_tensor_tensor(out=h[:, s, :], in0=h[:, s, :], scalar=stat[:, 8 + s:9 + s],
                                                          in1=gfin[:], op0=ALU.mult, op1=ALU.mult),
                  reads=[stat.b, gfin.b], writes=[h.b])
        sc.dma("sp", out[qt * 512:(qt + 1) * 512, :].rearrange("(s p) d -> p s d", p=128), h[:], reads=[h.b])

    try:
        _body()
    except _Stop:
        sc.dma("sp", out[0:512, :].rearrange("(s p) d -> p s d", p=128), h[:], reads=[h.b])
    sc.finish()
    return nc


_CACHE = {}


def _rope_np(npos, dim):
    inv = (1.0 / (np.float32(ROPE_THETA) ** (np.arange(0, dim, 2, dtype=np.float32) / np.float32(dim)))).astype(np.float32)
    ang = np.arange(npos, dtype=np.float32)[:, None] * inv[None, :]
    return np.cos(ang).astype(np.float32), np.sin(ang).astype(np.float32)


def _consts(S):
    NKT = S // 128
    NC = S // 16 - 1
    NCT = (NC + 127) // 128
    ca, sa = _rope_np(S, NSA_ROT)
    cb, sb = _rope_np(S, MLA_ROPE)
    rkn = np.zeros((128, S), np.float32)
    rkn[0:12] = ca.T
    rkn[12:24] = ca.T
    rkn[96:108] = -sa.T
    rkn[108:120] = sa.T
    rqm = np.zeros((128, S), np.float32)
    rqm[64:80] = cb.T
    rqm[80:96] = cb.T
    rqm[96:112] = -sb.T
    rqm[112:128] = sb.T
    tkr = np.concatenate([cb, cb, -sb, sb], axis=1).astype(np.float32)
    kpos = (np.arange(NKT)[None, :] * 128 + np.arange(128)[:, None]).astype(np.float32)
    cend = (16 * (np.arange(NCT)[None, :] * 128 + np.arange(128)[:, None]) + 31).astype(np.float32)
    jidx = np.tile(np.arange(128, dtype=np.float32)[None, :], (128, 1))
    cs = np.arange(NCT * 128) * 16
    ce = cs + 32
    ss = np.arange(128) * 64
    se = ss + 64
    ov = np.clip(np.minimum(ce[:, None], se[None, :]) - np.maximum(cs[:, None], ss[None, :]), 0, None).astype(np.float32) / 32.0
    ov[NC:] = 0.0
    msel = np.ascontiguousarray(ov.reshape(NCT, 128, 128).transpose(1, 0, 2))
    gsel = (np.arange(128)[:, None] == (np.arange(S)[None, :] // 64)).astype(np.float32)
    return dict(rkn=rkn, rqm_full=rqm, tkr=tkr, kpos=kpos, cend=cend, jidx=jidx, msel=msel, gsel=gsel,
                ident=np.eye(128, dtype=np.float32))


def kernel(**inputs):
    x = np.asarray(inputs["x"], dtype=np.float32)
    B, S, _ = x.shape
    NQT = S // 1024
    ncores = 2 * B
    if S not in _CACHE:
        _CACHE[S] = (build_program(S), _consts(S))
    nc, cst = _CACHE[S]

    def sq(name):
        a = np.asarray(inputs[name], dtype=np.float32)
        return np.ascontiguousarray(a[0])

    shared = {
        "g_mix": sq("g_mix").reshape(1, D), "g_xattn": sq("g_xattn").reshape(1, D), "g_mem": sq("g_mem").reshape(1, D),
        "g_mlp": sq("g_mlp").reshape(1, D), "g_final": np.asarray(inputs["g_final"], np.float32).reshape(1, D),
        "mla_g_q": sq("mla_g_q").reshape(1, -1), "mla_g_kv": sq("mla_g_kv").reshape(1, -1),
    }
    for k in ("w_in", "cmp_pos_k", "cmp_w1_k", "cmp_w2_k", "cmp_pos_v", "cmp_w1_v", "cmp_w2_v", "mla_w_uq", "mla_w_uk",
              "mla_w_uv", "w_o_nsa", "w_o_mla", "w_out", "xa_wq", "xa_wkv", "xa_wo", "w_ff1", "w_ff2"):
        shared[k] = sq(k)
    for k in ("ident", "rkn", "tkr", "kpos", "cend", "jidx", "msel", "gsel"):
        shared[k] = cst[k]
    memv = np.asarray(inputs["mem"], dtype=np.float32)
    in_maps = []
    for c in range(ncores):
        b, r = c // 2, c % 2
        tiles = [2 * k + r for k in range(NQT)]
        pos = np.concatenate([np.arange(t * 512, (t + 1) * 512) for t in tiles])
        m = dict(shared)
        m["x_full"] = np.ascontiguousarray(x[b])
        m["x_own"] = np.ascontiguousarray(x[b][pos])
        m["mem"] = np.ascontiguousarray(memv[b])
        m["rqn"] = np.ascontiguousarray(cst["rkn"][:, pos].reshape(128, NQT, 512).transpose(1, 0, 2))
        m["rqm"] = np.ascontiguousarray(cst["rqm_full"][:, pos].reshape(128, NQT, 512).transpose(1, 0, 2))
        m["qpos"] = pos.astype(np.float32).reshape(NQT, 512)
        m["cur"] = np.ascontiguousarray((pos // 64).astype(np.float32).reshape(NQT, 4, 128).transpose(0, 2, 1))
        in_maps.append(m)
    res = run_bass_kernel_spmd(nc, in_maps, core_ids=list(range(ncores)))
    outp = np.empty((B, S, D), dtype=np.float32)
    for c in range(ncores):
        b, r = c // 2, c % 2
        o = res.results[c]["out"]
        for k in range(NQT):
            t = 2 * k + r
            outp[b, t * 512:(t + 1) * 512] = o[k * 512:(k + 1) * 512]
    return outp
```
